# Optimizing a Trainium2 kernel written in Bass

```python
import math
import jax, jax.numpy as jnp
from jax import lax
import numpy as np

D_MODEL = 1024
BATCH = 16
SEQ = 256
DEPTH = 2
DEC_BATCH = 8
DEC_SEQ = 4096
PAST_LEN = 512

GRID_W = 64
N_EVEN = (DEPTH + 1) // 2
N_ODD = DEPTH // 2
CHUNK = 128
Q_BLOCK = 128
EPS = 1e-6
ROPE_BASE = 10000.0
A_HEADS = 4
A_QK = 64
A_V = 2 * A_QK
A_QKW = A_HEADS * 2 * A_QK
A_WIDTH = A_HEADS * A_V
N_FREQ = A_QK // 4
B_GROUPS = 4
B_CH = 128
B_WIDTH = B_GROUPS * B_CH
MIX0 = A_WIDTH + B_WIDTH
IN0 = 2 * A_QKW + A_WIDTH + 2 * B_WIDTH
C_HEADS = 4
C_INNER = D_MODEL
C_HD = C_INNER // C_HEADS
IN1 = 2 * C_INNER + 4 * C_HEADS
D_FF = ((8 * D_MODEL // 3 + 127) // 128) * 128
N_MOD = 6

kernel_name = 'hybrid_diffusion_diffattn_gmlp_mlstm_step'


def _rmsnorm(x, w):
    xf = x.astype(jnp.float32)
    y = xf * lax.rsqrt(jnp.mean(xf * xf, axis=-1, keepdims=True) + EPS)
    return (y * w.astype(jnp.float32)).astype(x.dtype)


def _ada(cond, w, b):
    mod = jax.nn.silu(cond) @ w + b
    return jnp.split(mod[..., None, :], N_MOD, axis=-1)


def _modulate(x, w, shift, scale):
    return _rmsnorm(x, w) * (1 + scale) + shift


def _dwconv3(x, w, b):
    xp = jnp.pad(x, ((0, 0), (1, 1), (0, 0)))
    return xp[:, :-2] * w[0] + xp[:, 1:-1] * w[1] + xp[:, 2:] * w[2] + b


def _axial_rope_tables(length):
    n_rows = length // GRID_W
    rows = jnp.repeat(jnp.arange(n_rows, dtype=jnp.float32), GRID_W)
    cols = jnp.tile(jnp.arange(GRID_W, dtype=jnp.float32), n_rows)
    inv = ROPE_BASE ** (-jnp.arange(N_FREQ, dtype=jnp.float32) / N_FREQ)
    ang = jnp.stack([rows[:, None] * inv, cols[:, None] * inv], axis=1)
    return jnp.cos(ang), jnp.sin(ang)


def _rope(x, cos, sin):
    xs = x.reshape(*x.shape[:-1], 2, 2, N_FREQ)
    x1, x2 = xs[..., 0, :], xs[..., 1, :]
    c = cos.reshape(cos.shape[0], 1, 1, 2, N_FREQ).astype(x.dtype)
    s = sin.reshape(sin.shape[0], 1, 1, 2, N_FREQ).astype(x.dtype)
    out = jnp.stack([x1 * c - x2 * s, x2 * c + x1 * s], axis=-2)
    return out.reshape(x.shape)


def _diff_attention(q, k, v, lam):
    Bn, Lq = q.shape[:2]
    nb = Lq // Q_BLOCK
    qb = jnp.moveaxis(q.reshape(Bn, nb, Q_BLOCK, *q.shape[2:]), 1, 0)
    scale = A_QK ** -0.5

    def block(qi):
        s = jnp.einsum('bqhmd,bkhmd->mbhqk', qi, k).astype(jnp.float32) * scale
        p = jax.nn.softmax(s, axis=-1)
        w = (p[0] - lam * p[1]).astype(v.dtype)
        return jnp.einsum('bhqk,bkhe->bqhe', w, v)

    out = lax.map(block, qb)
    return jnp.moveaxis(out, 0, 1).reshape(Bn, Lq, A_HEADS, A_V)


def _chunk_gmlp(u, v, norm_w, w_s, b_s):
    Bn, L, _ = u.shape
    vc = _rmsnorm(v, norm_w).reshape(Bn, L // CHUNK, CHUNK, B_GROUPS, B_CH)
    s = jnp.einsum('gts,bnsgc->bntgc', w_s, vc) + b_s.T[:, :, None]
    return u * s.reshape(Bn, L, B_WIDTH)


def _even_mixer(h, layer_idx, w_in, lq1, lk1, lq2, lk2, subln_w, gn_w, w_s, b_s, w_out,
                rope=None, ctx_k=None, ctx_v=None):
    Bn, L, _ = h.shape
    z = h @ w_in
    q, k, v, gu, gv = jnp.split(z, [A_QKW, 2 * A_QKW, 2 * A_QKW + A_WIDTH,
                                    2 * A_QKW + A_WIDTH + B_WIDTH], axis=-1)
    q = q.reshape(Bn, L, A_HEADS, 2, A_QK)
    k = k.reshape(Bn, L, A_HEADS, 2, A_QK)
    v = v.reshape(Bn, L, A_HEADS, A_V)
    k_own, v_own = k, v
    if rope is not None:
        q = _rope(q, *rope)
        k = jnp.concatenate([ctx_k.reshape(Bn, -1, A_HEADS, 2, A_QK).astype(k.dtype),
                             _rope(k, *rope)], axis=1)
        v = jnp.concatenate([ctx_v.astype(v.dtype), v], axis=1)
    lam_init = 0.8 - 0.6 * math.exp(-0.3 * layer_idx)
    f32 = jnp.float32
    lam = (jnp.exp(jnp.sum(lq1.astype(f32) * lk1.astype(f32)))
           - jnp.exp(jnp.sum(lq2.astype(f32) * lk2.astype(f32))) + lam_init)
    a = _diff_attention(q, k, v, lam)
    a = (_rmsnorm(a, subln_w) * (1 - lam_init)).reshape(Bn, L, A_WIDTH)
    g = _chunk_gmlp(jax.nn.gelu(gu), jax.nn.gelu(gv), gn_w, w_s, b_s)
    y = jnp.concatenate([a, g], axis=-1) @ w_out
    return y, k_own.reshape(Bn, L, A_HEADS, 2 * A_QK), v_own


def _mlstm_scan(q, k, v, li, lf, C0, n0, m0):
    Bn, H, L, _ = q.shape
    nc = L // CHUNK

    def chunks(t):
        return jnp.moveaxis(t.reshape(Bn, H, nc, CHUNK, *t.shape[3:]), 2, 0)

    lower = jnp.tril(jnp.ones((CHUNK, CHUNK), dtype=bool))

    def step(carry, xs):
        C, n, m = carry
        qc, kc, vc, ic, fc = xs
        b = jnp.cumsum(fc, axis=-1)
        a_inter = b + m[..., None]
        d = jnp.where(lower, b[..., :, None] - b[..., None, :] + ic[..., None, :], -jnp.inf)
        m_t = jnp.maximum(a_inter, jnp.max(d, axis=-1))
        w_inter = jnp.exp(a_inter - m_t)
        s = jnp.einsum('bhtd,bhsd->bhts', qc, kc) * jnp.exp(d - m_t[..., None])
        num = (jnp.einsum('bhts,bhse->bhte', s, vc)
               + w_inter[..., None] * jnp.einsum('bhtd,bhde->bhte', qc, C))
        den = jnp.sum(s, axis=-1) + w_inter * jnp.einsum('bhtd,bhd->bht', qc, n)
        h = num / jnp.maximum(jnp.abs(den), jnp.exp(-m_t))[..., None]
        b_end = b[..., -1]
        g = b_end[..., None] - b + ic
        m_new = jnp.maximum(b_end + m, jnp.max(g, axis=-1))
        decay = jnp.exp(b_end + m - m_new)
        ws = jnp.exp(g - m_new[..., None])
        C_new = decay[..., None, None] * C + jnp.einsum('bhs,bhsd,bhse->bhde', ws, kc, vc)
        n_new = decay[..., None] * n + jnp.einsum('bhs,bhsd->bhd', ws, kc)
        return (C_new, n_new, m_new), h

    (C, n, m), hs = lax.scan(step, (C0, n0, m0),
                             (chunks(q), chunks(k), chunks(v), chunks(li), chunks(lf)))
    return jnp.moveaxis(hs, 0, 2).reshape(Bn, H, L, -1), C, n, m


def _mlstm_bidir(q, k, v, gates, init_f, init_b):
    rev = lambda t: jnp.flip(t, axis=2)
    h_f, C_f, n_f, m_f = _mlstm_scan(q, k, v, gates[0], jax.nn.log_sigmoid(gates[1]), *init_f)
    h_b, C_b, n_b, m_b = _mlstm_scan(rev(q), rev(k), rev(v), rev(gates[2]),
                                     rev(jax.nn.log_sigmoid(gates[3])), *init_b)
    return h_f + rev(h_b), (C_f, n_f, m_f), (C_b, n_b, m_b)


def _odd_mixer(h, init_f, init_b, w_in, b_g, cw, cb, wq, wk, wv, hn_w, skip, w_out):
    Bn, L, _ = h.shape
    f32 = jnp.float32
    z = h @ w_in
    xm, og, g = jnp.split(z, [C_INNER, 2 * C_INNER], axis=-1)
    xc = jax.nn.silu(_dwconv3(xm, cw, cb))
    heads = lambda t: t.reshape(Bn, L, C_HEADS, C_HD)
    q = jnp.einsum('blhd,hde->bhle', heads(xc), wq).astype(f32)
    k = (jnp.einsum('blhd,hde->bhle', heads(xc), wk) * C_HD ** -0.5).astype(f32)
    v = jnp.einsum('blhd,hde->bhle', heads(xm), wv).astype(f32)
    gates = (g + b_g).astype(f32).reshape(Bn, L, 4, C_HEADS).transpose(2, 0, 3, 1)
    hsum, st_f, st_b = _mlstm_bidir(q, k, v, gates, init_f, init_b)
    hsum = hsum.transpose(0, 2, 1, 3).astype(h.dtype)
    hn = _rmsnorm(hsum, hn_w.reshape(C_HEADS, C_HD)).reshape(Bn, L, C_INNER)
    y = jax.nn.sigmoid(og) * (hn + skip * xc)
    return y @ w_out, st_f, st_b


def _conv_ffn(h, w_up, cw, cb, w_down):
    a, g = jnp.split(h @ w_up, 2, axis=-1)
    return (jax.nn.gelu(_dwconv3(a, cw, cb)) * g) @ w_down


def setup_inputs(seed: int = 0) -> dict:
    key = jax.random.key(seed)
    ks = jax.random.split(key, 48)
    nrm = lambda k, shape, scale: scale * jax.random.normal(k, shape, jnp.float32)
    gain = lambda k, shape: 1.0 + 0.05 * jax.random.normal(k, shape, jnp.float32)
    d = D_MODEL
    f_bias = jnp.broadcast_to(jnp.linspace(3.0, 6.0, C_HEADS, dtype=jnp.float32), (N_ODD, C_HEADS))
    b_gates = jnp.concatenate([
        nrm(ks[20], (N_ODD, C_HEADS), 0.1),
        f_bias + nrm(ks[21], (N_ODD, C_HEADS), 0.01),
        nrm(ks[22], (N_ODD, C_HEADS), 0.1),
        f_bias + nrm(ks[23], (N_ODD, C_HEADS), 0.01)], axis=-1)
    return {
        'x_prompt': nrm(ks[0], (BATCH, SEQ, d), 1.0),
        'x_sample': nrm(ks[1], (DEC_BATCH, DEC_SEQ, d), 1.0),
        'cache_k': nrm(ks[2], (DEC_BATCH, N_EVEN, PAST_LEN, A_HEADS, 2 * A_QK), 1.0),
        'cache_v': nrm(ks[3], (DEC_BATCH, N_EVEN, PAST_LEN, A_HEADS, A_V), 1.0),
        'state_C': nrm(ks[4], (DEC_BATCH, N_ODD, 2, C_HEADS, C_HD, C_HD), 0.05),
        'state_n': nrm(ks[5], (DEC_BATCH, N_ODD, 2, C_HEADS, C_HD), 0.05),
        'state_m': nrm(ks[6], (DEC_BATCH, N_ODD, 2, C_HEADS), 0.5),
        'c': nrm(ks[7], (DEC_BATCH, d), 1.0),
        'c_ctx': nrm(ks[8], (d,), 1.0),
        'w_mod': nrm(ks[9], (DEPTH, d, N_MOD * d), 0.5 * d ** -0.5),
        'b_mod': nrm(ks[10], (DEPTH, N_MOD * d), 0.02),
        'norm1_w': gain(ks[11], (DEPTH, d)),
        'norm2_w': gain(ks[12], (DEPTH, d)),
        'w_in0': nrm(ks[13], (N_EVEN, d, IN0), d ** -0.5),
        'lam_q1': nrm(ks[14], (N_EVEN, A_QK), 0.1),
        'lam_k1': nrm(ks[15], (N_EVEN, A_QK), 0.1),
        'lam_q2': nrm(ks[16], (N_EVEN, A_QK), 0.1),
        'lam_k2': nrm(ks[17], (N_EVEN, A_QK), 0.1),
        'subln_w': gain(ks[18], (N_EVEN, A_V)),
        'gate_norm_w': gain(ks[19], (N_EVEN, B_WIDTH)),
        'w_spatial': nrm(ks[24], (N_EVEN, B_GROUPS, CHUNK, CHUNK), CHUNK ** -0.5),
        'b_spatial': 1.0 + nrm(ks[25], (N_EVEN, B_GROUPS, CHUNK), 0.02),
        'w_out0': nrm(ks[26], (N_EVEN, MIX0, d), MIX0 ** -0.5),
        'w_in1': nrm(ks[27], (N_ODD, d, IN1), d ** -0.5),
        'b_gates': b_gates,
        'mconv_w': nrm(ks[28], (N_ODD, 3, C_INNER), 0.5),
        'mconv_b': nrm(ks[29], (N_ODD, C_INNER), 0.02),
        'w_q': nrm(ks[30], (N_ODD, C_HEADS, C_HD, C_HD), C_HD ** -0.5),
        'w_k': nrm(ks[31], (N_ODD, C_HEADS, C_HD, C_HD), C_HD ** -0.5),
        'w_v': nrm(ks[32], (N_ODD, C_HEADS, C_HD, C_HD), C_HD ** -0.5),
        'head_norm_w': gain(ks[33], (N_ODD, C_INNER)),
        'skip_w': gain(ks[34], (N_ODD, C_INNER)),
        'w_out1': nrm(ks[35], (N_ODD, C_INNER, d), C_INNER ** -0.5),
        'w_up': nrm(ks[36], (DEPTH, d, 2 * D_FF), d ** -0.5),
        'fconv_w': nrm(ks[37], (DEPTH, 3, D_FF), 0.5),
        'fconv_b': nrm(ks[38], (DEPTH, D_FF), 0.02),
        'w_down': nrm(ks[39], (DEPTH, D_FF, d), D_FF ** -0.5),
        'final_norm_w': gain(ks[40], (d,)),
    }


def reference(x_prompt, x_sample, cache_k, cache_v, state_C, state_n, state_m, c, c_ctx,
              w_mod, b_mod, norm1_w, norm2_w,
              w_in0, lam_q1, lam_k1, lam_q2, lam_k2, subln_w, gate_norm_w, w_spatial, b_spatial, w_out0,
              w_in1, b_gates, mconv_w, mconv_b, w_q, w_k, w_v, head_norm_w, skip_w, w_out1,
              w_up, fconv_w, fconv_b, w_down, final_norm_w):
    f32 = jnp.float32

    x = x_prompt
    Bp = x_prompt.shape[0]
    ks, vs, Cs, ns, ms = [], [], [], [], []
    for l in range(DEPTH):
        sh1, sc1, g1, sh2, sc2, g2 = _ada(c_ctx, w_mod[l], b_mod[l])
        h = _modulate(x, norm1_w[l], sh1, sc1)
        if l % 2 == 0:
            e = l // 2
            y, k_ctx, v_ctx = _even_mixer(h, l, w_in0[e], lam_q1[e], lam_k1[e], lam_q2[e], lam_k2[e],
                                          subln_w[e], gate_norm_w[e], w_spatial[e], b_spatial[e], w_out0[e])
            ks.append(k_ctx)
            vs.append(v_ctx)
        else:
            o = l // 2
            zero = (jnp.zeros((Bp, C_HEADS, C_HD, C_HD), f32), jnp.zeros((Bp, C_HEADS, C_HD), f32),
                    jnp.zeros((Bp, C_HEADS), f32))
            y, st_f, st_b = _odd_mixer(h, zero, zero, w_in1[o], b_gates[o], mconv_w[o], mconv_b[o],
                                       w_q[o], w_k[o], w_v[o], head_norm_w[o], skip_w[o], w_out1[o])
            Cs.append(jnp.stack([st_f[0], st_b[0]], axis=1))
            ns.append(jnp.stack([st_f[1], st_b[1]], axis=1))
            ms.append(jnp.stack([st_f[2], st_b[2]], axis=1))
        x = x + g1 * y
        h = _modulate(x, norm2_w[l], sh2, sc2)
        x = x + g2 * _conv_ffn(h, w_up[l], fconv_w[l], fconv_b[l], w_down[l])
    y_prompt = _rmsnorm(x, final_norm_w)
    new_cache_k = jnp.stack(ks, axis=1)
    new_cache_v = jnp.stack(vs, axis=1)
    new_state_C = jnp.stack(Cs, axis=1)
    new_state_n = jnp.stack(ns, axis=1)
    new_state_m = jnp.stack(ms, axis=1)

    x = x_sample
    rope = _axial_rope_tables(x_sample.shape[1])
    for l in range(DEPTH):
        sh1, sc1, g1, sh2, sc2, g2 = _ada(c, w_mod[l], b_mod[l])
        h = _modulate(x, norm1_w[l], sh1, sc1)
        if l % 2 == 0:
            e = l // 2
            y, _, _ = _even_mixer(h, l, w_in0[e], lam_q1[e], lam_k1[e], lam_q2[e], lam_k2[e],
                                  subln_w[e], gate_norm_w[e], w_spatial[e], b_spatial[e], w_out0[e],
                                  rope=rope, ctx_k=cache_k[:, e], ctx_v=cache_v[:, e])
        else:
            o = l // 2
            init_f = (state_C[:, o, 0].astype(f32), state_n[:, o, 0].astype(f32), state_m[:, o, 0].astype(f32))
            init_b = (state_C[:, o, 1].astype(f32), state_n[:, o, 1].astype(f32), state_m[:, o, 1].astype(f32))
            y, _, _ = _odd_mixer(h, init_f, init_b, w_in1[o], b_gates[o], mconv_w[o], mconv_b[o],
                                 w_q[o], w_k[o], w_v[o], head_norm_w[o], skip_w[o], w_out1[o])
        x = x + g1 * y
        h = _modulate(x, norm2_w[l], sh2, sc2)
        x = x + g2 * _conv_ffn(h, w_up[l], fconv_w[l], fconv_b[l], w_down[l])
    y_sample = _rmsnorm(x, final_norm_w)

    return (y_prompt, y_sample, new_cache_k, new_cache_v, new_state_C, new_state_n, new_state_m)
```

```python
import math
import numpy as np
import concourse.bass as bass
import concourse.mybir as mybir
from concourse.bass_utils import run_bass_kernel_spmd
from contextlib import ExitStack

F32 = mybir.dt.float32
BF16 = mybir.dt.bfloat16
AF = mybir.ActivationFunctionType
ALU = mybir.AluOpType
AX = mybir.AxisListType
COMPUTE = ('pe', 'act', 'dve', 'pool')
EPS = 1e-6
STRICT_SAME_ENGINE = True
NEG = 30000.0


class Tok:
    __slots__ = ('name', 'w', 'r', 'rd')

    def __init__(self, name):
        self.name = name
        self.w = None
        self.r = {}
        self.rd = []


class Ins:
    __slots__ = ('eng', 'fn', 'deps', 'dma', 'signals', 'sem', 'tick', 'prev_same_sem')

    def __init__(self, eng, fn, dma):
        self.eng = eng
        self.fn = fn
        self.dma = dma
        self.deps = []
        self.signals = dma
        self.sem = None
        self.tick = 0
        self.prev_same_sem = None


class _Rec:
    def __init__(self):
        self.call = None

    def __getattr__(self, name):
        def f(*a, **k):
            self.call = (name, a, k)
            return self
        return f


class Prog:
    def __init__(self, nc, n_dma_sems=16):
        self.nc = nc
        self.q = {e: [] for e in ('pe', 'act', 'dve', 'pool', 'sp')}
        self.toks = {}
        self.n_dma_sems = n_dma_sems
        self.es = ExitStack()
        self.n_alloc = 0

    def sbuf(self, shape, dtype, name=None):
        self.n_alloc += 1
        return self.es.enter_context(self.nc.sbuf_tensor("sb_" + (name or f"t{self.n_alloc}"), list(shape), dtype))

    def psum(self, shape, dtype, name=None):
        self.n_alloc += 1
        return self.es.enter_context(self.nc.psum_tensor("ps_" + (name or f"t{self.n_alloc}"), list(shape), dtype))

    def tok(self, *key):
        t = self.toks.get(key)
        if t is None:
            t = Tok(key)
            self.toks[key] = t
        return t

    def op(self, eng, fn, reads=(), writes=(), dma=False):
        rec = _Rec()
        fn(rec)
        assert rec.call is not None
        ins = Ins(eng, rec.call, dma)
        deps = {}

        def add(p, raw):
            if p is None or p is ins:
                return
            if (not p.dma) and (not dma) and p.eng == eng:
                if eng == 'pe' or (not raw and not STRICT_SAME_ENGINE):
                    return
            deps[id(p)] = p

        for t in reads:
            add(t.w, True)
            if t.name[0] == 'bank':
                for r in t.r.values():
                    if r.eng != eng:
                        add(r, False)
        for t in writes:
            add(t.w, False)
            for r in t.r.values():
                add(r, False)
            for r in t.rd:
                add(r, False)
        for t in reads:
            if dma:
                t.rd.append(ins)
            else:
                t.r[eng] = ins
        for t in writes:
            t.w = ins
            t.r = {}
            t.rd = []
        ins.deps = list(deps.values())
        for p in ins.deps:
            p.signals = True
        self.q[eng].append(ins)
        return ins

    def emit(self):
        nc = self.nc
        es = self.es
        sems = {e: es.enter_context(nc.semaphore(f"s_{e}")) for e in COMPUTE}
        dsems = {e: [es.enter_context(nc.semaphore(f"d_{e}{i}")) for i in range(self.n_dma_sems)]
                 for e in ('sp', 'pool', 'act')}
        final = {}
        for e in COMPUTE + ('sp',):
            cnt = 0
            dcnt = [0] * self.n_dma_sems
            dlast = [None] * self.n_dma_sems
            rr = 0
            for ins in self.q[e]:
                if ins.dma:
                    s = rr % self.n_dma_sems
                    rr += 1
                    dcnt[s] += 16
                    ins.sem = dsems[e][s]
                    ins.tick = dcnt[s]
                    ins.prev_same_sem = dlast[s]
                    dlast[s] = ins
                elif ins.signals:
                    cnt += 1
                    ins.sem = sems[e]
                    ins.tick = cnt
            final[e] = [(dsems[e][s], dcnt[s]) for s in range(self.n_dma_sems) if dcnt[s]]
        self.stats = {e: len(self.q[e]) for e in self.q}
        nwaits = {e: 0 for e in self.q}

        def run(e, eng):
            seen = {}

            def wait(sem, val):
                k = id(sem)
                if seen.get(k, 0) >= val:
                    return
                seen[k] = val
                eng.wait_ge(sem, val)
                nwaits[e] += 1

            for ins in self.q[e]:
                if ins.dma and ins.prev_same_sem is not None:
                    wait(ins.prev_same_sem.sem, ins.prev_same_sem.tick)
                for p in ins.deps:
                    wait(p.sem, p.tick)
                name, a, k = ins.fn
                bi = getattr(eng, name)(*a, **k)
                if ins.dma:
                    bi.then_inc(ins.sem, 16)
                elif ins.signals:
                    bi.then_inc(ins.sem, 1)
            for (sem, val) in final.get(e, []):
                wait(sem, val)

        with nc.Block() as block:
            @block.tensor
            def _(eng):
                run('pe', eng)

            @block.scalar
            def _(eng):
                run('act', eng)

            @block.vector
            def _(eng):
                run('dve', eng)

            @block.gpsimd
            def _(eng):
                run('pool', eng)

            @block.sync
            def _(eng):
                run('sp', eng)
        self.nwaits = nwaits
        es.close()


class Buf:
    def __init__(self, ap2d, toks, cpp, w):
        self.ap = ap2d
        self.toks = toks
        self.cpp = cpp
        self.w = w

    def v(self, i, a=0, b=None):
        b = self.w if b is None else b
        return self.ap[:, i * self.w + a: i * self.w + b]

    def t(self, i, a=0, b=None):
        b = self.w if b is None else b
        c0 = i * self.w + a
        c1 = i * self.w + b
        return self.toks[c0 // self.cpp: (c1 - 1) // self.cpp + 1]

    def v3(self, i0, i1, a=0, b=None):
        b = self.w if b is None else b
        return self.ap[:, i0 * self.w: i1 * self.w].rearrange("p (n c) -> p n c", c=self.w)[:, :, a:b]

    def t3(self, i0, i1, a=0, b=None):
        out = []
        seen = set()
        for i in range(i0, i1):
            for t in self.t(i, a, b):
                if id(t) not in seen:
                    seen.add(id(t))
                    out.append(t)
        return out

    def all(self):
        return self.toks


def chunkmajor(W, cs=128):
    K, N = W.shape
    kc = K // 128
    n = N // cs
    return np.ascontiguousarray(W.reshape(kc, 128, n, cs).transpose(2, 1, 0, 3).reshape(n, 128, kc * cs))


def colvec(v):
    return np.ascontiguousarray(np.asarray(v).reshape(-1, 128).T)


VEC_LAYOUT = [('bmod0', 48), ('bmod1', 48), ('n1w0', 8), ('n1w1', 8), ('n2w0', 8), ('n2w1', 8), ('fnw', 8),
              ('fcw0', 66), ('fcw1', 66), ('fcb0', 22), ('fcb1', 22), ('mcw', 24), ('mcb', 8), ('hnw', 8),
              ('skw', 8), ('sbw', 1), ('lam', 4), ('gnw', 512), ('bg', 16), ('wsT', 512)]
VEC_OFF = {}
_o = 0
for _n, _w in VEC_LAYOUT:
    VEC_OFF[_n] = (_o, _w)
    _o += _w
NVEC = _o
CST_LAYOUT = ['ident', 'ones', 'Uf', 'Ub', 'self', 'selb', 'perm', 'nmf', 'nmb', 'pmf', 'pmb']
CST_OFF = {n: i * 128 for i, n in enumerate(CST_LAYOUT)}
NCST = 128 * len(CST_LAYOUT)


def make_consts():
    c = np.zeros((128, NCST), np.float32)
    i = np.arange(128)
    s, t = np.meshgrid(i, i, indexing='ij')

    def put(n, m):
        c[:, CST_OFF[n]:CST_OFF[n] + 128] = m
    put('ident', np.eye(128))
    put('ones', np.ones((128, 128)))
    put('Uf', (s <= t))
    put('Ub', (s >= t))
    put('self', (s == 127))
    put('selb', (s == 0))
    perm = np.zeros((128, 128))
    for p in range(128):
        half = (p % 32) // 16
        partner = p + 16 if half == 0 else p - 16
        perm[partner, p] = 1.0
    put('perm', perm)
    put('nmf', np.where(t <= s, 0.0, -NEG))
    put('nmb', np.where(t >= s, 0.0, -NEG))
    put('pmf', np.where(s <= t, 0.0, NEG))
    put('pmb', np.where(s >= t, 0.0, NEG))
    return c


def rope_tables(T):
    n_freq = 16
    inv = (10000.0 ** (-np.arange(n_freq, dtype=np.float32) / n_freq)).astype(np.float32)
    tt = np.arange(T)
    rows = (tt // 64).astype(np.float32)
    cols = (tt % 64).astype(np.float32)
    C = np.zeros((128, T), np.float32)
    S = np.zeros((128, T), np.float32)
    for p in range(128):
        axis = (p % 64) // 32
        half = (p % 32) // 16
        f = p % 16
        pos = rows if axis == 0 else cols
        ang = (pos * inv[f]).astype(np.float32)
        C[p] = np.cos(ang)
        S[p] = np.sin(ang) * (-1.0 if half == 0 else 1.0)
    return C, S


class Builder:
    def __init__(self, TS, NPR, TP, PAST):
        self.TS, self.NPR, self.TP, self.PAST = TS, NPR, TP, PAST
        self.TPT = NPR * TP
        nc = bass.Bass("TRN2", target_bir_lowering=False)
        self.nc = nc
        self.P = Prog(nc)
        self.d = {}
        self.decl_io()
        self.alloc()
        self.prologue()
        self.early_x = None
        gp = dict(name='p', X=self.XP, T=self.TPT, seqs=[(i * TP, TP) for i in range(NPR)], sample=False, cond=1)
        gs = dict(name='s', X=self.XS, T=TS, seqs=[(0, TS)], sample=True, cond=0)
        st = 'x0f1gy'
        grp = 'ps'
        early = ('x' in st) and ('s' in grp) and TS > self.TPT
        if early:
            self.load_x(gs, sts=range(self.TPT // 128, TS // 128))
        self.prologue_mods()
        for g in (gp, gs):
            if g['name'] not in grp:
                continue
            if 'x' in st:
                if g['sample'] and early:
                    self.load_x(g, sts=range(0, self.TPT // 128))
                else:
                    self.load_x(g)
            if '0' in st:
                self.layer0(g)
            if 'f' in st:
                self.ffn(g, 0)
            if '1' in st:
                self.layer1(g)
            if 'g' in st:
                self.ffn(g, 1)
            if 'y' in st:
                self.final(g)
        self.P.emit()

    def din(self, name, shape, dt=F32):
        self.d[name] = self.nc.dram_tensor(name, list(shape), dt, kind="ExternalInput").ap()

    def dout(self, name, shape):
        self.d[name] = self.nc.dram_tensor(name, list(shape), F32, kind="ExternalOutput").ap()

    def decl_io(self):
        TS, TPT, PAST, NPR = self.TS, self.TPT, self.PAST, self.NPR
        self.din('xs', [TS, 1024]); self.din('xp', [TPT, 1024])
        self.din('ck', [PAST, 512]); self.din('cv', [PAST, 512])
        self.din('sC', [8, 2, 128, 256]); self.din('sn', [128, 16]); self.din('sm', [128, 8])
        self.din('cond', [128, 16])
        self.din('w_in0c', [20, 128, 1024]); self.din('w_in0w', [3, 128, 4096]); self.din('w_out0c', [8, 128, 1024])
        self.din('w_in1c', [16, 128, 1024]); self.din('w_in1g', [128, 128])
        self.din('wqc', [8, 128, 256]); self.din('wkc', [8, 128, 256]); self.din('wkw', [4, 128, 512]); self.din('wvw', [4, 128, 512])
        self.din('w_out1c', [8, 128, 1024]); self.din('w_upc', [88, 128, 1024]); self.din('w_downc', [16, 128, 2816])
        self.din('w_modc', [96, 128, 1024])
        self.din('vec', [128, NVEC]); self.din('bsrow', [1, 512]); self.din('cst', [128, NCST])
        self.din('ropeC', [128, TS]); self.din('ropeS', [128, TS])
        self.dout('ys', [TS, 1024]); self.dout('yp', [TPT, 1024])
        self.dout('nk', [TPT, 512]); self.dout('nv', [TPT, 512])
        self.dout('nC', [NPR, 8, 2, 128, 256]); self.dout('nn', [NPR, 128, 16]); self.dout('nm', [NPR, 1, 8])
        TK = PAST + TS
        self.TK = TK
        self.d['KTd'] = self.nc.dram_tensor('KTd', [4, 128, TK], BF16, kind="Internal").ap()
        self.d['Vd'] = self.nc.dram_tensor('Vd', [4, 128, TK // 128, 128], BF16, kind="Internal").ap()
        self.d['Hfd'] = self.nc.dram_tensor('Hfd', [TS // 128, 128, 1024], BF16, kind="Internal").ap()

    def page_buf(self, n, w, dtype):
        size = 4 if dtype == F32 else 2
        cols = n * w
        pages = (cols * size + 1023) // 1024
        p0 = self.page_next
        self.page_next += pages
        assert self.page_next <= self.NPAGES, ("arena overflow", self.page_next)
        ap = self.arena[:, p0 * 512:(p0 + pages) * 512]
        if dtype == F32:
            ap = ap.bitcast(F32)
        ap = ap[:, 0:cols]
        toks = [self.P.tok('pg', p0 + i) for i in range(pages)]
        return Buf(ap, toks, 1024 // size, w)

    def own_buf(self, n, w, dtype, name):
        t = self.P.sbuf([128, n * w], dtype, name)
        size = 4 if dtype == F32 else 2
        cpp = 1024 // size
        pages = (n * w + cpp - 1) // cpp
        toks = [self.P.tok(name, i) for i in range(pages)]
        return Buf(t[:], toks, cpp, w)

    def alloc(self):
        P = self.P
        self.XS = self.own_buf(8, self.TS, BF16, 'XS')
        self.XP = self.XS
        self.vec = P.sbuf([128, NVEC], F32, 'vec')
        self.cst = P.sbuf([128, 7 * 128], F32, 'cst')
        self.cb = P.sbuf([128, 7 * 128], BF16, 'cb')
        self.tv = P.tok('vec'); self.tc = P.tok('cst'); self.tcb = P.tok('cb')
        self.mods = P.sbuf([128, 2 * 96], F32, 'mods')
        self.AB = P.sbuf([128, 2 * 2 * 6 * 8], F32, 'AB')
        self.tAB = P.tok('AB')
        self.small = P.sbuf([128, 64], F32, 'small')
        self.tsmall = P.tok('small')
        self.bsrow = P.sbuf([1, 512], BF16, 'bsrow')
        self.wsTb = P.sbuf([128, 512], BF16, 'wsTb')
        self.wgb = P.sbuf([128, 128], BF16, 'wgb')
        self.NPAGES = 127
        self.arena = P.sbuf([128, self.NPAGES * 512], BF16, 'arena')[:]
        self.page_next = 0
        self.NSLOT = 8
        self.slots = [self.page_buf(1, 1024, BF16) for _ in range(self.NSLOT)]
        self.wslots = []
        self.slot_rr = 0
        self.wslot_rr = 0
        self.page_base = self.page_next
        self.banks = [P.psum([128, 512], F32, f'bank{i}') for i in range(8)]
        self.btok = [P.tok('bank', i) for i in range(8)]
        self.bank_rr = 0
        self.held = set()

    def reset_pages(self):
        self.page_next = self.page_base

    def bank(self, hold=False):
        while True:
            i = self.bank_rr % 8
            self.bank_rr += 1
            if i not in self.held:
                break
        if hold:
            self.held.add(i)
        return self.banks[i], self.btok[i]

    def bank_wait(self, limit=8):
        while len(self.held) >= limit:
            yield
        return self.bank(hold=True)

    def release(self, *bks):
        for (b, t) in bks:
            self.held.discard(self.btok.index(t))

    def vcol(self, name, a=0, b=None):
        o, w = VEC_OFF[name]
        b = w if b is None else b
        return self.vec[:, o + a:o + b]

    def ccol(self, name):
        o = CST_OFF[name]
        return self.cst[:, o:o + 128]

    def AB_(self, l, cond, which, kc=None):
        base = ((l * 2 + cond) * 6 + which) * 8
        if kc is None:
            return self.AB[:, base:base + 8]
        return self.AB[:, base + kc:base + kc + 1]

    def wload(self, src_ap, ncols):
        s = self.slots[self.slot_rr % self.NSLOT]
        self.slot_rr += 1
        self.P.op('pool', lambda e, s=s, src_ap=src_ap, ncols=ncols: e.dma_start(out=s.ap[:, 0:ncols], in_=src_ap),
                  writes=s.all(), dma=True)
        return s

    def wload_wide(self, src_ap, ncols):
        s = self.wslots[self.wslot_rr % len(self.wslots)]
        self.wslot_rr += 1
        self.P.op('pool', lambda e, s=s, src_ap=src_ap, ncols=ncols: e.dma_start(out=s.ap[:, 0:ncols], in_=src_ap),
                  writes=s.all(), dma=True)
        return s

    def prologue(self):
        P, d = self.P, self.d
        P.op('sp', lambda e: e.dma_start(out=self.vec[:], in_=d['vec'][:, :]), writes=[self.tv], dma=True)
        P.op('sp', lambda e: e.dma_start(out=self.cst[:], in_=d['cst'][:, 0:7 * 128]), writes=[self.tc], dma=True)
        P.op('pool', lambda e: e.dma_start(out=self.cb[:, 3 * 128:7 * 128], in_=d['cst'][:, 7 * 128:11 * 128]), writes=[P.tok('cbm')], dma=True)
        P.op('pool', lambda e: e.dma_start(out=self.bsrow[:], in_=d['bsrow'][:, :]), writes=[P.tok('bsrow')], dma=True)
        P.op('pool', lambda e: e.dma_start(out=self.wgb[:], in_=d['w_in1g'][:, :]), writes=[P.tok('wgb')], dma=True)
        for i, n in enumerate(('ident', 'ones', 'perm')):
            P.op('dve', lambda e, i=i, n=n: e.tensor_copy(out=self.cb[:, i * 128:(i + 1) * 128], in_=self.ccol(n)),
                 reads=[self.tc], writes=[self.tcb])
        P.op('dve', lambda e: e.tensor_copy(out=self.wsTb[:], in_=self.vcol('wsT')), reads=[self.tv], writes=[P.tok('wsTb')])
        self.identb = self.cb[:, 0:128]; self.onesb = self.cb[:, 128:256]; self.permb = self.cb[:, 256:384]

    def prologue_mods(self):
        P, d = self.P, self.d
        self.page_next = self.page_base + 8
        cnd = self.page_buf(1, 16, F32)
        scb = self.page_buf(1, 16, BF16)
        P.op('sp', lambda e: e.dma_start(out=cnd.ap, in_=d['cond'][:, :]), writes=cnd.all(), dma=True)
        P.op('act', lambda e: e.activation(out=scb.ap, in_=cnd.ap, func=AF.Silu), reads=cnd.all(), writes=scb.all())
        tm = P.tok('mods')
        for l in range(2):
            ps, tps = self.bank()
            for c in range(48):
                s = self.wload(d['w_modc'][l * 48 + c], 1024)
                for kc in range(8):
                    P.op('pe', lambda e, ps=ps, s=s, kc=kc, c=c: e.matmul(
                        ps[:, c * 2:c * 2 + 2], lhsT=s.ap[:, kc * 128:(kc + 1) * 128], rhs=scb.ap[:, kc * 2:kc * 2 + 2],
                        start=(kc == 0), stop=(kc == 7)), reads=s.all() + scb.all(), writes=[tps])
            bm = self.vcol('bmod%d' % l)
            P.op('dve', lambda e, ps=ps, l=l, bm=bm: e.tensor_tensor(
                out=self.mods[:, l * 96:(l + 1) * 96].rearrange("p (c n) -> p c n", n=2),
                in0=ps[:, 0:96].rearrange("p (c n) -> p c n", n=2),
                in1=bm.unsqueeze(2).to_broadcast([128, 48, 2]), op=ALU.add), reads=[tps, self.tv], writes=[tm])
            for cond in range(2):
                def mv(which, l=l, cond=cond):
                    return self.mods[:, l * 96:(l + 1) * 96].rearrange("p (c n) -> p c n", n=2)[:, which * 8:(which + 1) * 8, cond]
                for half in range(2):
                    nw = self.vcol('n%dw%d' % (half + 1, l))
                    P.op('dve', lambda e, l=l, cond=cond, half=half, nw=nw, mv=mv: e.scalar_tensor_tensor(
                        out=self.AB_(l, cond, half * 3 + 0), in0=mv(half * 3 + 1), scalar=1.0, in1=nw,
                        op0=ALU.add, op1=ALU.mult), reads=[tm, self.tv], writes=[self.tAB])
                    P.op('dve', lambda e, l=l, cond=cond, half=half, mv=mv: e.tensor_copy(
                        out=self.AB_(l, cond, half * 3 + 1), in_=mv(half * 3 + 0)), reads=[tm], writes=[self.tAB])
                    P.op('dve', lambda e, l=l, cond=cond, half=half, mv=mv: e.tensor_copy(
                        out=self.AB_(l, cond, half * 3 + 2), in_=mv(half * 3 + 2)), reads=[tm], writes=[self.tAB])
        lam_init = 0.8 - 0.6 * math.exp(-0.3 * 0)
        lo, _ = VEC_OFF['lam']
        pr = self.page_buf(1, 2, F32)
        P.op('dve', lambda e: e.tensor_tensor(out=pr.ap[0:64, 0:1], in0=self.vec[0:64, lo:lo + 1], in1=self.vec[0:64, lo + 1:lo + 2], op=ALU.mult),
             reads=[self.tv], writes=pr.all())
        P.op('dve', lambda e: e.tensor_tensor(out=pr.ap[0:64, 1:2], in0=self.vec[0:64, lo + 2:lo + 3], in1=self.vec[0:64, lo + 3:lo + 4], op=ALU.mult),
             reads=[self.tv] + pr.all(), writes=pr.all())
        ps, tps = self.bank()
        P.op('pe', lambda e, ps=ps: e.matmul(ps[:, 0:2], lhsT=self.ccol('ones')[0:64, :], rhs=pr.ap[0:64, 0:2], start=True, stop=True),
             reads=[self.tc] + pr.all(), writes=[tps])
        ex = self.page_buf(1, 2, F32)
        P.op('act', lambda e, ps=ps: e.activation(out=ex.ap, in_=ps[:, 0:2], func=AF.Exp), reads=[tps], writes=ex.all())
        P.op('dve', lambda e: e.scalar_tensor_tensor(out=self.small[:, 0:1], in0=ex.ap[:, 1:2], scalar=-lam_init, in1=ex.ap[:, 0:1],
                                                     op0=ALU.add, op1=ALU.subtract), reads=ex.all(), writes=[self.tsmall])
        P.op('dve', lambda e: e.tensor_scalar(out=self.small[:, 1:2], in0=self.vcol('sbw'), scalar1=(1.0 - lam_init), scalar2=None, op0=ALU.mult),
             reads=[self.tv, self.tsmall], writes=[self.tsmall])

    def load_x(self, g, sts=None):
        P, d = self.P, self.d
        X = g['X']
        src = d['xs'] if g['sample'] else d['xp']
        self.reset_pages()
        stg = [self.page_buf(1, 1024, F32) for _ in range(2)]
        for st in (range(g['T'] // 128) if sts is None else sts):
            sb = stg[st % 2]
            P.op('sp', lambda e, sb=sb, st=st: e.dma_start(out=sb.ap, in_=src[st * 128:(st + 1) * 128, :]), writes=sb.all(), dma=True)
            for half in range(2):
                ps, tps = self.bank()
                for k4 in range(4):
                    kc = half * 4 + k4
                    P.op('pe', lambda e, ps=ps, sb=sb, kc=kc, k4=k4: e.transpose(ps[:, k4 * 128:(k4 + 1) * 128], sb.ap[:, kc * 128:(kc + 1) * 128], self.ccol('ident')),
                         reads=sb.all() + [self.tc], writes=[tps])
                for k4 in range(4):
                    kc = half * 4 + k4
                    eng = 'act' if k4 % 2 == 0 else 'dve'
                    if eng == 'act':
                        P.op('act', lambda e, ps=ps, kc=kc, k4=k4, st=st: e.activation(out=X.v(kc, st * 128, (st + 1) * 128), in_=ps[:, k4 * 128:(k4 + 1) * 128], func=AF.Copy),
                             reads=[tps], writes=X.t(kc, st * 128, (st + 1) * 128))
                    else:
                        P.op('dve', lambda e, ps=ps, kc=kc, k4=k4, st=st: e.tensor_copy(out=X.v(kc, st * 128, (st + 1) * 128), in_=ps[:, k4 * 128:(k4 + 1) * 128]),
                             reads=[tps], writes=X.t(kc, st * 128, (st + 1) * 128))

    def mod_scratch(self, w=512):
        return dict(sq=[self.page_buf(1, w, BF16) for _ in range(2)], t32=[self.page_buf(1, w, F32) for _ in range(2)],
                    rs=self.page_buf(1, w, F32))

    def modulate(self, g, c0, n, A, B, dst, o0, ms):
        P = self.P
        X = g['X']
        rs = ms['rs']
        ps, tps = self.bank()
        for kc in range(8):
            sq = ms['sq'][kc % 2]
            P.op('act', lambda e, kc=kc, sq=sq: e.activation(out=sq.v(0, 0, n), in_=X.v(kc, c0, c0 + n), func=AF.Square),
                 reads=X.t(kc, c0, c0 + n), writes=sq.all())
            P.op('pe', lambda e, ps=ps, kc=kc, sq=sq: e.matmul(ps[:, 0:n], lhsT=self.onesb, rhs=sq.v(0, 0, n), start=(kc == 0), stop=(kc == 7)),
                 reads=sq.all() + [self.tcb], writes=[tps])
        P.op('act', lambda e, ps=ps: e.activation(out=rs.v(0, 0, n), in_=ps[:, 0:n], func=AF.Ln, scale=1.0 / 1024, bias=EPS),
             reads=[tps], writes=rs.all())
        P.op('act', lambda e: e.activation(out=rs.v(0, 0, n), in_=rs.v(0, 0, n), func=AF.Exp, scale=-0.5), reads=rs.all(), writes=rs.all())
        for kc in range(8):
            t32 = ms['t32'][kc % 2]
            P.op('dve', lambda e, kc=kc, t32=t32: e.tensor_tensor(out=t32.v(0, 0, n), in0=X.v(kc, c0, c0 + n), in1=rs.v(0, 0, n), op=ALU.mult),
                 reads=X.t(kc, c0, c0 + n) + rs.all(), writes=t32.all())
            if B is not None:
                P.op('act', lambda e, kc=kc, t32=t32: e.activation(out=dst.v(kc, o0, o0 + n), in_=t32.v(0, 0, n), func=AF.Identity, scale=A(kc), bias=B(kc)),
                     reads=t32.all() + [self.tAB, self.tv], writes=dst.t(kc, o0, o0 + n))
            else:
                P.op('act', lambda e, kc=kc, t32=t32: e.activation(out=dst.v(kc, o0, o0 + n), in_=t32.v(0, 0, n), func=AF.Identity, scale=A(kc)),
                     reads=t32.all() + [self.tAB, self.tv], writes=dst.t(kc, o0, o0 + n))

    def fm_proj(self, s, ncols_k, rhs_of, rhs_toks_of, n, nk=8):
        P = self.P
        ps, tps = self.bank()
        for kc in range(nk):
            P.op('pe', lambda e, ps=ps, kc=kc: e.matmul(ps[:, 0:n], lhsT=s.ap[:, kc * ncols_k:(kc + 1) * ncols_k], rhs=rhs_of(kc),
                                                        start=(kc == 0), stop=(kc == nk - 1)),
                 reads=s.all() + rhs_toks_of(kc), writes=[tps])
        return ps, tps

    def layer0(self, g):
        P, d = self.P, self.d
        X = g['X']
        sample = g['sample']
        cond = g['cond']
        PAST = self.PAST if sample else 0
        A1 = lambda kc: self.AB_(0, cond, 0, kc)
        B1 = lambda kc: self.AB_(0, cond, 1, kc)
        G1 = lambda kc: self.AB_(0, cond, 2, kc)
        self.reset_pages()
        hb = self.page_buf(8, 512, BF16)
        self.wslots = [self.page_buf(1, 4096, BF16)]
        ms = self.mod_scratch()
        ropeC = self.page_buf(1, 512, F32); ropeS = self.page_buf(1, 512, F32)
        rsets = [(self.page_buf(1, 512, BF16), self.page_buf(1, 512, F32), self.page_buf(1, 512, F32)) for _ in range(2)]
        rrot = [0]
        kst = [self.page_buf(1, 512, BF16) for _ in range(2)]
        vst = [self.page_buf(1, 512, BF16) for _ in range(2)]
        ost = [self.page_buf(1, 512, F32) for _ in range(2)]
        ntile = g['T'] // 512
        KTd, Vd = d['KTd'], d['Vd']
        tK = lambda h, ch: P.tok('KTd', h, ch)
        tV = lambda h, ch: P.tok('Vd', h, ch)

        def rope_or_copy(ps, tps, dst_ap, dst_toks, scale=None):
            if sample:
                zb, ta, tb = rsets[rrot[0] % len(rsets)]
                rrot[0] += 1
                P.op('act', lambda e: e.activation(out=zb.ap, in_=ps[:, 0:512], func=AF.Copy), reads=[tps], writes=zb.all())
                pp, tpp = self.bank()
                P.op('pe', lambda e: e.matmul(pp[:, 0:512], lhsT=self.permb, rhs=zb.ap, start=True, stop=True),
                     reads=zb.all() + [self.tcb], writes=[tpp])
                P.op('dve', lambda e: e.tensor_tensor(out=ta.ap, in0=ps[:, 0:512], in1=ropeC.ap, op=ALU.mult),
                     reads=[tps] + ropeC.all(), writes=ta.all())
                P.op('dve', lambda e: e.tensor_tensor(out=tb.ap, in0=pp[:, 0:512], in1=ropeS.ap, op=ALU.mult),
                     reads=[tpp] + ropeS.all(), writes=tb.all())
                P.op('dve', lambda e: e.tensor_tensor(out=dst_ap, in0=ta.ap, in1=tb.ap, op=ALU.add),
                     reads=ta.all() + tb.all(), writes=dst_toks)
            else:
                P.op('act', lambda e: e.activation(out=dst_ap, in_=ps[:, 0:512], func=AF.Copy), reads=[tps], writes=dst_toks)

        if sample:
            cs = [self.page_buf(1, 512, F32) for _ in range(2)]
            for st in range(self.PAST // 128):
                sb = cs[st % 2]
                P.op('sp', lambda e, sb=sb, st=st: e.dma_start(out=sb.ap, in_=d['ck'][st * 128:(st + 1) * 128, :]), writes=sb.all(), dma=True)
                ps, tps = self.bank()
                for h in range(4):
                    P.op('pe', lambda e, ps=ps, sb=sb, h=h: e.transpose(ps[:, h * 128:(h + 1) * 128], sb.ap[:, h * 128:(h + 1) * 128], self.ccol('ident')),
                         reads=sb.all() + [self.tc], writes=[tps])
                kb = kst[st % 2]
                P.op('act', lambda e, ps=ps, kb=kb: e.activation(out=kb.ap, in_=ps[:, 0:512], func=AF.Copy), reads=[tps], writes=kb.all())
                for h in range(4):
                    P.op('sp', lambda e, kb=kb, h=h, st=st: e.dma_start(out=KTd[h, :, st * 128:(st + 1) * 128], in_=kb.ap[:, h * 128:(h + 1) * 128]),
                         reads=kb.all(), writes=[tK(h, st)], dma=True)
                vb = vst[st % 2]
                P.op('pool', lambda e, vb=vb, st=st: e.dma_start(out=vb.ap, in_=d['cv'][st * 128:(st + 1) * 128, :]), writes=vb.all(), dma=True)
                for h in range(4):
                    P.op('sp', lambda e, vb=vb, h=h, st=st: e.dma_start(out=Vd[h, :, st, :], in_=vb.ap[:, h * 128:(h + 1) * 128]),
                         reads=vb.all(), writes=[tV(h, st)], dma=True)

        for ti in range(ntile):
            c0 = ti * 512
            if sample:
                P.op('sp', lambda e, c0=c0: e.dma_start(out=ropeC.ap, in_=d['ropeC'][:, c0:c0 + 512]), writes=ropeC.all(), dma=True)
                P.op('sp', lambda e, c0=c0: e.dma_start(out=ropeS.ap, in_=d['ropeS'][:, c0:c0 + 512]), writes=ropeS.all(), dma=True)
            self.modulate(g, c0, 512, A1, B1, hb, 0, ms)
            for h in range(4):
                s = self.wload(d['w_in0c'][4 + h], 1024)
                ps, tps = self.fm_proj(s, 128, lambda kc: hb.v(kc), lambda kc: hb.t(kc), 512)
                kb = kst[h % 2]
                rope_or_copy(ps, tps, kb.ap, kb.all())
                kcol = PAST + c0
                P.op('sp', lambda e, kb=kb, h=h, kcol=kcol: e.dma_start(out=KTd[h, :, kcol:kcol + 512], in_=kb.ap),
                     reads=kb.all(), writes=[tK(h, kcol // 128 + i) for i in range(4)], dma=True)
            sw = self.wload_wide(d['w_in0w'][1], 4096)
            for sub in range(4):
                ps, tps = self.bank()
                for kc in range(8):
                    P.op('pe', lambda e, ps=ps, kc=kc, sub=sub: e.matmul(ps[:, 0:512], lhsT=hb.v(kc, sub * 128, (sub + 1) * 128), rhs=sw.ap[:, kc * 512:(kc + 1) * 512],
                                                                         start=(kc == 0), stop=(kc == 7)), reads=hb.t(kc) + sw.all(), writes=[tps])
                vb = vst[sub % 2]
                P.op('act', lambda e, ps=ps, vb=vb: e.activation(out=vb.ap, in_=ps[:, 0:512], func=AF.Copy), reads=[tps], writes=vb.all())
                ch = (PAST + c0) // 128 + sub
                for h in range(4):
                    P.op('sp', lambda e, vb=vb, h=h, ch=ch: e.dma_start(out=Vd[h, :, ch, :], in_=vb.ap[:, h * 128:(h + 1) * 128]),
                         reads=vb.all(), writes=[tV(h, ch)], dma=True)
                if not sample:
                    ob = ost[sub % 2]
                    P.op('dve', lambda e, ps=ps, ob=ob: e.tensor_copy(out=ob.ap, in_=ps[:, 0:512]), reads=[tps], writes=ob.all())
                    r0 = c0 + sub * 128
                    P.op('sp', lambda e, ob=ob, r0=r0: e.dma_start(out=d['nv'][r0:r0 + 128, :], in_=ob.ap), reads=ob.all(), dma=True)
            if not sample:
                sw = self.wload_wide(d['w_in0w'][0], 4096)
                for sub in range(4):
                    ps, tps = self.bank()
                    for kc in range(8):
                        P.op('pe', lambda e, ps=ps, kc=kc, sub=sub: e.matmul(ps[:, 0:512], lhsT=hb.v(kc, sub * 128, (sub + 1) * 128), rhs=sw.ap[:, kc * 512:(kc + 1) * 512],
                                                                             start=(kc == 0), stop=(kc == 7)), reads=hb.t(kc) + sw.all(), writes=[tps])
                    ob = ost[sub % 2]
                    P.op('dve', lambda e, ps=ps, ob=ob: e.tensor_copy(out=ob.ap, in_=ps[:, 0:512]), reads=[tps], writes=ob.all())
                    r0 = c0 + sub * 128
                    P.op('sp', lambda e, ob=ob, r0=r0: e.dma_start(out=d['nk'][r0:r0 + 128, :], in_=ob.ap), reads=ob.all(), dma=True)

        self.reset_pages()
        hb = self.page_buf(8, 512, BF16)
        self.wslots = [self.page_buf(1, 4096, BF16)]
        ms = self.mod_scratch()
        ropeC = self.page_buf(1, 512, F32); ropeS = self.page_buf(1, 512, F32)
        rsets = [(self.page_buf(1, 512, BF16), self.page_buf(1, 512, F32), self.page_buf(1, 512, F32))]
        mix = self.page_buf(8, 512, BF16)
        QT = self.page_buf(4, 512, BF16)
        ub = self.page_buf(4, 512, BF16)
        vg = self.page_buf(1, 512, F32); junk = self.page_buf(1, 512, F32)
        vc = self.page_buf(1, 512, BF16)
        ss1 = self.page_buf(1, 16, F32)
        NE = 4
        E = [[self.page_buf(1, 512, BF16) for _ in range(NE)] for _ in range(2)]
        acc = [None, self.page_buf(1, 512, F32)]
        r0b = self.page_buf(1, 512, F32); r1b = self.page_buf(1, 512, F32)
        a0b = self.page_buf(1, 512, F32); a1b = self.page_buf(1, 512, F32)
        asq = self.page_buf(1, 512, BF16)
        maxk = max(ln for (_, ln) in g['seqs']) + PAST
        KTs = [self.page_buf(1, maxk, BF16) for _ in range(2)]
        Vs = [self.page_buf(maxk // 128, 128, BF16) for _ in range(2)]
        kv_rr = 0
        for ti in range(ntile):
            c0 = ti * 512
            if sample:
                P.op('sp', lambda e, c0=c0: e.dma_start(out=ropeC.ap, in_=d['ropeC'][:, c0:c0 + 512]), writes=ropeC.all(), dma=True)
                P.op('sp', lambda e, c0=c0: e.dma_start(out=ropeS.ap, in_=d['ropeS'][:, c0:c0 + 512]), writes=ropeS.all(), dma=True)
            self.modulate(g, c0, 512, A1, B1, hb, 0, ms)
            for h in range(4):
                s = self.wload(d['w_in0c'][h], 1024)
                ps, tps = self.fm_proj(s, 128, lambda kc: hb.v(kc), lambda kc: hb.t(kc), 512)
                rope_or_copy(ps, tps, QT.v(h), QT.t(h))
            for gi in range(4):
                s = self.wload(d['w_in0c'][12 + gi], 1024)
                ps, tps = self.fm_proj(s, 128, lambda kc: hb.v(kc), lambda kc: hb.t(kc), 512)
                P.op('act', lambda e, ps=ps, gi=gi: e.activation(out=ub.v(gi), in_=ps[:, 0:512], func=AF.Gelu_apprx_tanh), reads=[tps], writes=ub.t(gi))
            sw = self.wload_wide(d['w_in0w'][2], 4096)
            sbanks = [self.bank(hold=True) for _ in range(4)]
            vgs = [vg, r1b, a0b, a1b]
            P.op('dve', lambda e: e.memset(ss1.ap[:, 0:4], 0.0), writes=ss1.all())
            for sub in range(4):
                ps, tps = self.bank()
                for kc in range(8):
                    P.op('pe', lambda e, ps=ps, kc=kc, sub=sub: e.matmul(ps[:, 0:512], lhsT=hb.v(kc, sub * 128, (sub + 1) * 128), rhs=sw.ap[:, kc * 512:(kc + 1) * 512],
                                                                         start=(kc == 0), stop=(kc == 7)), reads=hb.t(kc) + sw.all(), writes=[tps])
                P.op('act', lambda e, ps=ps, sub=sub: e.activation(out=vgs[sub].ap, in_=ps[:, 0:512], func=AF.Gelu_apprx_tanh), reads=[tps], writes=vgs[sub].all())
                P.op('act', lambda e, sub=sub: e.activation(out=junk.ap, in_=vgs[sub].ap, func=AF.Square, accum_out=ss1.ap[:, sub:sub + 1]),
                     reads=vgs[sub].all() + ss1.all(), writes=junk.all() + ss1.all())
            P.op('act', lambda e: e.activation(out=ss1.ap[:, 4:8], in_=ss1.ap[:, 0:4], func=AF.Ln, scale=1.0 / 512, bias=EPS), reads=ss1.all(), writes=ss1.all())
            P.op('act', lambda e: e.activation(out=ss1.ap[:, 8:12], in_=ss1.ap[:, 4:8], func=AF.Exp, scale=-0.5), reads=ss1.all(), writes=ss1.all())
            for sub in range(4):
                P.op('dve', lambda e, sub=sub: e.scalar_tensor_tensor(out=vc.ap, in0=vgs[sub].ap, scalar=ss1.ap[:, 8 + sub:9 + sub], in1=self.vcol('gnw'), op0=ALU.mult, op1=ALU.mult),
                     reads=vgs[sub].all() + ss1.all() + [self.tv], writes=vc.all())
                for gi in range(4):
                    sbk, tsb = sbanks[gi]
                    P.op('pe', lambda e, sbk=sbk, gi=gi, sub=sub: e.matmul(sbk[:, sub * 128:(sub + 1) * 128], lhsT=vc.ap[:, gi * 128:(gi + 1) * 128],
                                                                           rhs=self.wsTb[:, gi * 128:(gi + 1) * 128], start=True, stop=False),
                         reads=vc.all() + [P.tok('wsTb')], writes=[tsb])
                    P.op('pe', lambda e, sbk=sbk, gi=gi, sub=sub: e.matmul(sbk[:, sub * 128:(sub + 1) * 128], lhsT=self.onesb[0:1, :],
                                                                           rhs=self.bsrow[0:1, gi * 128:(gi + 1) * 128], start=False, stop=True),
                         reads=[self.tcb, P.tok('bsrow')], writes=[tsb])
            for gi in range(4):
                sbk, tsb = sbanks[gi]
                P.op('dve', lambda e, sbk=sbk, gi=gi: e.tensor_tensor(out=mix.v(4 + gi), in0=sbk[:, 0:512], in1=ub.v(gi), op=ALU.mult),
                     reads=[tsb] + ub.t(gi), writes=mix.t(4 + gi))
            self.release(*sbanks)
            for (s0, sl) in g['seqs']:
                a = max(s0, c0); b = min(s0 + sl, c0 + 512)
                if a >= b:
                    continue
                qa, qb, nq = a - c0, b - c0, b - a
                nkc = (PAST + sl) // 128
                kbase = (0 if sample else s0)
                def head_loop(h, KTb, Vb, prev_done):
                    P.op('sp', lambda e, KTb=KTb, h=h, kbase=kbase, nkc=nkc: e.dma_start(out=KTb.ap[:, 0:nkc * 128], in_=KTd[h, :, kbase:kbase + nkc * 128]),
                         reads=[tK(h, kbase // 128 + i) for i in range(nkc)], writes=KTb.all(), dma=True)
                    P.op('sp', lambda e, Vb=Vb, h=h, kbase=kbase, nkc=nkc: e.dma_start(
                        out=Vb.ap[:, 0:nkc * 128].rearrange("p (c e) -> p c e", e=128), in_=Vd[h, :, kbase // 128:kbase // 128 + nkc, :]),
                         reads=[tV(h, kbase // 128 + i) for i in range(nkc)], writes=Vb.all(), dma=True)
                    O = [self.bank(hold=True), self.bank(hold=True)]
                    D0 = self.bank(hold=True)

                    def issue_st(kc):
                        for m in range(2):
                            st_, tst = self.bank()
                            P.op('pe', lambda e, st_=st_, m=m: e.matmul(
                                st_[:, 0:nq], lhsT=KTb.ap[m * 64:(m + 1) * 64, kc * 128:(kc + 1) * 128], rhs=QT.ap[m * 64:(m + 1) * 64, h * 512 + qa:h * 512 + qb],
                                start=True, stop=True), reads=KTb.all() + QT.t(h), writes=[tst])
                            Eb = E[m][kc % NE]
                            P.op('act', lambda e, st_=st_, Eb=Eb: e.activation(out=Eb.ap[:, 0:nq], in_=st_[:, 0:nq], func=AF.Exp, scale=0.125),
                                 reads=[tst], writes=Eb.all())

                    def issue_pv(kc):
                        for m in range(2):
                            Eb = E[m][kc % NE]
                            P.op('pe', lambda e, m=m, Eb=Eb: e.matmul(O[m][0][:, 0:nq], lhsT=Vb.v(kc), rhs=Eb.ap[:, 0:nq],
                                                                     start=(kc == 0), stop=(kc == nkc - 1)),
                                 reads=Vb.all() + Eb.all(), writes=[O[m][1]])
                            if m == 0:
                                P.op('pe', lambda e, Eb=Eb: e.matmul(D0[0][:, 0:nq], lhsT=self.onesb, rhs=Eb.ap[:, 0:nq], start=(kc == 0), stop=(kc == nkc - 1)),
                                     reads=Eb.all() + [self.tcb], writes=[D0[1]])
                            elif kc == 0:
                                P.op('dve', lambda e, m=m, Eb=Eb: e.tensor_copy(out=acc[m].ap[:, 0:nq], in_=Eb.ap[:, 0:nq]), reads=Eb.all(), writes=acc[m].all())
                            else:
                                P.op('dve', lambda e, m=m, Eb=Eb: e.tensor_tensor(out=acc[m].ap[:, 0:nq], in0=acc[m].ap[:, 0:nq], in1=Eb.ap[:, 0:nq], op=ALU.add),
                                     reads=Eb.all() + acc[m].all(), writes=acc[m].all())

                    LA = 2
                    for kc in range(min(LA, nkc)):
                        issue_st(kc)
                    for kc in range(nkc):
                        if kc + LA < nkc:
                            issue_st(kc + LA)
                        issue_pv(kc)
                        yield
                    while not prev_done[0]:
                        yield
                    P.op('dve', lambda e: e.tensor_copy(out=a0b.ap[:, 0:nq], in_=O[0][0][:, 0:nq]), reads=[O[0][1]], writes=a0b.all())
                    P.op('dve', lambda e: e.tensor_copy(out=a1b.ap[:, 0:nq], in_=O[1][0][:, 0:nq]), reads=[O[1][1]], writes=a1b.all())
                    P.op('act', lambda e: e.activation(out=r0b.ap[:, 0:nq], in_=D0[0][:, 0:nq], func=AF.Ln), reads=[D0[1]], writes=r0b.all())
                    self.release(*O)
                    self.release(D0)
                    D1 = self.bank(hold=True)
                    P.op('pe', lambda e: e.matmul(D1[0][:, 0:nq], lhsT=self.ccol('ones'), rhs=acc[1].ap[:, 0:nq], start=True, stop=True),
                         reads=acc[1].all() + [self.tc], writes=[D1[1]])
                    P.op('act', lambda e: e.activation(out=r1b.ap[:, 0:nq], in_=D1[0][:, 0:nq], func=AF.Ln), reads=[D1[1]], writes=r1b.all())
                    self.release(D1)
                def head_epi(h, done):
                    P.op('act', lambda e: e.activation(out=r0b.ap[:, 0:nq], in_=r0b.ap[:, 0:nq], func=AF.Exp, scale=-1.0), reads=r0b.all(), writes=r0b.all())
                    yield
                    P.op('dve', lambda e: e.tensor_tensor(out=a0b.ap[:, 0:nq], in0=a0b.ap[:, 0:nq], in1=r0b.ap[:, 0:nq], op=ALU.mult),
                         reads=a0b.all() + r0b.all(), writes=a0b.all())
                    yield
                    P.op('act', lambda e: e.activation(out=r1b.ap[:, 0:nq], in_=r1b.ap[:, 0:nq], func=AF.Exp, scale=-1.0), reads=r1b.all(), writes=r1b.all())
                    yield
                    P.op('dve', lambda e: e.tensor_tensor(out=a1b.ap[:, 0:nq], in0=a1b.ap[:, 0:nq], in1=r1b.ap[:, 0:nq], op=ALU.mult),
                         reads=a1b.all() + r1b.all(), writes=a1b.all())
                    yield
                    P.op('dve', lambda e: e.scalar_tensor_tensor(out=a0b.ap[:, 0:nq], in0=a1b.ap[:, 0:nq], scalar=self.small[:, 0:1], in1=a0b.ap[:, 0:nq],
                                                                 op0=ALU.mult, op1=ALU.add), reads=a0b.all() + a1b.all() + [self.tsmall], writes=a0b.all())
                    yield
                    P.op('act', lambda e: e.activation(out=asq.ap[:, 0:nq], in_=a0b.ap[:, 0:nq], func=AF.Square), reads=a0b.all(), writes=asq.all())
                    yield
                    ps, tps = self.bank()
                    P.op('pe', lambda e, ps=ps: e.matmul(ps[:, 0:nq], lhsT=self.onesb, rhs=asq.ap[:, 0:nq], start=True, stop=True),
                         reads=asq.all() + [self.tcb], writes=[tps])
                    P.op('act', lambda e, ps=ps: e.activation(out=r0b.ap[:, 0:nq], in_=ps[:, 0:nq], func=AF.Ln, scale=1.0 / 128, bias=EPS), reads=[tps], writes=r0b.all())
                    yield
                    P.op('act', lambda e: e.activation(out=r0b.ap[:, 0:nq], in_=r0b.ap[:, 0:nq], func=AF.Exp, scale=-0.5), reads=r0b.all(), writes=r0b.all())
                    yield
                    P.op('dve', lambda e, h=h: e.scalar_tensor_tensor(out=mix.v(h, qa, qb), in0=a0b.ap[:, 0:nq], scalar=self.small[:, 1:2], in1=r0b.ap[:, 0:nq],
                                                                      op0=ALU.mult, op1=ALU.mult), reads=a0b.all() + r0b.all() + [self.tsmall], writes=mix.t(h))
                    yield
                    done[0] = True

                def run_rr2(gens):
                    gens = list(gens)
                    while gens:
                        for gn in list(gens):
                            try:
                                next(gn)
                            except StopIteration:
                                gens.remove(gn)

                prev_done = [True]
                pend = None
                for h in range(4):
                    KTb = KTs[kv_rr % 2]; Vb = Vs[kv_rr % 2]; kv_rr += 1
                    thr = [head_loop(h, KTb, Vb, prev_done)]
                    if pend is not None:
                        thr.append(pend)
                    run_rr2(thr)
                    prev_done = [False]
                    pend = head_epi(h, prev_done)
                run_rr2([pend])
            for o in range(8):
                s = self.wload(d['w_out0c'][o], 1024)
                ps, tps = self.fm_proj(s, 128, lambda kc: mix.v(kc), lambda kc: mix.t(kc), 512)
                P.op('dve', lambda e, ps=ps, o=o, c0=c0: e.scalar_tensor_tensor(out=X.v(o, c0, c0 + 512), in0=ps[:, 0:512], scalar=G1(o), in1=X.v(o, c0, c0 + 512),
                                                                                op0=ALU.mult, op1=ALU.add),
                     reads=[tps, self.tAB] + X.t(o, c0, c0 + 512), writes=X.t(o, c0, c0 + 512))

    def ffn(self, g, l):
        P, d = self.P, self.d
        X = g['X']
        cond = g['cond']
        A2 = lambda kc: self.AB_(l, cond, 3, kc)
        B2 = lambda kc: self.AB_(l, cond, 4, kc)
        G2 = lambda kc: self.AB_(l, cond, 5, kc)
        self.reset_pages()
        H2 = [self.page_buf(8, 512, BF16) for _ in range(2)]
        ms = self.mod_scratch()
        act = self.page_buf(22, 512, BF16)
        c1 = self.page_buf(1, 512, F32); c2 = self.page_buf(1, 512, F32); gl = self.page_buf(1, 512, F32)
        tiles = []
        for (s0, sl) in g['seqs']:
            nt = (sl + 509) // 510
            W = ((sl + nt - 1) // nt + 1) // 2 * 2
            t0 = 0
            while t0 < sl:
                w = min(W, sl - t0)
                tiles.append((s0, sl, t0, w))
                t0 += w
        fw = lambda k, j: self.vcol('fcw%d' % l, k * 22 + j, k * 22 + j + 1)
        fb = lambda j: self.vcol('fcb%d' % l, j, j + 1)

        def make_h2(idx):
            s0, sl, t0, w = tiles[idx]
            hb = H2[idx % 2]
            lo = max(t0 - 1, 0); hi = min(t0 + w + 1, sl)
            off = lo - (t0 - 1)
            if off > 0:
                P.op('dve', lambda e: e.memset(hb.v3(0, 8, 0, 1), 0.0), writes=hb.t3(0, 8, 0, 1))
            if hi < t0 + w + 1:
                P.op('dve', lambda e: e.memset(hb.v3(0, 8, w + 1, w + 2), 0.0), writes=hb.t3(0, 8, w + 1, w + 2))
            self.modulate(g, s0 + lo, hi - lo, A2, B2, hb, off, ms)

        make_h2(0)
        for idx, (s0, sl, t0, w) in enumerate(tiles):
            hb = H2[idx % 2]
            if idx + 1 < len(tiles):
                make_h2(idx + 1)
            for j in range(22):
                sa = self.wload(d['w_upc'][l * 44 + j], 1024)
                sg = self.wload(d['w_upc'][l * 44 + 22 + j], 1024)
                pa, tpa = self.fm_proj(sa, 128, lambda kc: hb.v(kc, 0, w + 2), lambda kc: hb.t(kc), w + 2)
                pg, tpg = self.fm_proj(sg, 128, lambda kc: hb.v(kc, 1, w + 1), lambda kc: hb.t(kc), w)
                P.op('dve', lambda e, pa=pa, j=j: e.tensor_scalar(out=c1.ap[:, 0:w], in0=pa[:, 0:w], scalar1=fw(0, j), scalar2=None, op0=ALU.mult),
                     reads=[tpa, self.tv], writes=c1.all())
                P.op('dve', lambda e, pa=pa, j=j: e.scalar_tensor_tensor(out=c2.ap[:, 0:w], in0=pa[:, 1:w + 1], scalar=fw(1, j), in1=c1.ap[:, 0:w], op0=ALU.mult, op1=ALU.add),
                     reads=[tpa, self.tv] + c1.all(), writes=c2.all())
                P.op('dve', lambda e, pa=pa, j=j: e.scalar_tensor_tensor(out=c1.ap[:, 0:w], in0=pa[:, 2:w + 2], scalar=fw(2, j), in1=c2.ap[:, 0:w], op0=ALU.mult, op1=ALU.add),
                     reads=[tpa, self.tv] + c2.all(), writes=c1.all())
                P.op('act', lambda e, j=j: e.activation(out=gl.ap[:, 0:w], in_=c1.ap[:, 0:w], func=AF.Gelu_apprx_tanh, bias=fb(j)),
                     reads=c1.all() + [self.tv], writes=gl.all())
                P.op('dve', lambda e, pg=pg, j=j: e.tensor_tensor(out=act.v(j, 0, w), in0=pg[:, 0:w], in1=gl.ap[:, 0:w], op=ALU.mult),
                     reads=[tpg] + gl.all(), writes=act.t(j))
            for o in range(8):
                pieces = []
                for (k0, k1) in ((0, 8), (8, 16), (16, 22)):
                    pieces.append((k0, k1, self.wload(d['w_downc'][l * 8 + o][:, k0 * 128:k1 * 128], (k1 - k0) * 128)))
                ps, tps = self.bank()
                for (k0, k1, s) in pieces:
                    for kk in range(k0, k1):
                        P.op('pe', lambda e, ps=ps, s=s, kk=kk, k0=k0: e.matmul(ps[:, 0:w], lhsT=s.ap[:, (kk - k0) * 128:(kk - k0 + 1) * 128], rhs=act.v(kk, 0, w),
                                                                               start=(kk == 0), stop=(kk == 21)), reads=s.all() + act.t(kk), writes=[tps])
                ca, cb_ = s0 + t0, s0 + t0 + w
                P.op('dve', lambda e, ps=ps, o=o, ca=ca, cb_=cb_: e.scalar_tensor_tensor(out=X.v(o, ca, cb_), in0=ps[:, 0:w], scalar=G2(o), in1=X.v(o, ca, cb_),
                                                                                       op0=ALU.mult, op1=ALU.add),
                     reads=[tps, self.tAB] + X.t(o, ca, cb_), writes=X.t(o, ca, cb_))

    def layer1(self, g):
        P, d = self.P, self.d
        X = g['X']
        sample = g['sample']
        cond = g['cond']
        A1 = lambda kc: self.AB_(1, cond, 0, kc)
        B1 = lambda kc: self.AB_(1, cond, 1, kc)
        G1 = lambda kc: self.AB_(1, cond, 2, kc)
        self.reset_pages()
        NW = 258
        H1 = [self.page_buf(8, NW, BF16) for _ in range(2)]
        ms = self.mod_scratch(NW)
        xm = self.page_buf(8, NW, BF16)
        xc = self.page_buf(8, 256, BF16)
        skx = self.page_buf(8, 256, BF16)
        sgm = self.page_buf(8, 256, BF16)
        qT = self.page_buf(8, 256, BF16)
        kT = self.page_buf(8, 256, BF16)
        ktok = self.page_buf(8, 256, BF16)
        vaug = self.page_buf(8, 257, BF16)
        Gt = self.page_buf(2, 16, F32)
        C = self.page_buf(8, 257, F32)
        Cb = self.page_buf(8, 257, BF16)
        gms = [self.page_buf(1, 96, F32) for _ in range(2)]
        hsm = [self.page_buf(1, 8, F32) for _ in range(4)]
        mst = [self.page_buf(1, 4, F32) for _ in range(2)]
        dg = self.page_buf(4, 128, F32)
        DTs = [self.page_buf(4, 128, F32) for _ in range(2)]
        sTws = [self.page_buf(1, 128, BF16) for _ in range(4)]
        nds = [self.page_buf(1, 257, F32) for _ in range(4)]
        kws = [self.page_buf(1, 256, BF16) for _ in range(4)]
        hst = [self.page_buf(1, 1024, BF16) for _ in range(2)]
        hnb = self.page_buf(1, 1024, BF16)
        junks = [self.page_buf(1, 256, F32) for _ in range(2)]
        cc1, cc2 = junks
        tt1 = self.page_buf(1, 1024, BF16)
        ybuf = self.page_buf(8, 256, BF16)
        Hfd = d['Hfd']
        tH = lambda ch: P.tok('Hfd', ch)
        mcw = lambda k, c: self.vcol('mcw', k * 8 + c, k * 8 + c + 1)
        tiles = []
        for si, (s0, sl) in enumerate(g['seqs']):
            t0 = 0
            while t0 < sl:
                n = min(256, sl - t0)
                tiles.append((si, s0, sl, t0, n))
                t0 += n

        def make_h1(idx):
            si, s0, sl, t0, n = tiles[idx]
            hb = H1[idx % 2]
            lo = max(t0 - 1, 0); hi = min(t0 + n + 1, sl)
            off = lo - (t0 - 1)
            if off > 0:
                P.op('dve', lambda e: e.memset(hb.v3(0, 8, 0, 1), 0.0), writes=hb.t3(0, 8, 0, 1))
            if hi < t0 + n + 1:
                P.op('dve', lambda e: e.memset(hb.v3(0, 8, n + 1, n + 2), 0.0), writes=hb.t3(0, 8, n + 1, n + 2))
            self.modulate(g, s0 + lo, hi - lo, A1, B1, hb, off, ms)

        P.op('dve', lambda e: e.memset(vaug.ap, 1.0), writes=vaug.all())

        for dr in range(2):
            order = list(range(len(tiles))) if dr == 0 else list(range(len(tiles) - 1, -1, -1))
            U = self.ccol('Uf' if dr == 0 else 'Ub')
            nm = self.cb[:, (3 if dr == 0 else 4) * 128:(4 if dr == 0 else 5) * 128]
            pm = self.cb[:, (5 if dr == 0 else 6) * 128:(6 if dr == 0 else 7) * 128]
            sel = self.ccol('self' if dr == 0 else 'selb')
            mcur = 0
            make_h1(order[0])
            for oi, idx in enumerate(order):
                si, s0, sl, t0, n = tiles[idx]
                hb = H1[idx % 2]
                nsub = n // 128
                first_of_seq = (t0 == 0) if dr == 0 else (t0 + n == sl)
                last_of_seq = (t0 + n == sl) if dr == 0 else (t0 == 0)
                if oi + 1 < len(order):
                    make_h1(order[oi + 1])
                if first_of_seq:
                    if sample:
                        for h in range(4):
                            for dk in range(2):
                                P.op('sp', lambda e, h=h, dk=dk: e.dma_start(out=C.v(h * 2 + dk, 0, 256), in_=d['sC'][dr * 4 + h, dk, :, :]),
                                     writes=C.t(h * 2 + dk), dma=True)
                        for h in range(4):
                            for dk in range(2):
                                P.op('sp', lambda e, h=h, dk=dk: e.dma_start(out=C.v(h * 2 + dk, 256, 257), in_=d['sn'][:, (dr * 4 + h) * 2 + dk:(dr * 4 + h) * 2 + dk + 1], allow_slow_non_contiguous=True),
                                     writes=C.t(h * 2 + dk), dma=True)
                        P.op('sp', lambda e, mcur=mcur: e.dma_start(out=mst[mcur].ap, in_=d['sm'][:, dr * 4:dr * 4 + 4], allow_slow_non_contiguous=True), writes=mst[mcur].all(), dma=True)
                    else:
                        P.op('dve', lambda e: e.memset(C.ap, 0.0), writes=C.all())
                        P.op('dve', lambda e, mcur=mcur: e.memset(mst[mcur].ap, 0.0), writes=mst[mcur].all())
                    P.op('act', lambda e: e.activation(out=Cb.ap, in_=C.ap, func=AF.Copy), reads=C.all(), writes=Cb.all())
                subs = list(range(nsub)) if dr == 0 else list(range(nsub - 1, -1, -1))

                def gate_thread(sub, gmb, mc, mn):
                    gma = gmb.all()
                    G = lambda a, b: gmb.ap[:, a:b]
                    li = Gt.v(sub, dr * 8, dr * 8 + 4)
                    fg = Gt.v(sub, dr * 8 + 4, dr * 8 + 8)
                    DTb = DTs[sub % 2]
                    P.op('act', lambda e: e.activation(out=G(0, 4), in_=fg, func=AF.Exp, scale=-1.0), reads=Gt.t(sub), writes=gma)
                    P.op('act', lambda e: e.activation(out=G(0, 4), in_=G(0, 4), func=AF.Ln, bias=1.0), reads=gma, writes=gma)
                    yield
                    pcp, tpcp = (yield from self.bank_wait())
                    P.op('pe', lambda e: e.matmul(pcp[:, 0:4], lhsT=U, rhs=G(0, 4), start=True, stop=True), reads=gma + [self.tc], writes=[tpcp])
                    yield
                    P.op('dve', lambda e: e.tensor_copy(out=G(72, 76), in_=pcp[:, 0:4]), reads=[tpcp] + gma, writes=gma)
                    self.release((pcp, tpcp))
                    P.op('dve', lambda e: e.tensor_tensor(out=G(4, 8), in0=G(72, 76), in1=li, op=ALU.add), reads=Gt.t(sub) + gma, writes=gma)
                    P.op('dve', lambda e: e.tensor_tensor(out=dg.v3(0, 4), in0=self.ccol('ident').unsqueeze(1).to_broadcast([128, 4, 128]),
                                                          in1=G(4, 8).unsqueeze(2).to_broadcast([128, 4, 128]), op=ALU.mult), reads=gma + [self.tc], writes=dg.all())
                    yield
                    pcb, tpcb = (yield from self.bank_wait())
                    for h in range(4):
                        P.op('pe', lambda e, h=h: e.matmul(pcb[:, h * 128:(h + 1) * 128], lhsT=self.ccol('ones'), rhs=dg.v(h), start=True, stop=False),
                             reads=dg.all() + [self.tc], writes=[tpcb])
                        P.op('pe', lambda e, h=h: e.matmul(pcb[:, h * 128:(h + 1) * 128], lhsT=self.identb, rhs=nm, start=False, stop=True),
                             reads=[self.tcb, P.tok('cbm')], writes=[tpcb])
                    yield
                    P.op('dve', lambda e: e.tensor_reduce(out=G(8, 12), in_=pcb[:, 0:512].rearrange("p (h s) -> p h s", s=128), axis=AX.X, op=ALU.max),
                         reads=[tpcb] + gma, writes=gma)
                    self.release((pcb, tpcb))
                    P.op('dve', lambda e: e.tensor_tensor(out=G(12, 16), in0=G(8, 12), in1=mc.ap, op=ALU.max), reads=gma + mc.all(), writes=gma)
                    P.op('dve', lambda e: e.tensor_tensor(out=G(16, 20), in0=G(12, 16), in1=G(72, 76), op=ALU.subtract), reads=gma, writes=gma)
                    yield
                    psel, tpsel = (yield from self.bank_wait())
                    P.op('pe', lambda e: e.matmul(psel[:, 0:8], lhsT=sel, rhs=G(12, 20), start=True, stop=True), reads=gma + [self.tc], writes=[tpsel])
                    yield
                    P.op('act', lambda e: e.activation(out=mn.ap, in_=psel[:, 4:8], func=AF.Copy), reads=[tpsel], writes=mn.all())
                    P.op('act', lambda e: e.activation(out=G(52, 56), in_=psel[:, 0:4], func=AF.Copy), reads=[tpsel] + gma, writes=gma)
                    self.release((psel, tpsel))
                    yield
                    P.op('dve', lambda e: e.tensor_tensor(out=G(20, 24), in0=mc.ap, in1=G(12, 16), op=ALU.subtract), reads=gma + mc.all(), writes=gma)
                    P.op('dve', lambda e: e.tensor_scalar(out=G(24, 28), in0=G(16, 20), scalar1=-1.0, scalar2=None, op0=ALU.mult), reads=gma, writes=gma)
                    P.op('dve', lambda e: e.tensor_tensor(out=G(28, 32), in0=mc.ap, in1=G(52, 56), op=ALU.subtract), reads=gma + mc.all(), writes=gma)
                    P.op('dve', lambda e: e.tensor_tensor(out=G(32, 36), in0=G(4, 8), in1=G(52, 56), op=ALU.subtract), reads=gma, writes=gma)
                    yield
                    P.op('act', lambda e: e.activation(out=G(36, 52), in_=G(20, 36), func=AF.Exp), reads=gma, writes=gma)
                    P.op('dve', lambda e: e.tensor_tensor(out=dg.v3(0, 4), in0=self.ccol('ident').unsqueeze(1).to_broadcast([128, 4, 128]),
                                                          in1=G(12, 16).unsqueeze(2).to_broadcast([128, 4, 128]), op=ALU.mult), reads=gma + [self.tc], writes=dg.all())
                    yield
                    pmb_, tpmb = (yield from self.bank_wait())
                    for h in range(4):
                        P.op('pe', lambda e, h=h: e.matmul(pmb_[:, h * 128:(h + 1) * 128], lhsT=self.ccol('ones'), rhs=dg.v(h), start=True, stop=False),
                             reads=dg.all() + [self.tc], writes=[tpmb])
                        P.op('pe', lambda e, h=h: e.matmul(pmb_[:, h * 128:(h + 1) * 128], lhsT=self.identb, rhs=pm, start=False, stop=True),
                             reads=[self.tcb, P.tok('cbm')], writes=[tpmb])
                    yield
                    for h in range(4):
                        P.op('act', lambda e, h=h: e.activation(out=DTb.v(h), in_=pmb_[:, h * 128:(h + 1) * 128], func=AF.Exp, scale=-1.0, bias=G(4 + h, 5 + h)),
                             reads=[tpmb] + gma, writes=DTb.t(h))
                        if h == 1:
                            yield
                    self.release((pmb_, tpmb))

                def head_thread(sub, h, gmb, hfb):
                    gma = gmb.all()
                    G = lambda a, b: gmb.ap[:, a:b]
                    WI = G(36 + h, 37 + h); EM = G(40 + h, 41 + h); DEC = G(44 + h, 45 + h); WS = G(48 + h, 49 + h)
                    DTb = DTs[sub % 2]
                    hm = hsm[h]
                    H_ = lambda a, b: hm.ap[:, a:b]
                    ndh = nds[h]
                    cs0, cs1 = sub * 128, (sub + 1) * 128
                    pst, tpst = (yield from self.bank_wait())
                    for dk in range(2):
                        P.op('pe', lambda e, dk=dk: e.matmul(pst[:, 0:128], lhsT=kT.v(h * 2 + dk, cs0, cs1), rhs=qT.v(h * 2 + dk, cs0, cs1),
                                                             start=(dk == 0), stop=(dk == 1)), reads=kT.t(h * 2 + dk) + qT.t(h * 2 + dk), writes=[tpst])
                    pB, tpB = (yield from self.bank_wait())
                    for dk in range(2):
                        P.op('pe', lambda e, dk=dk: e.matmul(pB[:, 0:257], lhsT=qT.v(h * 2 + dk, cs0, cs1), rhs=Cb.v(h * 2 + dk),
                                                             start=(dk == 0), stop=(dk == 1)), reads=qT.t(h * 2 + dk) + Cb.t(h * 2 + dk), writes=[tpB])
                    yield
                    P.op('dve', lambda e: e.tensor_tensor(out=sTws[h].ap, in0=pst[:, 0:128], in1=DTb.v(h), op=ALU.mult), reads=[tpst] + DTb.t(h), writes=sTws[h].all())
                    self.release((pst, tpst))
                    P.op('act', lambda e: e.activation(out=ndh.ap, in_=pB[:, 0:257], func=AF.Identity, scale=WI), reads=[tpB] + gma, writes=ndh.all())
                    self.release((pB, tpB))
                    P.op('dve', lambda e: e.tensor_scalar(out=kws[h].ap, in0=ktok.v(sub * 4 + h), scalar1=WS, scalar2=None, op0=ALU.mult),
                         reads=ktok.t(sub * 4 + h) + gma, writes=kws[h].all())
                    yield
                    pA, tpA = (yield from self.bank_wait())
                    P.op('pe', lambda e: e.matmul(pA[:, 0:257], lhsT=sTws[h].ap, rhs=vaug.v(sub * 4 + h), start=True, stop=True),
                         reads=sTws[h].all() + vaug.t(sub * 4 + h), writes=[tpA])
                    pCs = []
                    for dk in range(2):
                        pC, tpC = (yield from self.bank_wait())
                        pCs.append((pC, tpC))
                        P.op('pe', lambda e, pC=pC, dk=dk: e.matmul(pC[:, 0:257], lhsT=kws[h].ap[:, dk * 128:(dk + 1) * 128], rhs=vaug.v(sub * 4 + h), start=True, stop=True),
                             reads=kws[h].all() + vaug.t(sub * 4 + h), writes=[tpC])
                    yield
                    P.op('dve', lambda e: e.tensor_tensor(out=ndh.ap, in0=pA[:, 0:257], in1=ndh.ap, op=ALU.add), reads=[tpA] + ndh.all(), writes=ndh.all())
                    self.release((pA, tpA))
                    for dk in range(2):
                        pC, tpC = pCs[dk]
                        P.op('dve', lambda e, pC=pC, dk=dk: e.scalar_tensor_tensor(out=C.v(h * 2 + dk), in0=C.v(h * 2 + dk), scalar=DEC, in1=pC[:, 0:257], op0=ALU.mult, op1=ALU.add),
                             reads=[tpC] + gma + C.t(h * 2 + dk), writes=C.t(h * 2 + dk))
                    self.release(*pCs)
                    yield
                    P.op('act', lambda e: e.activation(out=H_(0, 1), in_=ndh.ap[:, 256:257], func=AF.Abs), reads=ndh.all() + hm.all(), writes=hm.all())
                    P.op('act', lambda e: e.activation(out=Cb.v3(h * 2, h * 2 + 2), in_=C.v3(h * 2, h * 2 + 2), func=AF.Copy), reads=C.t3(h * 2, h * 2 + 2), writes=Cb.t3(h * 2, h * 2 + 2))
                    yield
                    P.op('dve', lambda e: e.tensor_tensor(out=H_(1, 2), in0=H_(0, 1), in1=EM, op=ALU.max), reads=gma + hm.all(), writes=hm.all())
                    P.op('dve', lambda e: e.reciprocal(out=H_(2, 3), in_=H_(1, 2)), reads=hm.all(), writes=hm.all())
                    yield
                    if dr == 0:
                        P.op('dve', lambda e: e.tensor_scalar(out=hfb.ap[:, h * 256:(h + 1) * 256], in0=ndh.ap[:, 0:256], scalar1=H_(2, 3), scalar2=None, op0=ALU.mult),
                             reads=ndh.all() + hm.all(), writes=hfb.all())
                    else:
                        P.op('dve', lambda e: e.scalar_tensor_tensor(out=ndh.ap[:, 0:256], in0=ndh.ap[:, 0:256], scalar=H_(2, 3), in1=hfb.ap[:, h * 256:(h + 1) * 256],
                                                                     op0=ALU.mult, op1=ALU.add), reads=ndh.all() + hm.all() + hfb.all(), writes=ndh.all())
                        P.op('dve', lambda e: e.memset(H_(3, 4), 0.0), reads=hm.all(), writes=hm.all())
                        yield
                        P.op('act', lambda e: e.activation(out=junks[h % 2].ap, in_=ndh.ap[:, 0:256], func=AF.Square, accum_out=H_(3, 4)),
                             reads=ndh.all() + hm.all(), writes=junks[h % 2].all() + hm.all())
                        P.op('act', lambda e: e.activation(out=H_(4, 5), in_=H_(3, 4), func=AF.Ln, scale=1.0 / 256, bias=EPS), reads=hm.all(), writes=hm.all())
                        P.op('act', lambda e: e.activation(out=H_(5, 6), in_=H_(4, 5), func=AF.Exp, scale=-0.5), reads=hm.all(), writes=hm.all())
                        yield
                        P.op('dve', lambda e: e.tensor_scalar(out=hnb.ap[:, h * 256:(h + 1) * 256], in0=ndh.ap[:, 0:256], scalar1=H_(5, 6), scalar2=None, op0=ALU.mult),
                             reads=ndh.all() + hm.all(), writes=hnb.all())

                def run_rr(gens):
                    gens = list(gens)
                    while gens:
                        for gn in list(gens):
                            try:
                                next(gn)
                            except StopIteration:
                                gens.remove(gn)

                for sub in range(nsub):
                    ps, tps = self.bank()
                    for kc in range(8):
                        P.op('pe', lambda e, ps=ps, kc=kc, sub=sub: e.matmul(ps[:, 0:16], lhsT=hb.v(kc, 1 + sub * 128, 1 + (sub + 1) * 128), rhs=self.wgb[:, kc * 16:(kc + 1) * 16],
                                                                             start=(kc == 0), stop=(kc == 7)), reads=hb.t(kc) + [P.tok('wgb')], writes=[tps])
                    P.op('dve', lambda e, ps=ps, sub=sub: e.tensor_tensor(out=Gt.v(sub), in0=ps[:, 0:16], in1=self.vcol('bg'), op=ALU.add),
                         reads=[tps, self.tv], writes=Gt.t(sub))
                def proj_thread():
                    for c in range(8):
                        s = self.wload(d['w_in1c'][c], 1024)
                        ps, tps = self.fm_proj(s, 128, lambda kc: hb.v(kc, 0, n + 2), lambda kc: hb.t(kc), n + 2)
                        P.op('act', lambda e, ps=ps, c=c: e.activation(out=xm.v(c, 0, n + 2), in_=ps[:, 0:n + 2], func=AF.Copy), reads=[tps], writes=xm.t(c))
                        P.op('dve', lambda e, ps=ps, c=c: e.tensor_scalar(out=cc1.ap[:, 0:n], in0=ps[:, 0:n], scalar1=mcw(0, c), scalar2=None, op0=ALU.mult),
                             reads=[tps, self.tv], writes=cc1.all())
                        P.op('dve', lambda e, ps=ps, c=c: e.scalar_tensor_tensor(out=cc2.ap[:, 0:n], in0=ps[:, 1:n + 1], scalar=mcw(1, c), in1=cc1.ap[:, 0:n], op0=ALU.mult, op1=ALU.add),
                             reads=[tps, self.tv] + cc1.all(), writes=cc2.all())
                        P.op('dve', lambda e, ps=ps, c=c: e.scalar_tensor_tensor(out=cc1.ap[:, 0:n], in0=ps[:, 2:n + 2], scalar=mcw(2, c), in1=cc2.ap[:, 0:n], op0=ALU.mult, op1=ALU.add),
                             reads=[tps, self.tv] + cc2.all(), writes=cc1.all())
                        P.op('act', lambda e, c=c: e.activation(out=xc.v(c, 0, n), in_=cc1.ap[:, 0:n], func=AF.Silu, bias=self.vcol('mcb', c, c + 1)),
                             reads=cc1.all() + [self.tv], writes=xc.t(c))
                        yield
                        if dr == 1:
                            P.op('dve', lambda e, c=c: e.tensor_scalar(out=skx.v(c, 0, n), in0=xc.v(c, 0, n), scalar1=self.vcol('skw', c, c + 1), scalar2=None, op0=ALU.mult),
                                 reads=xc.t(c) + [self.tv], writes=skx.t(c))
                    if dr == 1:
                        for c in range(8):
                            s = self.wload(d['w_in1c'][8 + c], 1024)
                            ps, tps = self.fm_proj(s, 128, lambda kc: hb.v(kc, 1, n + 1), lambda kc: hb.t(kc), n)
                            P.op('act', lambda e, ps=ps, c=c: e.activation(out=sgm.v(c, 0, n), in_=ps[:, 0:n], func=AF.Sigmoid), reads=[tps], writes=sgm.t(c))
                            yield
                    phase2[0] = True
                    for h in range(4):
                        for dc in range(2):
                            s = self.wload(d['wqc'][h * 2 + dc], 256)
                            ps, tps = self.fm_proj(s, 128, lambda kc: xc.v(h * 2 + kc, 0, n), lambda kc: xc.t(h * 2 + kc), n, nk=2)
                            P.op('act', lambda e, ps=ps, h=h, dc=dc: e.activation(out=qT.v(h * 2 + dc, 0, n), in_=ps[:, 0:n], func=AF.Copy), reads=[tps], writes=qT.t(h * 2 + dc))
                            s = self.wload(d['wkc'][h * 2 + dc], 256)
                            ps, tps = self.fm_proj(s, 128, lambda kc: xc.v(h * 2 + kc, 0, n), lambda kc: xc.t(h * 2 + kc), n, nk=2)
                            P.op('act', lambda e, ps=ps, h=h, dc=dc: e.activation(out=kT.v(h * 2 + dc, 0, n), in_=ps[:, 0:n], func=AF.Identity, scale=1.0 / 16), reads=[tps], writes=kT.t(h * 2 + dc))
                            yield
                        sk = self.wload(d['wkw'][h], 512)
                        sv = self.wload(d['wvw'][h], 512)
                        for sub in range(nsub):
                            ps, tps = self.bank()
                            for kk in range(2):
                                P.op('pe', lambda e, ps=ps, kk=kk, sub=sub, h=h, sk=sk: e.matmul(ps[:, 0:256], lhsT=xc.v(h * 2 + kk, sub * 128, (sub + 1) * 128), rhs=sk.ap[:, kk * 256:(kk + 1) * 256],
                                                                                                 start=(kk == 0), stop=(kk == 1)), reads=xc.t(h * 2 + kk) + sk.all(), writes=[tps])
                            P.op('act', lambda e, ps=ps, sub=sub, h=h: e.activation(out=ktok.v(sub * 4 + h), in_=ps[:, 0:256], func=AF.Identity, scale=1.0 / 16), reads=[tps], writes=ktok.t(sub * 4 + h))
                            ps, tps = self.bank()
                            for kk in range(2):
                                P.op('pe', lambda e, ps=ps, kk=kk, sub=sub, h=h, sv=sv: e.matmul(ps[:, 0:256], lhsT=xm.v(h * 2 + kk, 1 + sub * 128, 1 + (sub + 1) * 128), rhs=sv.ap[:, kk * 256:(kk + 1) * 256],
                                                                                                 start=(kk == 0), stop=(kk == 1)), reads=xm.t(h * 2 + kk) + sv.all(), writes=[tps])
                            P.op('act', lambda e, ps=ps, sub=sub, h=h: e.activation(out=vaug.v(sub * 4 + h, 0, 256), in_=ps[:, 0:256], func=AF.Copy), reads=[tps], writes=vaug.t(sub * 4 + h))
                            yield

                first_sub = subs[0]
                phase2 = [False]

                def gated_gate_thread():
                    while not phase2[0]:
                        yield
                    yield from gate_thread(first_sub, gms[first_sub % 2], mst[mcur], mst[1 - mcur])

                run_rr([proj_thread(), gated_gate_thread()])
                for si_, sub in enumerate(subs):
                    ch = (s0 + t0) // 128 + sub
                    mc = mst[mcur]; mn = mst[1 - mcur]
                    gmb = gms[sub % 2]
                    hfb = hst[sub % 2]
                    if dr == 1:
                        P.op('sp', lambda e, hfb=hfb, ch=ch: e.dma_start(out=hfb.ap, in_=Hfd[ch, :, :]), reads=[tH(ch)], writes=hfb.all(), dma=True)
                    threads = [head_thread(sub, h, gmb, hfb) for h in range(4)]
                    mcur = 1 - mcur
                    if si_ + 1 < len(subs):
                        nsb = subs[si_ + 1]
                        threads.append(gate_thread(nsb, gms[nsb % 2], mst[mcur], mst[1 - mcur]))
                    run_rr(threads)
                    cs0, cs1 = sub * 128, (sub + 1) * 128
                    if dr == 0:
                        P.op('sp', lambda e, hfb=hfb, ch=ch: e.dma_start(out=Hfd[ch, :, :], in_=hfb.ap), reads=hfb.all(), writes=[tH(ch)], dma=True)
                    else:
                        pT, tpT = self.bank()
                        pTb = pT[:].bitcast(BF16)
                        for kc in range(8):
                            P.op('pe', lambda e, pTb=pTb, kc=kc: e.transpose(pTb[:, kc * 128:(kc + 1) * 128], hnb.ap[:, kc * 128:(kc + 1) * 128], self.identb),
                                 reads=hnb.all() + [self.tcb], writes=[tpT])
                        P.op('dve', lambda e, pTb=pTb: e.tensor_tensor(out=tt1.ap.rearrange("p (k t) -> p k t", t=128), in0=pTb[:, 0:1024].rearrange("p (k t) -> p k t", t=128),
                                                                       in1=self.vcol('hnw').unsqueeze(2).to_broadcast([128, 8, 128]), op=ALU.mult),
                             reads=[tpT, self.tv], writes=tt1.all())
                        P.op('dve', lambda e: e.tensor_tensor(out=tt1.ap.rearrange("p (k t) -> p k t", t=128), in0=tt1.ap.rearrange("p (k t) -> p k t", t=128),
                                                              in1=skx.v3(0, 8, cs0, cs1), op=ALU.add), reads=tt1.all() + skx.all(), writes=tt1.all())
                        P.op('dve', lambda e: e.tensor_tensor(out=ybuf.v3(0, 8, cs0, cs1), in0=tt1.ap.rearrange("p (k t) -> p k t", t=128),
                                                              in1=sgm.v3(0, 8, cs0, cs1), op=ALU.mult), reads=tt1.all() + sgm.all(), writes=ybuf.all())
                if last_of_seq and not sample:
                    for h in range(4):
                        for dk in range(2):
                            P.op('sp', lambda e, h=h, dk=dk, si=si: e.dma_start(out=d['nC'][si, dr * 4 + h, dk, :, :], in_=C.v(h * 2 + dk, 0, 256)),
                                 reads=C.t(h * 2 + dk), dma=True)
                            col = (dr * 4 + h) * 2 + dk
                            P.op('sp', lambda e, h=h, dk=dk, si=si, col=col: e.dma_start(out=d['nn'][si, :, col:col + 1], in_=C.v(h * 2 + dk, 256, 257), allow_slow_non_contiguous=True),
                                 reads=C.t(h * 2 + dk), dma=True)
                    P.op('sp', lambda e, si=si, mcur=mcur: e.dma_start(out=d['nm'][si, :, dr * 4:dr * 4 + 4], in_=mst[mcur].ap[0:1, :]), reads=mst[mcur].all(), dma=True)
                if dr == 1:
                    for o in range(8):
                        s = self.wload(d['w_out1c'][o], 1024)
                        ps, tps = self.fm_proj(s, 128, lambda kc: ybuf.v(kc, 0, n), lambda kc: ybuf.t(kc), n)
                        ca, cb_ = s0 + t0, s0 + t0 + n
                        P.op('dve', lambda e, ps=ps, o=o, ca=ca, cb_=cb_: e.scalar_tensor_tensor(out=X.v(o, ca, cb_), in0=ps[:, 0:n], scalar=G1(o), in1=X.v(o, ca, cb_),
                                                                                               op0=ALU.mult, op1=ALU.add),
                             reads=[tps, self.tAB] + X.t(o, ca, cb_), writes=X.t(o, ca, cb_))

    def final(self, g):
        P, d = self.P, self.d
        X = g['X']
        dst = d['ys'] if g['sample'] else d['yp']
        self.reset_pages()
        ms = self.mod_scratch()
        yf = self.page_buf(8, 512, F32)
        ost = [self.page_buf(1, 1024, F32) for _ in range(2)]
        A = lambda kc: self.vcol('fnw', kc, kc + 1)
        for ti in range(g['T'] // 512):
            c0 = ti * 512
            self.modulate(g, c0, 512, A, None, yf, 0, ms)
            for sub in range(4):
                ob = ost[sub % 2]
                for half in range(2):
                    ps, tps = self.bank()
                    for k4 in range(4):
                        kc = half * 4 + k4
                        P.op('pe', lambda e, ps=ps, kc=kc, k4=k4, sub=sub: e.transpose(ps[:, k4 * 128:(k4 + 1) * 128], yf.v(kc, sub * 128, (sub + 1) * 128), self.ccol('ident')),
                             reads=yf.t(kc) + [self.tc], writes=[tps])
                    if half == 0:
                        P.op('act', lambda e, ps=ps, ob=ob: e.activation(out=ob.ap[:, 0:512], in_=ps[:, 0:512], func=AF.Copy), reads=[tps], writes=ob.all())
                    else:
                        P.op('dve', lambda e, ps=ps, ob=ob: e.tensor_copy(out=ob.ap[:, 512:1024], in_=ps[:, 0:512]), reads=[tps], writes=ob.all())
                r0 = c0 + sub * 128
                P.op('sp', lambda e, ob=ob, r0=r0: e.dma_start(out=dst[r0:r0 + 128, :], in_=ob.ap), reads=ob.all(), dma=True)


def prep_shared(inp, TS):
    f = lambda a: np.ascontiguousarray(np.asarray(a, dtype=np.float32))
    sh = {}
    w_in0 = f(inp['w_in0'])[0]
    sh['w_in0c'] = chunkmajor(w_in0)
    sh['w_in0w'] = np.ascontiguousarray(np.stack([chunkmajor(w_in0[:, 512:1024], 512)[0], chunkmajor(w_in0[:, 1024:1536], 512)[0],
                                                  chunkmajor(w_in0[:, 2048:2560], 512)[0]]))
    sh['w_out0c'] = chunkmajor(f(inp['w_out0'])[0])
    w_in1 = f(inp['w_in1'])[0]
    sh['w_in1c'] = chunkmajor(w_in1[:, :2048])
    sh['w_in1g'] = chunkmajor(w_in1[:, 2048:2064], 16)[0]
    wq, wk, wv = f(inp['w_q'])[0], f(inp['w_k'])[0], f(inp['w_v'])[0]
    sh['wqc'] = np.ascontiguousarray(np.concatenate([chunkmajor(wq[h]) for h in range(4)]))
    sh['wkc'] = np.ascontiguousarray(np.concatenate([chunkmajor(wk[h]) for h in range(4)]))
    sh['wkw'] = np.ascontiguousarray(np.stack([chunkmajor(wk[h], 256)[0] for h in range(4)]))
    sh['wvw'] = np.ascontiguousarray(np.stack([chunkmajor(wv[h], 256)[0] for h in range(4)]))
    sh['w_out1c'] = chunkmajor(f(inp['w_out1'])[0])
    w_up = f(inp['w_up'])
    sh['w_upc'] = np.ascontiguousarray(np.concatenate([chunkmajor(w_up[l]) for l in range(2)]))
    w_down = f(inp['w_down'])
    sh['w_downc'] = np.ascontiguousarray(np.concatenate([chunkmajor(w_down[l]) for l in range(2)]))
    w_mod = f(inp['w_mod'])
    sh['w_modc'] = np.ascontiguousarray(np.concatenate([chunkmajor(w_mod[l]) for l in range(2)]))
    vec = np.zeros((128, NVEC), np.float32)

    def put(n, a):
        o, w = VEC_OFF[n]
        a = np.asarray(a, np.float32)
        vec[:a.shape[0], o:o + w] = a
    for l in range(2):
        put('bmod%d' % l, colvec(f(inp['b_mod'])[l]))
        put('n1w%d' % l, colvec(f(inp['norm1_w'])[l]))
        put('n2w%d' % l, colvec(f(inp['norm2_w'])[l]))
        fw = f(inp['fconv_w'])[l]
        put('fcw%d' % l, np.concatenate([colvec(fw[k]) for k in range(3)], axis=1))
        put('fcb%d' % l, colvec(f(inp['fconv_b'])[l]))
    put('fnw', colvec(f(inp['final_norm_w'])))
    mw = f(inp['mconv_w'])[0]
    put('mcw', np.concatenate([colvec(mw[k]) for k in range(3)], axis=1))
    put('mcb', colvec(f(inp['mconv_b'])[0]))
    put('hnw', colvec(f(inp['head_norm_w'])[0]))
    put('skw', colvec(f(inp['skip_w'])[0]))
    put('sbw', f(inp['subln_w'])[0].reshape(128, 1))
    put('lam', np.stack([f(inp['lam_q1'])[0], f(inp['lam_k1'])[0], f(inp['lam_q2'])[0], f(inp['lam_k2'])[0]], axis=1))
    put('gnw', np.broadcast_to(f(inp['gate_norm_w'])[0][None, :], (128, 512)))
    put('bg', np.broadcast_to(f(inp['b_gates'])[0][None, :], (128, 16)))
    ws = f(inp['w_spatial'])[0]
    put('wsT', ws.transpose(2, 0, 1).reshape(128, 512))
    sh['vec'] = vec
    sh['bsrow'] = np.ascontiguousarray(f(inp['b_spatial'])[0].reshape(1, 512))
    sh['cst'] = make_consts()
    C, S = rope_tables(TS)
    sh['ropeC'] = C
    sh['ropeS'] = S
    return sh


_CACHE = {}


def kernel(**inp):
    f = lambda a: np.ascontiguousarray(np.asarray(a, dtype=np.float32))
    xp, xs = f(inp['x_prompt']), f(inp['x_sample'])
    NB, TS = xs.shape[0], xs.shape[1]
    BP, TP = xp.shape[0], xp.shape[1]
    ncores = NB
    NPR = BP // ncores
    PAST = inp['cache_k'].shape[2]
    key = (TS, NPR, TP, PAST)
    if key not in _CACHE:
        _CACHE[key] = Builder(TS, NPR, TP, PAST)
    bld = _CACHE[key]
    sh = prep_shared(inp, TS)
    ck, cv = f(inp['cache_k']), f(inp['cache_v'])
    sC, sn, sm = f(inp['state_C']), f(inp['state_n']), f(inp['state_m'])
    c, cctx = f(inp['c']), f(inp['c_ctx'])
    in_maps = []
    for b in range(ncores):
        m = dict(sh)
        m['xs'] = xs[b]
        m['xp'] = np.ascontiguousarray(xp[b * NPR:(b + 1) * NPR].reshape(NPR * TP, 1024))
        m['ck'] = np.ascontiguousarray(ck[b, 0].reshape(PAST, 512))
        m['cv'] = np.ascontiguousarray(cv[b, 0].reshape(PAST, 512))
        m['sC'] = np.ascontiguousarray(sC[b, 0].reshape(8, 2, 128, 256))
        m['sn'] = np.ascontiguousarray(sn[b, 0].reshape(8, 2, 128).transpose(2, 0, 1).reshape(128, 16))
        m['sm'] = np.ascontiguousarray(np.broadcast_to(sm[b, 0].reshape(1, 8), (128, 8)))
        cd = np.stack([c[b], cctx], axis=1)
        m['cond'] = np.ascontiguousarray(cd.reshape(8, 128, 2).transpose(1, 0, 2).reshape(128, 16))
        in_maps.append(m)
    res = run_bass_kernel_spmd(bld.nc, in_maps, core_ids=list(range(ncores)))
    R = res.results
    y_prompt = np.concatenate([R[b]['yp'].reshape(NPR, TP, 1024) for b in range(ncores)], axis=0)
    y_sample = np.stack([R[b]['ys'] for b in range(ncores)], axis=0)
    nk = np.concatenate([R[b]['nk'].reshape(NPR, 1, TP, 4, 128) for b in range(ncores)], axis=0)
    nv = np.concatenate([R[b]['nv'].reshape(NPR, 1, TP, 4, 128) for b in range(ncores)], axis=0)
    nC = np.concatenate([R[b]['nC'].reshape(NPR, 1, 2, 4, 256, 256) for b in range(ncores)], axis=0)
    nn = np.concatenate([R[b]['nn'].reshape(NPR, 128, 8, 2).transpose(0, 2, 3, 1).reshape(NPR, 1, 2, 4, 256) for b in range(ncores)], axis=0)
    nm = np.concatenate([R[b]['nm'].reshape(NPR, 1, 2, 4) for b in range(ncores)], axis=0)
    return (y_prompt.astype(np.float32), y_sample.astype(np.float32), nk.astype(np.float32), nv.astype(np.float32),
            nC.astype(np.float32), nn.astype(np.float32), nm.astype(np.float32))
```

```python
import math
import numpy as np
import concourse.bass as bass
import concourse.mybir as mybir
from concourse.bass_utils import run_bass_kernel_spmd
from contextlib import ExitStack

F32 = mybir.dt.float32
BF16 = mybir.dt.bfloat16
AF = mybir.ActivationFunctionType
ALU = mybir.AluOpType
AX = mybir.AxisListType
COMPUTE = ('pe', 'act', 'dve', 'pool')
EPS = 1e-6
STRICT_SAME_ENGINE = True
NEG = 30000.0


class Tok:
    __slots__ = ('name', 'w', 'r', 'rd')

    def __init__(self, name):
        self.name = name
        self.w = None
        self.r = {}
        self.rd = []


class Ins:
    __slots__ = ('eng', 'fn', 'deps', 'dma', 'signals', 'sem', 'tick', 'prev_same_sem')

    def __init__(self, eng, fn, dma):
        self.eng = eng
        self.fn = fn
        self.dma = dma
        self.deps = []
        self.signals = dma
        self.sem = None
        self.tick = 0
        self.prev_same_sem = None


class _Rec:
    def __init__(self):
        self.call = None

    def __getattr__(self, name):
        def f(*a, **k):
            self.call = (name, a, k)
            return self
        return f


class Prog:
    def __init__(self, nc, n_dma_sems=16):
        self.nc = nc
        self.q = {e: [] for e in ('pe', 'act', 'dve', 'pool', 'sp')}
        self.toks = {}
        self.n_dma_sems = n_dma_sems
        self.es = ExitStack()
        self.n_alloc = 0

    def sbuf(self, shape, dtype, name=None):
        self.n_alloc += 1
        return self.es.enter_context(self.nc.sbuf_tensor("sb_" + (name or f"t{self.n_alloc}"), list(shape), dtype))

    def psum(self, shape, dtype, name=None):
        self.n_alloc += 1
        return self.es.enter_context(self.nc.psum_tensor("ps_" + (name or f"t{self.n_alloc}"), list(shape), dtype))

    def tok(self, *key):
        t = self.toks.get(key)
        if t is None:
            t = Tok(key)
            self.toks[key] = t
        return t

    def op(self, eng, fn, reads=(), writes=(), dma=False):
        rec = _Rec()
        fn(rec)
        assert rec.call is not None
        ins = Ins(eng, rec.call, dma)
        deps = {}

        def add(p, raw):
            if p is None or p is ins:
                return
            if (not p.dma) and (not dma) and p.eng == eng:
                if eng == 'pe' or (not raw and not STRICT_SAME_ENGINE):
                    return
            deps[id(p)] = p

        for t in reads:
            add(t.w, True)
            if t.name[0] == 'bank':
                for r in t.r.values():
                    if r.eng != eng:
                        add(r, False)
        for t in writes:
            add(t.w, False)
            for r in t.r.values():
                add(r, False)
            for r in t.rd:
                add(r, False)
        for t in reads:
            if dma:
                t.rd.append(ins)
            else:
                t.r[eng] = ins
        for t in writes:
            t.w = ins
            t.r = {}
            t.rd = []
        ins.deps = list(deps.values())
        for p in ins.deps:
            p.signals = True
        self.q[eng].append(ins)
        return ins

    def emit(self):
        nc = self.nc
        es = self.es
        sems = {e: es.enter_context(nc.semaphore(f"s_{e}")) for e in COMPUTE}
        dsems = {e: [es.enter_context(nc.semaphore(f"d_{e}{i}")) for i in range(self.n_dma_sems)]
                 for e in ('sp', 'pool', 'act')}
        final = {}
        for e in COMPUTE + ('sp',):
            cnt = 0
            dcnt = [0] * self.n_dma_sems
            dlast = [None] * self.n_dma_sems
            rr = 0
            for ins in self.q[e]:
                if ins.dma:
                    s = rr % self.n_dma_sems
                    rr += 1
                    dcnt[s] += 16
                    ins.sem = dsems[e][s]
                    ins.tick = dcnt[s]
                    ins.prev_same_sem = dlast[s]
                    dlast[s] = ins
                elif ins.signals:
                    cnt += 1
                    ins.sem = sems[e]
                    ins.tick = cnt
            final[e] = [(dsems[e][s], dcnt[s]) for s in range(self.n_dma_sems) if dcnt[s]]
        self.stats = {e: len(self.q[e]) for e in self.q}
        nwaits = {e: 0 for e in self.q}

        def run(e, eng):
            seen = {}

            def wait(sem, val):
                k = id(sem)
                if seen.get(k, 0) >= val:
                    return
                seen[k] = val
                eng.wait_ge(sem, val)
                nwaits[e] += 1

            for ins in self.q[e]:
                if ins.dma and ins.prev_same_sem is not None:
                    wait(ins.prev_same_sem.sem, ins.prev_same_sem.tick)
                for p in ins.deps:
                    wait(p.sem, p.tick)
                name, a, k = ins.fn
                bi = getattr(eng, name)(*a, **k)
                if ins.dma:
                    bi.then_inc(ins.sem, 16)
                elif ins.signals:
                    bi.then_inc(ins.sem, 1)
            for (sem, val) in final.get(e, []):
                wait(sem, val)

        with nc.Block() as block:
            @block.tensor
            def _(eng):
                run('pe', eng)

            @block.scalar
            def _(eng):
                run('act', eng)

            @block.vector
            def _(eng):
                run('dve', eng)

            @block.gpsimd
            def _(eng):
                run('pool', eng)

            @block.sync
            def _(eng):
                run('sp', eng)
        self.nwaits = nwaits
        es.close()


class Buf:
    def __init__(self, ap2d, toks, cpp, w):
        self.ap = ap2d
        self.toks = toks
        self.cpp = cpp
        self.w = w

    def v(self, i, a=0, b=None):
        b = self.w if b is None else b
        return self.ap[:, i * self.w + a: i * self.w + b]

    def t(self, i, a=0, b=None):
        b = self.w if b is None else b
        c0 = i * self.w + a
        c1 = i * self.w + b
        return self.toks[c0 // self.cpp: (c1 - 1) // self.cpp + 1]

    def v3(self, i0, i1, a=0, b=None):
        b = self.w if b is None else b
        return self.ap[:, i0 * self.w: i1 * self.w].rearrange("p (n c) -> p n c", c=self.w)[:, :, a:b]

    def t3(self, i0, i1, a=0, b=None):
        out = []
        seen = set()
        for i in range(i0, i1):
            for t in self.t(i, a, b):
                if id(t) not in seen:
                    seen.add(id(t))
                    out.append(t)
        return out

    def all(self):
        return self.toks


def chunkmajor(W, cs=128):
    K, N = W.shape
    kc = K // 128
    n = N // cs
    return np.ascontiguousarray(W.reshape(kc, 128, n, cs).transpose(2, 1, 0, 3).reshape(n, 128, kc * cs))


def colvec(v):
    return np.ascontiguousarray(np.asarray(v).reshape(-1, 128).T)


VEC_LAYOUT = [('bmod0', 48), ('bmod1', 48), ('n1w0', 8), ('n1w1', 8), ('n2w0', 8), ('n2w1', 8), ('fnw', 8),
              ('fcw0', 66), ('fcw1', 66), ('fcb0', 22), ('fcb1', 22), ('mcw', 24), ('mcb', 8), ('hnw', 8),
              ('skw', 8), ('sbw', 1), ('lam', 4), ('gnw', 512), ('bg', 16), ('wsT', 512)]
VEC_OFF = {}
_o = 0
for _n, _w in VEC_LAYOUT:
    VEC_OFF[_n] = (_o, _w)
    _o += _w
NVEC = _o
CST_LAYOUT = ['ident', 'ones', 'Uf', 'Ub', 'self', 'selb', 'perm', 'nmf', 'nmb', 'pmf', 'pmb']
CST_OFF = {n: i * 128 for i, n in enumerate(CST_LAYOUT)}
NCST = 128 * len(CST_LAYOUT)


def make_consts():
    c = np.zeros((128, NCST), np.float32)
    i = np.arange(128)
    s, t = np.meshgrid(i, i, indexing='ij')

    def put(n, m):
        c[:, CST_OFF[n]:CST_OFF[n] + 128] = m
    put('ident', np.eye(128))
    put('ones', np.ones((128, 128)))
    put('Uf', (s <= t))
    put('Ub', (s >= t))
    put('self', (s == 127))
    put('selb', (s == 0))
    perm = np.zeros((128, 128))
    for p in range(128):
        half = (p % 32) // 16
        partner = p + 16 if half == 0 else p - 16
        perm[partner, p] = 1.0
    put('perm', perm)
    put('nmf', np.where(t <= s, 0.0, -NEG))
    put('nmb', np.where(t >= s, 0.0, -NEG))
    put('pmf', np.where(s <= t, 0.0, NEG))
    put('pmb', np.where(s >= t, 0.0, NEG))
    return c


def rope_tables(T):
    n_freq = 16
    inv = (10000.0 ** (-np.arange(n_freq, dtype=np.float32) / n_freq)).astype(np.float32)
    tt = np.arange(T)
    rows = (tt // 64).astype(np.float32)
    cols = (tt % 64).astype(np.float32)
    C = np.zeros((128, T), np.float32)
    S = np.zeros((128, T), np.float32)
    for p in range(128):
        axis = (p % 64) // 32
        half = (p % 32) // 16
        f = p % 16
        pos = rows if axis == 0 else cols
        ang = (pos * inv[f]).astype(np.float32)
        C[p] = np.cos(ang)
        S[p] = np.sin(ang) * (-1.0 if half == 0 else 1.0)
    return C, S


class Builder:
    def __init__(self, TS, NPR, TP, PAST):
        self.TS, self.NPR, self.TP, self.PAST = TS, NPR, TP, PAST
        self.TPT = NPR * TP
        nc = bass.Bass("TRN2", target_bir_lowering=False)
        self.nc = nc
        self.P = Prog(nc)
        self.d = {}
        self.decl_io()
        self.alloc()
        self.prologue()
        self.early_x = None
        gp = dict(name='p', X=self.XP, T=self.TPT, seqs=[(i * TP, TP) for i in range(NPR)], sample=False, cond=1)
        gs = dict(name='s', X=self.XS, T=TS, seqs=[(0, TS)], sample=True, cond=0)
        st = 'x0f1gy'
        grp = 'ps'
        early = ('x' in st) and ('s' in grp) and TS > self.TPT
        if early:
            self.load_x(gs, sts=range(self.TPT // 128, TS // 128))
        self.prologue_mods()
        for g in (gp, gs):
            if g['name'] not in grp:
                continue
            if 'x' in st:
                if g['sample'] and early:
                    self.load_x(g, sts=range(0, self.TPT // 128))
                else:
                    self.load_x(g)
            if '0' in st:
                self.layer0(g)
            if 'f' in st:
                self.ffn(g, 0)
            if '1' in st:
                self.layer1(g)
            if 'g' in st:
                self.ffn(g, 1)
            if 'y' in st:
                self.final(g)
        self.P.emit()

    def din(self, name, shape, dt=F32):
        self.d[name] = self.nc.dram_tensor(name, list(shape), dt, kind="ExternalInput").ap()

    def dout(self, name, shape):
        self.d[name] = self.nc.dram_tensor(name, list(shape), F32, kind="ExternalOutput").ap()

    def decl_io(self):
        TS, TPT, PAST, NPR = self.TS, self.TPT, self.PAST, self.NPR
        self.din('xs', [TS, 1024]); self.din('xp', [TPT, 1024])
        self.din('ck', [PAST, 512]); self.din('cv', [PAST, 512])
        self.din('sC', [8, 2, 128, 256]); self.din('sn', [128, 16]); self.din('sm', [128, 8])
        self.din('cond', [128, 16])
        self.din('w_in0c', [20, 128, 1024]); self.din('w_in0w', [3, 128, 4096]); self.din('w_out0c', [8, 128, 1024])
        self.din('w_in1c', [16, 128, 1024]); self.din('w_in1g', [128, 128])
        self.din('wqc', [8, 128, 256]); self.din('wkc', [8, 128, 256]); self.din('wkw', [4, 128, 512]); self.din('wvw', [4, 128, 512])
        self.din('w_out1c', [8, 128, 1024]); self.din('w_upc', [88, 128, 1024]); self.din('w_downc', [16, 128, 2816])
        self.din('w_modc', [96, 128, 1024])
        self.din('vec', [128, NVEC]); self.din('bsrow', [1, 512]); self.din('cst', [128, NCST])
        self.din('ropeC', [128, TS]); self.din('ropeS', [128, TS])
        self.dout('ys', [TS, 1024]); self.dout('yp', [TPT, 1024])
        self.dout('nk', [TPT, 512]); self.dout('nv', [TPT, 512])
        self.dout('nC', [NPR, 8, 2, 128, 256]); self.dout('nn', [NPR, 128, 16]); self.dout('nm', [NPR, 1, 8])
        TK = PAST + TS
        self.TK = TK
        self.d['KTd'] = self.nc.dram_tensor('KTd', [4, 128, TK], BF16, kind="Internal").ap()
        self.d['Vd'] = self.nc.dram_tensor('Vd', [4, 128, TK // 128, 128], BF16, kind="Internal").ap()
        self.d['Hfd'] = self.nc.dram_tensor('Hfd', [TS // 128, 128, 1024], BF16, kind="Internal").ap()

    def page_buf(self, n, w, dtype):
        size = 4 if dtype == F32 else 2
        cols = n * w
        pages = (cols * size + 1023) // 1024
        p0 = self.page_next
        self.page_next += pages
        assert self.page_next <= self.NPAGES, ("arena overflow", self.page_next)
        ap = self.arena[:, p0 * 512:(p0 + pages) * 512]
        if dtype == F32:
            ap = ap.bitcast(F32)
        ap = ap[:, 0:cols]
        toks = [self.P.tok('pg', p0 + i) for i in range(pages)]
        return Buf(ap, toks, 1024 // size, w)

    def own_buf(self, n, w, dtype, name):
        t = self.P.sbuf([128, n * w], dtype, name)
        size = 4 if dtype == F32 else 2
        cpp = 1024 // size
        pages = (n * w + cpp - 1) // cpp
        toks = [self.P.tok(name, i) for i in range(pages)]
        return Buf(t[:], toks, cpp, w)

    def alloc(self):
        P = self.P
        self.XS = self.own_buf(8, self.TS, BF16, 'XS')
        self.XP = self.XS
        self.vec = P.sbuf([128, NVEC], F32, 'vec')
        self.cst = P.sbuf([128, 7 * 128], F32, 'cst')
        self.cb = P.sbuf([128, 7 * 128], BF16, 'cb')
        self.tv = P.tok('vec'); self.tc = P.tok('cst'); self.tcb = P.tok('cb')
        self.mods = P.sbuf([128, 2 * 96], F32, 'mods')
        self.AB = P.sbuf([128, 2 * 2 * 6 * 8], F32, 'AB')
        self.tAB = P.tok('AB')
        self.small = P.sbuf([128, 64], F32, 'small')
        self.tsmall = P.tok('small')
        self.bsrow = P.sbuf([1, 512], BF16, 'bsrow')
        self.wsTb = P.sbuf([128, 512], BF16, 'wsTb')
        self.wgb = P.sbuf([128, 128], BF16, 'wgb')
        self.NPAGES = 127
        self.arena = P.sbuf([128, self.NPAGES * 512], BF16, 'arena')[:]
        self.page_next = 0
        self.NSLOT = 8
        self.slots = [self.page_buf(1, 1024, BF16) for _ in range(self.NSLOT)]
        self.wslots = []
        self.slot_rr = 0
        self.wslot_rr = 0
        self.page_base = self.page_next
        self.banks = [P.psum([128, 512], F32, f'bank{i}') for i in range(8)]
        self.btok = [P.tok('bank', i) for i in range(8)]
        self.bank_rr = 0
        self.held = set()

    def reset_pages(self):
        self.page_next = self.page_base

    def bank(self, hold=False):
        while True:
            i = self.bank_rr % 8
            self.bank_rr += 1
            if i not in self.held:
                break
        if hold:
            self.held.add(i)
        return self.banks[i], self.btok[i]

    def bank_wait(self, limit=8):
        while len(self.held) >= limit:
            yield
        return self.bank(hold=True)

    def release(self, *bks):
        for (b, t) in bks:
            self.held.discard(self.btok.index(t))

    def vcol(self, name, a=0, b=None):
        o, w = VEC_OFF[name]
        b = w if b is None else b
        return self.vec[:, o + a:o + b]

    def ccol(self, name):
        o = CST_OFF[name]
        return self.cst[:, o:o + 128]

    def AB_(self, l, cond, which, kc=None):
        base = ((l * 2 + cond) * 6 + which) * 8
        if kc is None:
            return self.AB[:, base:base + 8]
        return self.AB[:, base + kc:base + kc + 1]

    def wload(self, src_ap, ncols):
        s = self.slots[self.slot_rr % self.NSLOT]
        self.slot_rr += 1
        self.P.op('pool', lambda e, s=s, src_ap=src_ap, ncols=ncols: e.dma_start(out=s.ap[:, 0:ncols], in_=src_ap),
                  writes=s.all(), dma=True)
        return s

    def wload_wide(self, src_ap, ncols):
        s = self.wslots[self.wslot_rr % len(self.wslots)]
        self.wslot_rr += 1
        self.P.op('pool', lambda e, s=s, src_ap=src_ap, ncols=ncols: e.dma_start(out=s.ap[:, 0:ncols], in_=src_ap),
                  writes=s.all(), dma=True)
        return s

    def prologue(self):
        P, d = self.P, self.d
        P.op('sp', lambda e: e.dma_start(out=self.vec[:], in_=d['vec'][:, :]), writes=[self.tv], dma=True)
        P.op('sp', lambda e: e.dma_start(out=self.cst[:], in_=d['cst'][:, 0:7 * 128]), writes=[self.tc], dma=True)
        P.op('pool', lambda e: e.dma_start(out=self.cb[:, 3 * 128:7 * 128], in_=d['cst'][:, 7 * 128:11 * 128]), writes=[P.tok('cbm')], dma=True)
        P.op('pool', lambda e: e.dma_start(out=self.bsrow[:], in_=d['bsrow'][:, :]), writes=[P.tok('bsrow')], dma=True)
        P.op('pool', lambda e: e.dma_start(out=self.wgb[:], in_=d['w_in1g'][:, :]), writes=[P.tok('wgb')], dma=True)
        for i, n in enumerate(('ident', 'ones', 'perm')):
            P.op('dve', lambda e, i=i, n=n: e.tensor_copy(out=self.cb[:, i * 128:(i + 1) * 128], in_=self.ccol(n)),
                 reads=[self.tc], writes=[self.tcb])
        P.op('dve', lambda e: e.tensor_copy(out=self.wsTb[:], in_=self.vcol('wsT')), reads=[self.tv], writes=[P.tok('wsTb')])
        self.identb = self.cb[:, 0:128]; self.onesb = self.cb[:, 128:256]; self.permb = self.cb[:, 256:384]

    def prologue_mods(self):
        P, d = self.P, self.d
        self.page_next = self.page_base + 8
        cnd = self.page_buf(1, 16, F32)
        scb = self.page_buf(1, 16, BF16)
        P.op('sp', lambda e: e.dma_start(out=cnd.ap, in_=d['cond'][:, :]), writes=cnd.all(), dma=True)
        P.op('act', lambda e: e.activation(out=scb.ap, in_=cnd.ap, func=AF.Silu), reads=cnd.all(), writes=scb.all())
        tm = P.tok('mods')
        for l in range(2):
            ps, tps = self.bank()
            for c in range(48):
                s = self.wload(d['w_modc'][l * 48 + c], 1024)
                for kc in range(8):
                    P.op('pe', lambda e, ps=ps, s=s, kc=kc, c=c: e.matmul(
                        ps[:, c * 2:c * 2 + 2], lhsT=s.ap[:, kc * 128:(kc + 1) * 128], rhs=scb.ap[:, kc * 2:kc * 2 + 2],
                        start=(kc == 0), stop=(kc == 7)), reads=s.all() + scb.all(), writes=[tps])
            bm = self.vcol('bmod%d' % l)
            P.op('dve', lambda e, ps=ps, l=l, bm=bm: e.tensor_tensor(
                out=self.mods[:, l * 96:(l + 1) * 96].rearrange("p (c n) -> p c n", n=2),
                in0=ps[:, 0:96].rearrange("p (c n) -> p c n", n=2),
                in1=bm.unsqueeze(2).to_broadcast([128, 48, 2]), op=ALU.add), reads=[tps, self.tv], writes=[tm])
            for cond in range(2):
                def mv(which, l=l, cond=cond):
                    return self.mods[:, l * 96:(l + 1) * 96].rearrange("p (c n) -> p c n", n=2)[:, which * 8:(which + 1) * 8, cond]
                for half in range(2):
                    nw = self.vcol('n%dw%d' % (half + 1, l))
                    P.op('dve', lambda e, l=l, cond=cond, half=half, nw=nw, mv=mv: e.scalar_tensor_tensor(
                        out=self.AB_(l, cond, half * 3 + 0), in0=mv(half * 3 + 1), scalar=1.0, in1=nw,
                        op0=ALU.add, op1=ALU.mult), reads=[tm, self.tv], writes=[self.tAB])
                    P.op('dve', lambda e, l=l, cond=cond, half=half, mv=mv: e.tensor_copy(
                        out=self.AB_(l, cond, half * 3 + 1), in_=mv(half * 3 + 0)), reads=[tm], writes=[self.tAB])
                    P.op('dve', lambda e, l=l, cond=cond, half=half, mv=mv: e.tensor_copy(
                        out=self.AB_(l, cond, half * 3 + 2), in_=mv(half * 3 + 2)), reads=[tm], writes=[self.tAB])
        lam_init = 0.8 - 0.6 * math.exp(-0.3 * 0)
        lo, _ = VEC_OFF['lam']
        pr = self.page_buf(1, 2, F32)
        P.op('dve', lambda e: e.tensor_tensor(out=pr.ap[0:64, 0:1], in0=self.vec[0:64, lo:lo + 1], in1=self.vec[0:64, lo + 1:lo + 2], op=ALU.mult),
             reads=[self.tv], writes=pr.all())
        P.op('dve', lambda e: e.tensor_tensor(out=pr.ap[0:64, 1:2], in0=self.vec[0:64, lo + 2:lo + 3], in1=self.vec[0:64, lo + 3:lo + 4], op=ALU.mult),
             reads=[self.tv] + pr.all(), writes=pr.all())
        ps, tps = self.bank()
        P.op('pe', lambda e, ps=ps: e.matmul(ps[:, 0:2], lhsT=self.ccol('ones')[0:64, :], rhs=pr.ap[0:64, 0:2], start=True, stop=True),
             reads=[self.tc] + pr.all(), writes=[tps])
        ex = self.page_buf(1, 2, F32)
        P.op('act', lambda e, ps=ps: e.activation(out=ex.ap, in_=ps[:, 0:2], func=AF.Exp), reads=[tps], writes=ex.all())
        P.op('dve', lambda e: e.scalar_tensor_tensor(out=self.small[:, 0:1], in0=ex.ap[:, 1:2], scalar=-lam_init, in1=ex.ap[:, 0:1],
                                                     op0=ALU.add, op1=ALU.subtract), reads=ex.all(), writes=[self.tsmall])
        P.op('dve', lambda e: e.tensor_scalar(out=self.small[:, 1:2], in0=self.vcol('sbw'), scalar1=(1.0 - lam_init), scalar2=None, op0=ALU.mult),
             reads=[self.tv, self.tsmall], writes=[self.tsmall])

    def load_x(self, g, sts=None):
        P, d = self.P, self.d
        X = g['X']
        src = d['xs'] if g['sample'] else d['xp']
        self.reset_pages()
        stg = [self.page_buf(1, 1024, F32) for _ in range(2)]
        for st in (range(g['T'] // 128) if sts is None else sts):
            sb = stg[st % 2]
            P.op('sp', lambda e, sb=sb, st=st: e.dma_start(out=sb.ap, in_=src[st * 128:(st + 1) * 128, :]), writes=sb.all(), dma=True)
            for half in range(2):
                ps, tps = self.bank()
                for k4 in range(4):
                    kc = half * 4 + k4
                    P.op('pe', lambda e, ps=ps, sb=sb, kc=kc, k4=k4: e.transpose(ps[:, k4 * 128:(k4 + 1) * 128], sb.ap[:, kc * 128:(kc + 1) * 128], self.ccol('ident')),
                         reads=sb.all() + [self.tc], writes=[tps])
                for k4 in range(4):
                    kc = half * 4 + k4
                    eng = 'act' if k4 % 2 == 0 else 'dve'
                    if eng == 'act':
                        P.op('act', lambda e, ps=ps, kc=kc, k4=k4, st=st: e.activation(out=X.v(kc, st * 128, (st + 1) * 128), in_=ps[:, k4 * 128:(k4 + 1) * 128], func=AF.Copy),
                             reads=[tps], writes=X.t(kc, st * 128, (st + 1) * 128))
                    else:
                        P.op('dve', lambda e, ps=ps, kc=kc, k4=k4, st=st: e.tensor_copy(out=X.v(kc, st * 128, (st + 1) * 128), in_=ps[:, k4 * 128:(k4 + 1) * 128]),
                             reads=[tps], writes=X.t(kc, st * 128, (st + 1) * 128))

    def mod_scratch(self, w=512):
        return dict(sq=[self.page_buf(1, w, BF16) for _ in range(2)], t32=[self.page_buf(1, w, F32) for _ in range(2)],
                    rs=self.page_buf(1, w, F32))

    def modulate(self, g, c0, n, A, B, dst, o0, ms):
        P = self.P
        X = g['X']
        rs = ms['rs']
        ps, tps = self.bank()
        for kc in range(8):
            sq = ms['sq'][kc % 2]
            P.op('act', lambda e, kc=kc, sq=sq: e.activation(out=sq.v(0, 0, n), in_=X.v(kc, c0, c0 + n), func=AF.Square),
                 reads=X.t(kc, c0, c0 + n), writes=sq.all())
            P.op('pe', lambda e, ps=ps, kc=kc, sq=sq: e.matmul(ps[:, 0:n], lhsT=self.onesb, rhs=sq.v(0, 0, n), start=(kc == 0), stop=(kc == 7)),
                 reads=sq.all() + [self.tcb], writes=[tps])
        P.op('act', lambda e, ps=ps: e.activation(out=rs.v(0, 0, n), in_=ps[:, 0:n], func=AF.Ln, scale=1.0 / 1024, bias=EPS),
             reads=[tps], writes=rs.all())
        P.op('act', lambda e: e.activation(out=rs.v(0, 0, n), in_=rs.v(0, 0, n), func=AF.Exp, scale=-0.5), reads=rs.all(), writes=rs.all())
        for kc in range(8):
            t32 = ms['t32'][kc % 2]
            P.op('dve', lambda e, kc=kc, t32=t32: e.tensor_tensor(out=t32.v(0, 0, n), in0=X.v(kc, c0, c0 + n), in1=rs.v(0, 0, n), op=ALU.mult),
                 reads=X.t(kc, c0, c0 + n) + rs.all(), writes=t32.all())
            if B is not None:
                P.op('act', lambda e, kc=kc, t32=t32: e.activation(out=dst.v(kc, o0, o0 + n), in_=t32.v(0, 0, n), func=AF.Identity, scale=A(kc), bias=B(kc)),
                     reads=t32.all() + [self.tAB, self.tv], writes=dst.t(kc, o0, o0 + n))
            else:
                P.op('act', lambda e, kc=kc, t32=t32: e.activation(out=dst.v(kc, o0, o0 + n), in_=t32.v(0, 0, n), func=AF.Identity, scale=A(kc)),
                     reads=t32.all() + [self.tAB, self.tv], writes=dst.t(kc, o0, o0 + n))

    def fm_proj(self, s, ncols_k, rhs_of, rhs_toks_of, n, nk=8):
        P = self.P
        ps, tps = self.bank()
        for kc in range(nk):
            P.op('pe', lambda e, ps=ps, kc=kc: e.matmul(ps[:, 0:n], lhsT=s.ap[:, kc * ncols_k:(kc + 1) * ncols_k], rhs=rhs_of(kc),
                                                        start=(kc == 0), stop=(kc == nk - 1)),
                 reads=s.all() + rhs_toks_of(kc), writes=[tps])
        return ps, tps

    def layer0(self, g):
        P, d = self.P, self.d
        X = g['X']
        sample = g['sample']
        cond = g['cond']
        PAST = self.PAST if sample else 0
        A1 = lambda kc: self.AB_(0, cond, 0, kc)
        B1 = lambda kc: self.AB_(0, cond, 1, kc)
        G1 = lambda kc: self.AB_(0, cond, 2, kc)
        self.reset_pages()
        hb = self.page_buf(8, 512, BF16)
        self.wslots = [self.page_buf(1, 4096, BF16)]
        ms = self.mod_scratch()
        ropeC = self.page_buf(1, 512, F32); ropeS = self.page_buf(1, 512, F32)
        rsets = [(self.page_buf(1, 512, BF16), self.page_buf(1, 512, F32), self.page_buf(1, 512, F32)) for _ in range(2)]
        rrot = [0]
        kst = [self.page_buf(1, 512, BF16) for _ in range(2)]
        vst = [self.page_buf(1, 512, BF16) for _ in range(2)]
        ost = [self.page_buf(1, 512, F32) for _ in range(2)]
        ntile = g['T'] // 512
        KTd, Vd = d['KTd'], d['Vd']
        tK = lambda h, ch: P.tok('KTd', h, ch)
        tV = lambda h, ch: P.tok('Vd', h, ch)

        def rope_or_copy(ps, tps, dst_ap, dst_toks, scale=None):
            if sample:
                zb, ta, tb = rsets[rrot[0] % len(rsets)]
                rrot[0] += 1
                P.op('act', lambda e: e.activation(out=zb.ap, in_=ps[:, 0:512], func=AF.Copy), reads=[tps], writes=zb.all())
                pp, tpp = self.bank()
                P.op('pe', lambda e: e.matmul(pp[:, 0:512], lhsT=self.permb, rhs=zb.ap, start=True, stop=True),
                     reads=zb.all() + [self.tcb], writes=[tpp])
                P.op('dve', lambda e: e.tensor_tensor(out=ta.ap, in0=ps[:, 0:512], in1=ropeC.ap, op=ALU.mult),
                     reads=[tps] + ropeC.all(), writes=ta.all())
                P.op('dve', lambda e: e.tensor_tensor(out=tb.ap, in0=pp[:, 0:512], in1=ropeS.ap, op=ALU.mult),
                     reads=[tpp] + ropeS.all(), writes=tb.all())
                P.op('dve', lambda e: e.tensor_tensor(out=dst_ap, in0=ta.ap, in1=tb.ap, op=ALU.add),
                     reads=ta.all() + tb.all(), writes=dst_toks)
            else:
                P.op('act', lambda e: e.activation(out=dst_ap, in_=ps[:, 0:512], func=AF.Copy), reads=[tps], writes=dst_toks)

        if sample:
            cs = [self.page_buf(1, 512, F32) for _ in range(2)]
            for st in range(self.PAST // 128):
                sb = cs[st % 2]
                P.op('sp', lambda e, sb=sb, st=st: e.dma_start(out=sb.ap, in_=d['ck'][st * 128:(st + 1) * 128, :]), writes=sb.all(), dma=True)
                ps, tps = self.bank()
                for h in range(4):
                    P.op('pe', lambda e, ps=ps, sb=sb, h=h: e.transpose(ps[:, h * 128:(h + 1) * 128], sb.ap[:, h * 128:(h + 1) * 128], self.ccol('ident')),
                         reads=sb.all() + [self.tc], writes=[tps])
                kb = kst[st % 2]
                P.op('act', lambda e, ps=ps, kb=kb: e.activation(out=kb.ap, in_=ps[:, 0:512], func=AF.Copy), reads=[tps], writes=kb.all())
                for h in range(4):
                    P.op('sp', lambda e, kb=kb, h=h, st=st: e.dma_start(out=KTd[h, :, st * 128:(st + 1) * 128], in_=kb.ap[:, h * 128:(h + 1) * 128]),
                         reads=kb.all(), writes=[tK(h, st)], dma=True)
                vb = vst[st % 2]
                P.op('pool', lambda e, vb=vb, st=st: e.dma_start(out=vb.ap, in_=d['cv'][st * 128:(st + 1) * 128, :]), writes=vb.all(), dma=True)
                for h in range(4):
                    P.op('sp', lambda e, vb=vb, h=h, st=st: e.dma_start(out=Vd[h, :, st, :], in_=vb.ap[:, h * 128:(h + 1) * 128]),
                         reads=vb.all(), writes=[tV(h, st)], dma=True)

        for ti in range(ntile):
            c0 = ti * 512
            if sample:
                P.op('sp', lambda e, c0=c0: e.dma_start(out=ropeC.ap, in_=d['ropeC'][:, c0:c0 + 512]), writes=ropeC.all(), dma=True)
                P.op('sp', lambda e, c0=c0: e.dma_start(out=ropeS.ap, in_=d['ropeS'][:, c0:c0 + 512]), writes=ropeS.all(), dma=True)
            self.modulate(g, c0, 512, A1, B1, hb, 0, ms)
            for h in range(4):
                s = self.wload(d['w_in0c'][4 + h], 1024)
                ps, tps = self.fm_proj(s, 128, lambda kc: hb.v(kc), lambda kc: hb.t(kc), 512)
                kb = kst[h % 2]
                rope_or_copy(ps, tps, kb.ap, kb.all())
                kcol = PAST + c0
                P.op('sp', lambda e, kb=kb, h=h, kcol=kcol: e.dma_start(out=KTd[h, :, kcol:kcol + 512], in_=kb.ap),
                     reads=kb.all(), writes=[tK(h, kcol // 128 + i) for i in range(4)], dma=True)
            sw = self.wload_wide(d['w_in0w'][1], 4096)
            for sub in range(4):
                ps, tps = self.bank()
                for kc in range(8):
                    P.op('pe', lambda e, ps=ps, kc=kc, sub=sub: e.matmul(ps[:, 0:512], lhsT=hb.v(kc, sub * 128, (sub + 1) * 128), rhs=sw.ap[:, kc * 512:(kc + 1) * 512],
                                                                         start=(kc == 0), stop=(kc == 7)), reads=hb.t(kc) + sw.all(), writes=[tps])
                vb = vst[sub % 2]
                P.op('act', lambda e, ps=ps, vb=vb: e.activation(out=vb.ap, in_=ps[:, 0:512], func=AF.Copy), reads=[tps], writes=vb.all())
                ch = (PAST + c0) // 128 + sub
                for h in range(4):
                    P.op('sp', lambda e, vb=vb, h=h, ch=ch: e.dma_start(out=Vd[h, :, ch, :], in_=vb.ap[:, h * 128:(h + 1) * 128]),
                         reads=vb.all(), writes=[tV(h, ch)], dma=True)
                if not sample:
                    ob = ost[sub % 2]
                    P.op('dve', lambda e, ps=ps, ob=ob: e.tensor_copy(out=ob.ap, in_=ps[:, 0:512]), reads=[tps], writes=ob.all())
                    r0 = c0 + sub * 128
                    P.op('sp', lambda e, ob=ob, r0=r0: e.dma_start(out=d['nv'][r0:r0 + 128, :], in_=ob.ap), reads=ob.all(), dma=True)
            if not sample:
                sw = self.wload_wide(d['w_in0w'][0], 4096)
                for sub in range(4):
                    ps, tps = self.bank()
                    for kc in range(8):
                        P.op('pe', lambda e, ps=ps, kc=kc, sub=sub: e.matmul(ps[:, 0:512], lhsT=hb.v(kc, sub * 128, (sub + 1) * 128), rhs=sw.ap[:, kc * 512:(kc + 1) * 512],
                                                                             start=(kc == 0), stop=(kc == 7)), reads=hb.t(kc) + sw.all(), writes=[tps])
                    ob = ost[sub % 2]
                    P.op('dve', lambda e, ps=ps, ob=ob: e.tensor_copy(out=ob.ap, in_=ps[:, 0:512]), reads=[tps], writes=ob.all())
                    r0 = c0 + sub * 128
                    P.op('sp', lambda e, ob=ob, r0=r0: e.dma_start(out=d['nk'][r0:r0 + 128, :], in_=ob.ap), reads=ob.all(), dma=True)

        self.reset_pages()
        hb = self.page_buf(8, 512, BF16)
        self.wslots = [self.page_buf(1, 4096, BF16)]
        ms = self.mod_scratch()
        ropeC = self.page_buf(1, 512, F32); ropeS = self.page_buf(1, 512, F32)
        rsets = [(self.page_buf(1, 512, BF16), self.page_buf(1, 512, F32), self.page_buf(1, 512, F32))]
        mix = self.page_buf(8, 512, BF16)
        QT = self.page_buf(4, 512, BF16)
        ub = self.page_buf(4, 512, BF16)
        vg = self.page_buf(1, 512, F32); junk = self.page_buf(1, 512, F32)
        vc = self.page_buf(1, 512, BF16)
        ss1 = self.page_buf(1, 16, F32)
        NE = 3
        E = [[self.page_buf(1, 512, BF16) for _ in range(NE)] for _ in range(2)]
        acc = [None, self.page_buf(1, 512, F32)]
        r0b = self.page_buf(1, 512, F32); r1b = self.page_buf(1, 512, F32)
        a0b = self.page_buf(1, 512, F32); a1b = self.page_buf(1, 512, F32)
        asq = self.page_buf(1, 512, BF16)
        maxk = max(ln for (_, ln) in g['seqs']) + PAST
        KTs = [self.page_buf(1, maxk, BF16) for _ in range(2)]
        Vs = [self.page_buf(maxk // 128, 128, BF16) for _ in range(2)]
        kv_rr = 0
        for ti in range(ntile):
            c0 = ti * 512
            if sample:
                P.op('sp', lambda e, c0=c0: e.dma_start(out=ropeC.ap, in_=d['ropeC'][:, c0:c0 + 512]), writes=ropeC.all(), dma=True)
                P.op('sp', lambda e, c0=c0: e.dma_start(out=ropeS.ap, in_=d['ropeS'][:, c0:c0 + 512]), writes=ropeS.all(), dma=True)
            self.modulate(g, c0, 512, A1, B1, hb, 0, ms)
            for h in range(4):
                s = self.wload(d['w_in0c'][h], 1024)
                ps, tps = self.fm_proj(s, 128, lambda kc: hb.v(kc), lambda kc: hb.t(kc), 512)
                rope_or_copy(ps, tps, QT.v(h), QT.t(h))
            for gi in range(4):
                s = self.wload(d['w_in0c'][12 + gi], 1024)
                ps, tps = self.fm_proj(s, 128, lambda kc: hb.v(kc), lambda kc: hb.t(kc), 512)
                P.op('act', lambda e, ps=ps, gi=gi: e.activation(out=ub.v(gi), in_=ps[:, 0:512], func=AF.Gelu_apprx_tanh), reads=[tps], writes=ub.t(gi))
            sw = self.wload_wide(d['w_in0w'][2], 4096)
            sbanks = [self.bank(hold=True) for _ in range(4)]
            vgs = [vg, r1b, a0b, a1b]
            P.op('dve', lambda e: e.memset(ss1.ap[:, 0:4], 0.0), writes=ss1.all())
            for sub in range(4):
                ps, tps = self.bank()
                for kc in range(8):
                    P.op('pe', lambda e, ps=ps, kc=kc, sub=sub: e.matmul(ps[:, 0:512], lhsT=hb.v(kc, sub * 128, (sub + 1) * 128), rhs=sw.ap[:, kc * 512:(kc + 1) * 512],
                                                                         start=(kc == 0), stop=(kc == 7)), reads=hb.t(kc) + sw.all(), writes=[tps])
                P.op('act', lambda e, ps=ps, sub=sub: e.activation(out=vgs[sub].ap, in_=ps[:, 0:512], func=AF.Gelu_apprx_tanh), reads=[tps], writes=vgs[sub].all())
                P.op('act', lambda e, sub=sub: e.activation(out=junk.ap, in_=vgs[sub].ap, func=AF.Square, accum_out=ss1.ap[:, sub:sub + 1]),
                     reads=vgs[sub].all() + ss1.all(), writes=junk.all() + ss1.all())
            P.op('act', lambda e: e.activation(out=ss1.ap[:, 4:8], in_=ss1.ap[:, 0:4], func=AF.Ln, scale=1.0 / 512, bias=EPS), reads=ss1.all(), writes=ss1.all())
            P.op('act', lambda e: e.activation(out=ss1.ap[:, 8:12], in_=ss1.ap[:, 4:8], func=AF.Exp, scale=-0.5), reads=ss1.all(), writes=ss1.all())
            for sub in range(4):
                P.op('dve', lambda e, sub=sub: e.scalar_tensor_tensor(out=vc.ap, in0=vgs[sub].ap, scalar=ss1.ap[:, 8 + sub:9 + sub], in1=self.vcol('gnw'), op0=ALU.mult, op1=ALU.mult),
                     reads=vgs[sub].all() + ss1.all() + [self.tv], writes=vc.all())
                for gi in range(4):
                    sbk, tsb = sbanks[gi]
                    P.op('pe', lambda e, sbk=sbk, gi=gi, sub=sub: e.matmul(sbk[:, sub * 128:(sub + 1) * 128], lhsT=vc.ap[:, gi * 128:(gi + 1) * 128],
                                                                           rhs=self.wsTb[:, gi * 128:(gi + 1) * 128], start=True, stop=False),
                         reads=vc.all() + [P.tok('wsTb')], writes=[tsb])
                    P.op('pe', lambda e, sbk=sbk, gi=gi, sub=sub: e.matmul(sbk[:, sub * 128:(sub + 1) * 128], lhsT=self.onesb[0:1, :],
                                                                           rhs=self.bsrow[0:1, gi * 128:(gi + 1) * 128], start=False, stop=True),
                         reads=[self.tcb, P.tok('bsrow')], writes=[tsb])
            for gi in range(4):
                sbk, tsb = sbanks[gi]
                P.op('dve', lambda e, sbk=sbk, gi=gi: e.tensor_tensor(out=mix.v(4 + gi), in0=sbk[:, 0:512], in1=ub.v(gi), op=ALU.mult),
                     reads=[tsb] + ub.t(gi), writes=mix.t(4 + gi))
            self.release(*sbanks)
            for (s0, sl) in g['seqs']:
                a = max(s0, c0); b = min(s0 + sl, c0 + 512)
                if a >= b:
                    continue
                qa, qb, nq = a - c0, b - c0, b - a
                nkc = (PAST + sl) // 128
                kbase = (0 if sample else s0)
                def head_loop(h, KTb, Vb, prev_done):
                    P.op('sp', lambda e, KTb=KTb, h=h, kbase=kbase, nkc=nkc: e.dma_start(out=KTb.ap[:, 0:nkc * 128], in_=KTd[h, :, kbase:kbase + nkc * 128]),
                         reads=[tK(h, kbase // 128 + i) for i in range(nkc)], writes=KTb.all(), dma=True)
                    P.op('sp', lambda e, Vb=Vb, h=h, kbase=kbase, nkc=nkc: e.dma_start(
                        out=Vb.ap[:, 0:nkc * 128].rearrange("p (c e) -> p c e", e=128), in_=Vd[h, :, kbase // 128:kbase // 128 + nkc, :]),
                         reads=[tV(h, kbase // 128 + i) for i in range(nkc)], writes=Vb.all(), dma=True)
                    O = [self.bank(hold=True), self.bank(hold=True)]
                    D0 = self.bank(hold=True)

                    def issue_st(kc):
                        for m in range(2):
                            st_, tst = self.bank()
                            P.op('pe', lambda e, st_=st_, m=m: e.matmul(
                                st_[:, 0:nq], lhsT=KTb.ap[m * 64:(m + 1) * 64, kc * 128:(kc + 1) * 128], rhs=QT.ap[m * 64:(m + 1) * 64, h * 512 + qa:h * 512 + qb],
                                start=True, stop=True), reads=KTb.all() + QT.t(h), writes=[tst])
                            Eb = E[m][kc % NE]
                            P.op('act', lambda e, st_=st_, Eb=Eb: e.activation(out=Eb.ap[:, 0:nq], in_=st_[:, 0:nq], func=AF.Exp, scale=0.125),
                                 reads=[tst], writes=Eb.all())

                    def issue_pv(kc):
                        for m in range(2):
                            Eb = E[m][kc % NE]
                            P.op('pe', lambda e, m=m, Eb=Eb: e.matmul(O[m][0][:, 0:nq], lhsT=Vb.v(kc), rhs=Eb.ap[:, 0:nq],
                                                                     start=(kc == 0), stop=(kc == nkc - 1)),
                                 reads=Vb.all() + Eb.all(), writes=[O[m][1]])
                            if m == 0:
                                P.op('pe', lambda e, Eb=Eb: e.matmul(D0[0][:, 0:nq], lhsT=self.onesb, rhs=Eb.ap[:, 0:nq], start=(kc == 0), stop=(kc == nkc - 1)),
                                     reads=Eb.all() + [self.tcb], writes=[D0[1]])
                            elif kc == 0:
                                P.op('dve', lambda e, m=m, Eb=Eb: e.tensor_copy(out=acc[m].ap[:, 0:nq], in_=Eb.ap[:, 0:nq]), reads=Eb.all(), writes=acc[m].all())
                            else:
                                P.op('dve', lambda e, m=m, Eb=Eb: e.tensor_tensor(out=acc[m].ap[:, 0:nq], in0=acc[m].ap[:, 0:nq], in1=Eb.ap[:, 0:nq], op=ALU.add),
                                     reads=Eb.all() + acc[m].all(), writes=acc[m].all())

                    LA = 2
                    for kc in range(min(LA, nkc)):
                        issue_st(kc)
                    for kc in range(nkc):
                        if kc + LA < nkc:
                            issue_st(kc + LA)
                        issue_pv(kc)
                        yield
                    while not prev_done[0]:
                        yield
                    P.op('dve', lambda e: e.tensor_copy(out=a0b.ap[:, 0:nq], in_=O[0][0][:, 0:nq]), reads=[O[0][1]], writes=a0b.all())
                    P.op('dve', lambda e: e.tensor_copy(out=a1b.ap[:, 0:nq], in_=O[1][0][:, 0:nq]), reads=[O[1][1]], writes=a1b.all())
                    P.op('act', lambda e: e.activation(out=r0b.ap[:, 0:nq], in_=D0[0][:, 0:nq], func=AF.Ln), reads=[D0[1]], writes=r0b.all())
                    self.release(*O)
                    self.release(D0)
                    D1 = self.bank(hold=True)
                    P.op('pe', lambda e: e.matmul(D1[0][:, 0:nq], lhsT=self.ccol('ones'), rhs=acc[1].ap[:, 0:nq], start=True, stop=True),
                         reads=acc[1].all() + [self.tc], writes=[D1[1]])
                    P.op('act', lambda e: e.activation(out=r1b.ap[:, 0:nq], in_=D1[0][:, 0:nq], func=AF.Ln), reads=[D1[1]], writes=r1b.all())
                    self.release(D1)
                def head_epi(h, done):
                    P.op('act', lambda e: e.activation(out=r0b.ap[:, 0:nq], in_=r0b.ap[:, 0:nq], func=AF.Exp, scale=-1.0), reads=r0b.all(), writes=r0b.all())
                    yield
                    P.op('dve', lambda e: e.tensor_tensor(out=a0b.ap[:, 0:nq], in0=a0b.ap[:, 0:nq], in1=r0b.ap[:, 0:nq], op=ALU.mult),
                         reads=a0b.all() + r0b.all(), writes=a0b.all())
                    yield
                    P.op('act', lambda e: e.activation(out=r1b.ap[:, 0:nq], in_=r1b.ap[:, 0:nq], func=AF.Exp, scale=-1.0), reads=r1b.all(), writes=r1b.all())
                    yield
                    P.op('dve', lambda e: e.tensor_tensor(out=a1b.ap[:, 0:nq], in0=a1b.ap[:, 0:nq], in1=r1b.ap[:, 0:nq], op=ALU.mult),
                         reads=a1b.all() + r1b.all(), writes=a1b.all())
                    yield
                    P.op('dve', lambda e: e.scalar_tensor_tensor(out=a0b.ap[:, 0:nq], in0=a1b.ap[:, 0:nq], scalar=self.small[:, 0:1], in1=a0b.ap[:, 0:nq],
                                                                 op0=ALU.mult, op1=ALU.add), reads=a0b.all() + a1b.all() + [self.tsmall], writes=a0b.all())
                    yield
                    P.op('act', lambda e: e.activation(out=asq.ap[:, 0:nq], in_=a0b.ap[:, 0:nq], func=AF.Square), reads=a0b.all(), writes=asq.all())
                    yield
                    ps, tps = self.bank()
                    P.op('pe', lambda e, ps=ps: e.matmul(ps[:, 0:nq], lhsT=self.onesb, rhs=asq.ap[:, 0:nq], start=True, stop=True),
                         reads=asq.all() + [self.tcb], writes=[tps])
                    P.op('act', lambda e, ps=ps: e.activation(out=r0b.ap[:, 0:nq], in_=ps[:, 0:nq], func=AF.Ln, scale=1.0 / 128, bias=EPS), reads=[tps], writes=r0b.all())
                    yield
                    P.op('act', lambda e: e.activation(out=r0b.ap[:, 0:nq], in_=r0b.ap[:, 0:nq], func=AF.Exp, scale=-0.5), reads=r0b.all(), writes=r0b.all())
                    yield
                    P.op('dve', lambda e, h=h: e.scalar_tensor_tensor(out=mix.v(h, qa, qb), in0=a0b.ap[:, 0:nq], scalar=self.small[:, 1:2], in1=r0b.ap[:, 0:nq],
                                                                      op0=ALU.mult, op1=ALU.mult), reads=a0b.all() + r0b.all() + [self.tsmall], writes=mix.t(h))
                    yield
                    done[0] = True

                def run_rr2(gens):
                    gens = list(gens)
                    while gens:
                        for gn in list(gens):
                            try:
                                next(gn)
                            except StopIteration:
                                gens.remove(gn)

                prev_done = [True]
                pend = None
                for h in range(4):
                    KTb = KTs[kv_rr % 2]; Vb = Vs[kv_rr % 2]; kv_rr += 1
                    thr = [head_loop(h, KTb, Vb, prev_done)]
                    if pend is not None:
                        thr.append(pend)
                    run_rr2(thr)
                    prev_done = [False]
                    pend = head_epi(h, prev_done)
                run_rr2([pend])
            for o in range(8):
                s = self.wload(d['w_out0c'][o], 1024)
                ps, tps = self.fm_proj(s, 128, lambda kc: mix.v(kc), lambda kc: mix.t(kc), 512)
                P.op('dve', lambda e, ps=ps, o=o, c0=c0: e.scalar_tensor_tensor(out=X.v(o, c0, c0 + 512), in0=ps[:, 0:512], scalar=G1(o), in1=X.v(o, c0, c0 + 512),
                                                                                op0=ALU.mult, op1=ALU.add),
                     reads=[tps, self.tAB] + X.t(o, c0, c0 + 512), writes=X.t(o, c0, c0 + 512))

    def ffn(self, g, l):
        P, d = self.P, self.d
        X = g['X']
        cond = g['cond']
        A2 = lambda kc: self.AB_(l, cond, 3, kc)
        B2 = lambda kc: self.AB_(l, cond, 4, kc)
        G2 = lambda kc: self.AB_(l, cond, 5, kc)
        self.reset_pages()
        H2 = [self.page_buf(8, 512, BF16) for _ in range(2)]
        ms = self.mod_scratch()
        act = self.page_buf(22, 512, BF16)
        c1 = self.page_buf(1, 512, F32); c2 = self.page_buf(1, 512, F32); gl = self.page_buf(1, 512, F32)
        tiles = []
        for (s0, sl) in g['seqs']:
            nt = (sl + 509) // 510
            W = ((sl + nt - 1) // nt + 1) // 2 * 2
            t0 = 0
            while t0 < sl:
                w = min(W, sl - t0)
                tiles.append((s0, sl, t0, w))
                t0 += w
        fw = lambda k, j: self.vcol('fcw%d' % l, k * 22 + j, k * 22 + j + 1)
        fb = lambda j: self.vcol('fcb%d' % l, j, j + 1)

        def make_h2(idx):
            s0, sl, t0, w = tiles[idx]
            hb = H2[idx % 2]
            lo = max(t0 - 1, 0); hi = min(t0 + w + 1, sl)
            off = lo - (t0 - 1)
            if off > 0:
                P.op('dve', lambda e: e.memset(hb.v3(0, 8, 0, 1), 0.0), writes=hb.t3(0, 8, 0, 1))
            if hi < t0 + w + 1:
                P.op('dve', lambda e: e.memset(hb.v3(0, 8, w + 1, w + 2), 0.0), writes=hb.t3(0, 8, w + 1, w + 2))
            self.modulate(g, s0 + lo, hi - lo, A2, B2, hb, off, ms)

        paired = (len(tiles) == 2 and all(t[3] <= 256 for t in tiles))
        groups = [[0, 1]] if paired else [[i] for i in range(len(tiles))]
        for i in groups[0]:
            make_h2(i)
        for gi, grp_ in enumerate(groups):
            if gi + 1 < len(groups):
                for i in groups[gi + 1]:
                    make_h2(i)
            for j in range(22):
                sa = self.wload(d['w_upc'][l * 44 + j], 1024)
                sg = self.wload(d['w_upc'][l * 44 + 22 + j], 1024)
                for idx in grp_:
                    s0, sl, t0, w = tiles[idx]
                    hb = H2[idx % 2]
                    off = (idx - grp_[0]) * 256
                    pa, tpa = self.fm_proj(sa, 128, lambda kc: hb.v(kc, 0, w + 2), lambda kc: hb.t(kc), w + 2)
                    pg, tpg = self.fm_proj(sg, 128, lambda kc: hb.v(kc, 1, w + 1), lambda kc: hb.t(kc), w)
                    P.op('dve', lambda e: e.tensor_scalar(out=c1.ap[:, 0:w], in0=pa[:, 0:w], scalar1=fw(0, j), scalar2=None, op0=ALU.mult),
                         reads=[tpa, self.tv], writes=c1.all())
                    P.op('dve', lambda e: e.scalar_tensor_tensor(out=c2.ap[:, 0:w], in0=pa[:, 1:w + 1], scalar=fw(1, j), in1=c1.ap[:, 0:w], op0=ALU.mult, op1=ALU.add),
                         reads=[tpa, self.tv] + c1.all(), writes=c2.all())
                    P.op('dve', lambda e: e.scalar_tensor_tensor(out=c1.ap[:, 0:w], in0=pa[:, 2:w + 2], scalar=fw(2, j), in1=c2.ap[:, 0:w], op0=ALU.mult, op1=ALU.add),
                         reads=[tpa, self.tv] + c2.all(), writes=c1.all())
                    P.op('act', lambda e: e.activation(out=gl.ap[:, 0:w], in_=c1.ap[:, 0:w], func=AF.Gelu_apprx_tanh, bias=fb(j)),
                         reads=c1.all() + [self.tv], writes=gl.all())
                    P.op('dve', lambda e: e.tensor_tensor(out=act.v(j, off, off + w), in0=pg[:, 0:w], in1=gl.ap[:, 0:w], op=ALU.mult),
                         reads=[tpg] + gl.all(), writes=act.t(j))
            for o in range(8):
                pieces = []
                for (k0, k1) in ((0, 8), (8, 16), (16, 22)):
                    pieces.append((k0, k1, self.wload(d['w_downc'][l * 8 + o][:, k0 * 128:k1 * 128], (k1 - k0) * 128)))
                for idx in grp_:
                    s0, sl, t0, w = tiles[idx]
                    off = (idx - grp_[0]) * 256
                    ps, tps = self.bank()
                    for (k0, k1, sp_) in pieces:
                        for kk in range(k0, k1):
                            P.op('pe', lambda e: e.matmul(ps[:, 0:w], lhsT=sp_.ap[:, (kk - k0) * 128:(kk - k0 + 1) * 128], rhs=act.v(kk, off, off + w),
                                                          start=(kk == 0), stop=(kk == 21)), reads=sp_.all() + act.t(kk), writes=[tps])
                    ca, cb_ = s0 + t0, s0 + t0 + w
                    P.op('dve', lambda e: e.scalar_tensor_tensor(out=X.v(o, ca, cb_), in0=ps[:, 0:w], scalar=G2(o), in1=X.v(o, ca, cb_),
                                                                 op0=ALU.mult, op1=ALU.add),
                         reads=[tps, self.tAB] + X.t(o, ca, cb_), writes=X.t(o, ca, cb_))

    def layer1(self, g):
        P, d = self.P, self.d
        X = g['X']
        sample = g['sample']
        cond = g['cond']
        A1 = lambda kc: self.AB_(1, cond, 0, kc)
        B1 = lambda kc: self.AB_(1, cond, 1, kc)
        G1 = lambda kc: self.AB_(1, cond, 2, kc)
        self.reset_pages()
        NW = 258
        H1 = [self.page_buf(8, NW, BF16) for _ in range(2)]
        ms = self.mod_scratch(NW)
        xm = self.page_buf(8, NW, BF16)
        xc = self.page_buf(8, 256, BF16)
        skx = self.page_buf(8, 256, BF16)
        sgm = self.page_buf(8, 256, BF16)
        qT = self.page_buf(8, 256, BF16)
        kT = self.page_buf(8, 256, BF16)
        ktok = self.page_buf(8, 256, BF16)
        vaug = self.page_buf(8, 257, BF16)
        Gt = self.page_buf(2, 16, F32)
        C = self.page_buf(8, 257, F32)
        Cb = self.page_buf(8, 257, BF16)
        gms = [self.page_buf(1, 96, F32) for _ in range(2)]
        hsm = [self.page_buf(1, 8, F32) for _ in range(4)]
        mst = [self.page_buf(1, 4, F32) for _ in range(2)]
        dg = self.page_buf(4, 128, F32)
        DTs = [self.page_buf(4, 128, F32) for _ in range(2)]
        sTws = [self.page_buf(1, 128, BF16) for _ in range(4)]
        nds = [self.page_buf(1, 257, F32) for _ in range(4)]
        kws = [self.page_buf(1, 256, BF16) for _ in range(4)]
        hst = [self.page_buf(1, 1024, BF16) for _ in range(2)]
        hnb = self.page_buf(1, 1024, BF16)
        junks = [self.page_buf(1, 256, F32) for _ in range(2)]
        cc1, cc2 = junks
        tt1 = self.page_buf(1, 1024, BF16)
        ybuf = self.page_buf(8, 256, BF16)
        Hfd = d['Hfd']
        tH = lambda ch: P.tok('Hfd', ch)
        mcw = lambda k, c: self.vcol('mcw', k * 8 + c, k * 8 + c + 1)
        tiles = []
        for si, (s0, sl) in enumerate(g['seqs']):
            t0 = 0
            while t0 < sl:
                n = min(256, sl - t0)
                tiles.append((si, s0, sl, t0, n))
                t0 += n

        def make_h1(idx):
            si, s0, sl, t0, n = tiles[idx]
            hb = H1[idx % 2]
            lo = max(t0 - 1, 0); hi = min(t0 + n + 1, sl)
            off = lo - (t0 - 1)
            if off > 0:
                P.op('dve', lambda e: e.memset(hb.v3(0, 8, 0, 1), 0.0), writes=hb.t3(0, 8, 0, 1))
            if hi < t0 + n + 1:
                P.op('dve', lambda e: e.memset(hb.v3(0, 8, n + 1, n + 2), 0.0), writes=hb.t3(0, 8, n + 1, n + 2))
            self.modulate(g, s0 + lo, hi - lo, A1, B1, hb, off, ms)

        P.op('dve', lambda e: e.memset(vaug.ap, 1.0), writes=vaug.all())

        for dr in range(2):
            order = list(range(len(tiles))) if dr == 0 else list(range(len(tiles) - 1, -1, -1))
            U = self.ccol('Uf' if dr == 0 else 'Ub')
            nm = self.cb[:, (3 if dr == 0 else 4) * 128:(4 if dr == 0 else 5) * 128]
            pm = self.cb[:, (5 if dr == 0 else 6) * 128:(6 if dr == 0 else 7) * 128]
            sel = self.ccol('self' if dr == 0 else 'selb')
            mcur = 0
            make_h1(order[0])
            for oi, idx in enumerate(order):
                si, s0, sl, t0, n = tiles[idx]
                hb = H1[idx % 2]
                nsub = n // 128
                first_of_seq = (t0 == 0) if dr == 0 else (t0 + n == sl)
                last_of_seq = (t0 + n == sl) if dr == 0 else (t0 == 0)
                if oi + 1 < len(order):
                    make_h1(order[oi + 1])
                if first_of_seq:
                    if sample:
                        for h in range(4):
                            for dk in range(2):
                                P.op('sp', lambda e, h=h, dk=dk: e.dma_start(out=C.v(h * 2 + dk, 0, 256), in_=d['sC'][dr * 4 + h, dk, :, :]),
                                     writes=C.t(h * 2 + dk), dma=True)
                        for h in range(4):
                            for dk in range(2):
                                P.op('sp', lambda e, h=h, dk=dk: e.dma_start(out=C.v(h * 2 + dk, 256, 257), in_=d['sn'][:, (dr * 4 + h) * 2 + dk:(dr * 4 + h) * 2 + dk + 1], allow_slow_non_contiguous=True),
                                     writes=C.t(h * 2 + dk), dma=True)
                        P.op('sp', lambda e, mcur=mcur: e.dma_start(out=mst[mcur].ap, in_=d['sm'][:, dr * 4:dr * 4 + 4], allow_slow_non_contiguous=True), writes=mst[mcur].all(), dma=True)
                    else:
                        P.op('dve', lambda e: e.memset(C.ap, 0.0), writes=C.all())
                        P.op('dve', lambda e, mcur=mcur: e.memset(mst[mcur].ap, 0.0), writes=mst[mcur].all())
                    P.op('act', lambda e: e.activation(out=Cb.ap, in_=C.ap, func=AF.Copy), reads=C.all(), writes=Cb.all())
                subs = list(range(nsub)) if dr == 0 else list(range(nsub - 1, -1, -1))

                def gate_thread(sub, gmb, mc, mn):
                    gma = gmb.all()
                    G = lambda a, b: gmb.ap[:, a:b]
                    li = Gt.v(sub, dr * 8, dr * 8 + 4)
                    fg = Gt.v(sub, dr * 8 + 4, dr * 8 + 8)
                    DTb = DTs[sub % 2]
                    P.op('act', lambda e: e.activation(out=G(0, 4), in_=fg, func=AF.Exp, scale=-1.0), reads=Gt.t(sub), writes=gma)
                    P.op('act', lambda e: e.activation(out=G(0, 4), in_=G(0, 4), func=AF.Ln, bias=1.0), reads=gma, writes=gma)
                    yield
                    pcp, tpcp = (yield from self.bank_wait())
                    P.op('pe', lambda e: e.matmul(pcp[:, 0:4], lhsT=U, rhs=G(0, 4), start=True, stop=True), reads=gma + [self.tc], writes=[tpcp])
                    yield
                    P.op('dve', lambda e: e.tensor_copy(out=G(72, 76), in_=pcp[:, 0:4]), reads=[tpcp] + gma, writes=gma)
                    self.release((pcp, tpcp))
                    P.op('dve', lambda e: e.tensor_tensor(out=G(4, 8), in0=G(72, 76), in1=li, op=ALU.add), reads=Gt.t(sub) + gma, writes=gma)
                    P.op('dve', lambda e: e.tensor_tensor(out=dg.v3(0, 4), in0=self.ccol('ident').unsqueeze(1).to_broadcast([128, 4, 128]),
                                                          in1=G(4, 8).unsqueeze(2).to_broadcast([128, 4, 128]), op=ALU.mult), reads=gma + [self.tc], writes=dg.all())
                    yield
                    pcb, tpcb = (yield from self.bank_wait())
                    for h in range(4):
                        P.op('pe', lambda e, h=h: e.matmul(pcb[:, h * 128:(h + 1) * 128], lhsT=self.ccol('ones'), rhs=dg.v(h), start=True, stop=False),
                             reads=dg.all() + [self.tc], writes=[tpcb])
                        P.op('pe', lambda e, h=h: e.matmul(pcb[:, h * 128:(h + 1) * 128], lhsT=self.identb, rhs=nm, start=False, stop=True),
                             reads=[self.tcb, P.tok('cbm')], writes=[tpcb])
                    yield
                    P.op('dve', lambda e: e.tensor_reduce(out=G(8, 12), in_=pcb[:, 0:512].rearrange("p (h s) -> p h s", s=128), axis=AX.X, op=ALU.max),
                         reads=[tpcb] + gma, writes=gma)
                    self.release((pcb, tpcb))
                    P.op('dve', lambda e: e.tensor_tensor(out=G(12, 16), in0=G(8, 12), in1=mc.ap, op=ALU.max), reads=gma + mc.all(), writes=gma)
                    P.op('dve', lambda e: e.tensor_tensor(out=G(16, 20), in0=G(12, 16), in1=G(72, 76), op=ALU.subtract), reads=gma, writes=gma)
                    yield
                    psel, tpsel = (yield from self.bank_wait())
                    P.op('pe', lambda e: e.matmul(psel[:, 0:8], lhsT=sel, rhs=G(12, 20), start=True, stop=True), reads=gma + [self.tc], writes=[tpsel])
                    yield
                    P.op('act', lambda e: e.activation(out=mn.ap, in_=psel[:, 4:8], func=AF.Copy), reads=[tpsel], writes=mn.all())
                    P.op('act', lambda e: e.activation(out=G(52, 56), in_=psel[:, 0:4], func=AF.Copy), reads=[tpsel] + gma, writes=gma)
                    self.release((psel, tpsel))
                    yield
                    P.op('dve', lambda e: e.tensor_tensor(out=G(20, 24), in0=mc.ap, in1=G(12, 16), op=ALU.subtract), reads=gma + mc.all(), writes=gma)
                    P.op('dve', lambda e: e.tensor_scalar(out=G(24, 28), in0=G(16, 20), scalar1=-1.0, scalar2=None, op0=ALU.mult), reads=gma, writes=gma)
                    P.op('dve', lambda e: e.tensor_tensor(out=G(28, 32), in0=mc.ap, in1=G(52, 56), op=ALU.subtract), reads=gma + mc.all(), writes=gma)
                    P.op('dve', lambda e: e.tensor_tensor(out=G(32, 36), in0=G(4, 8), in1=G(52, 56), op=ALU.subtract), reads=gma, writes=gma)
                    yield
                    P.op('act', lambda e: e.activation(out=G(36, 52), in_=G(20, 36), func=AF.Exp), reads=gma, writes=gma)
                    P.op('dve', lambda e: e.tensor_tensor(out=dg.v3(0, 4), in0=self.ccol('ident').unsqueeze(1).to_broadcast([128, 4, 128]),
                                                          in1=G(12, 16).unsqueeze(2).to_broadcast([128, 4, 128]), op=ALU.mult), reads=gma + [self.tc], writes=dg.all())
                    yield
                    pmb_, tpmb = (yield from self.bank_wait())
                    for h in range(4):
                        P.op('pe', lambda e, h=h: e.matmul(pmb_[:, h * 128:(h + 1) * 128], lhsT=self.ccol('ones'), rhs=dg.v(h), start=True, stop=False),
                             reads=dg.all() + [self.tc], writes=[tpmb])
                        P.op('pe', lambda e, h=h: e.matmul(pmb_[:, h * 128:(h + 1) * 128], lhsT=self.identb, rhs=pm, start=False, stop=True),
                             reads=[self.tcb, P.tok('cbm')], writes=[tpmb])
                    yield
                    for h in range(4):
                        P.op('act', lambda e, h=h: e.activation(out=DTb.v(h), in_=pmb_[:, h * 128:(h + 1) * 128], func=AF.Exp, scale=-1.0, bias=G(4 + h, 5 + h)),
                             reads=[tpmb] + gma, writes=DTb.t(h))
                        if h == 1:
                            yield
                    self.release((pmb_, tpmb))

                def head_thread(sub, h, gmb, hfb):
                    gma = gmb.all()
                    G = lambda a, b: gmb.ap[:, a:b]
                    WI = G(36 + h, 37 + h); EM = G(40 + h, 41 + h); DEC = G(44 + h, 45 + h); WS = G(48 + h, 49 + h)
                    DTb = DTs[sub % 2]
                    hm = hsm[h]
                    H_ = lambda a, b: hm.ap[:, a:b]
                    ndh = nds[h]
                    cs0, cs1 = sub * 128, (sub + 1) * 128
                    pst, tpst = (yield from self.bank_wait())
                    for dk in range(2):
                        P.op('pe', lambda e, dk=dk: e.matmul(pst[:, 0:128], lhsT=kT.v(h * 2 + dk, cs0, cs1), rhs=qT.v(h * 2 + dk, cs0, cs1),
                                                             start=(dk == 0), stop=(dk == 1)), reads=kT.t(h * 2 + dk) + qT.t(h * 2 + dk), writes=[tpst])
                    pB, tpB = (yield from self.bank_wait())
                    for dk in range(2):
                        P.op('pe', lambda e, dk=dk: e.matmul(pB[:, 0:257], lhsT=qT.v(h * 2 + dk, cs0, cs1), rhs=Cb.v(h * 2 + dk),
                                                             start=(dk == 0), stop=(dk == 1)), reads=qT.t(h * 2 + dk) + Cb.t(h * 2 + dk), writes=[tpB])
                    yield
                    P.op('dve', lambda e: e.tensor_tensor(out=sTws[h].ap, in0=pst[:, 0:128], in1=DTb.v(h), op=ALU.mult), reads=[tpst] + DTb.t(h), writes=sTws[h].all())
                    self.release((pst, tpst))
                    P.op('act', lambda e: e.activation(out=ndh.ap, in_=pB[:, 0:257], func=AF.Identity, scale=WI), reads=[tpB] + gma, writes=ndh.all())
                    self.release((pB, tpB))
                    P.op('dve', lambda e: e.tensor_scalar(out=kws[h].ap, in0=ktok.v(sub * 4 + h), scalar1=WS, scalar2=None, op0=ALU.mult),
                         reads=ktok.t(sub * 4 + h) + gma, writes=kws[h].all())
                    yield
                    pA, tpA = (yield from self.bank_wait())
                    P.op('pe', lambda e: e.matmul(pA[:, 0:257], lhsT=sTws[h].ap, rhs=vaug.v(sub * 4 + h), start=True, stop=True),
                         reads=sTws[h].all() + vaug.t(sub * 4 + h), writes=[tpA])
                    pCs = []
                    for dk in range(2):
                        pC, tpC = (yield from self.bank_wait())
                        pCs.append((pC, tpC))
                        P.op('pe', lambda e, pC=pC, dk=dk: e.matmul(pC[:, 0:257], lhsT=kws[h].ap[:, dk * 128:(dk + 1) * 128], rhs=vaug.v(sub * 4 + h), start=True, stop=True),
                             reads=kws[h].all() + vaug.t(sub * 4 + h), writes=[tpC])
                    yield
                    P.op('dve', lambda e: e.tensor_tensor(out=ndh.ap, in0=pA[:, 0:257], in1=ndh.ap, op=ALU.add), reads=[tpA] + ndh.all(), writes=ndh.all())
                    self.release((pA, tpA))
                    for dk in range(2):
                        pC, tpC = pCs[dk]
                        P.op('dve', lambda e, pC=pC, dk=dk: e.scalar_tensor_tensor(out=C.v(h * 2 + dk), in0=C.v(h * 2 + dk), scalar=DEC, in1=pC[:, 0:257], op0=ALU.mult, op1=ALU.add),
                             reads=[tpC] + gma + C.t(h * 2 + dk), writes=C.t(h * 2 + dk))
                    self.release(*pCs)
                    yield
                    P.op('act', lambda e: e.activation(out=H_(0, 1), in_=ndh.ap[:, 256:257], func=AF.Abs), reads=ndh.all() + hm.all(), writes=hm.all())
                    P.op('act', lambda e: e.activation(out=Cb.v3(h * 2, h * 2 + 2), in_=C.v3(h * 2, h * 2 + 2), func=AF.Copy), reads=C.t3(h * 2, h * 2 + 2), writes=Cb.t3(h * 2, h * 2 + 2))
                    yield
                    P.op('dve', lambda e: e.tensor_tensor(out=H_(1, 2), in0=H_(0, 1), in1=EM, op=ALU.max), reads=gma + hm.all(), writes=hm.all())
                    P.op('dve', lambda e: e.reciprocal(out=H_(2, 3), in_=H_(1, 2)), reads=hm.all(), writes=hm.all())
                    yield
                    if dr == 0:
                        P.op('dve', lambda e: e.tensor_scalar(out=hfb.ap[:, h * 256:(h + 1) * 256], in0=ndh.ap[:, 0:256], scalar1=H_(2, 3), scalar2=None, op0=ALU.mult),
                             reads=ndh.all() + hm.all(), writes=hfb.all())
                    else:
                        P.op('dve', lambda e: e.scalar_tensor_tensor(out=ndh.ap[:, 0:256], in0=ndh.ap[:, 0:256], scalar=H_(2, 3), in1=hfb.ap[:, h * 256:(h + 1) * 256],
                                                                     op0=ALU.mult, op1=ALU.add), reads=ndh.all() + hm.all() + hfb.all(), writes=ndh.all())
                        P.op('dve', lambda e: e.memset(H_(3, 4), 0.0), reads=hm.all(), writes=hm.all())
                        yield
                        P.op('act', lambda e: e.activation(out=junks[h % 2].ap, in_=ndh.ap[:, 0:256], func=AF.Square, accum_out=H_(3, 4)),
                             reads=ndh.all() + hm.all(), writes=junks[h % 2].all() + hm.all())
                        P.op('act', lambda e: e.activation(out=H_(4, 5), in_=H_(3, 4), func=AF.Ln, scale=1.0 / 256, bias=EPS), reads=hm.all(), writes=hm.all())
                        P.op('act', lambda e: e.activation(out=H_(5, 6), in_=H_(4, 5), func=AF.Exp, scale=-0.5), reads=hm.all(), writes=hm.all())
                        yield
                        P.op('dve', lambda e: e.tensor_scalar(out=hnb.ap[:, h * 256:(h + 1) * 256], in0=ndh.ap[:, 0:256], scalar1=H_(5, 6), scalar2=None, op0=ALU.mult),
                             reads=ndh.all() + hm.all(), writes=hnb.all())

                def run_rr(gens):
                    gens = list(gens)
                    while gens:
                        for gn in list(gens):
                            try:
                                next(gn)
                            except StopIteration:
                                gens.remove(gn)

                for sub in range(nsub):
                    ps, tps = self.bank()
                    for kc in range(8):
                        P.op('pe', lambda e, ps=ps, kc=kc, sub=sub: e.matmul(ps[:, 0:16], lhsT=hb.v(kc, 1 + sub * 128, 1 + (sub + 1) * 128), rhs=self.wgb[:, kc * 16:(kc + 1) * 16],
                                                                             start=(kc == 0), stop=(kc == 7)), reads=hb.t(kc) + [P.tok('wgb')], writes=[tps])
                    P.op('dve', lambda e, ps=ps, sub=sub: e.tensor_tensor(out=Gt.v(sub), in0=ps[:, 0:16], in1=self.vcol('bg'), op=ALU.add),
                         reads=[tps, self.tv], writes=Gt.t(sub))
                def proj_thread():
                    for c in range(8):
                        s = self.wload(d['w_in1c'][c], 1024)
                        ps, tps = self.fm_proj(s, 128, lambda kc: hb.v(kc, 0, n + 2), lambda kc: hb.t(kc), n + 2)
                        P.op('act', lambda e, ps=ps, c=c: e.activation(out=xm.v(c, 0, n + 2), in_=ps[:, 0:n + 2], func=AF.Copy), reads=[tps], writes=xm.t(c))
                        P.op('dve', lambda e, ps=ps, c=c: e.tensor_scalar(out=cc1.ap[:, 0:n], in0=ps[:, 0:n], scalar1=mcw(0, c), scalar2=None, op0=ALU.mult),
                             reads=[tps, self.tv], writes=cc1.all())
                        P.op('dve', lambda e, ps=ps, c=c: e.scalar_tensor_tensor(out=cc2.ap[:, 0:n], in0=ps[:, 1:n + 1], scalar=mcw(1, c), in1=cc1.ap[:, 0:n], op0=ALU.mult, op1=ALU.add),
                             reads=[tps, self.tv] + cc1.all(), writes=cc2.all())
                        P.op('dve', lambda e, ps=ps, c=c: e.scalar_tensor_tensor(out=cc1.ap[:, 0:n], in0=ps[:, 2:n + 2], scalar=mcw(2, c), in1=cc2.ap[:, 0:n], op0=ALU.mult, op1=ALU.add),
                             reads=[tps, self.tv] + cc2.all(), writes=cc1.all())
                        P.op('act', lambda e, c=c: e.activation(out=xc.v(c, 0, n), in_=cc1.ap[:, 0:n], func=AF.Silu, bias=self.vcol('mcb', c, c + 1)),
                             reads=cc1.all() + [self.tv], writes=xc.t(c))
                        yield
                        if dr == 1:
                            P.op('dve', lambda e, c=c: e.tensor_scalar(out=skx.v(c, 0, n), in0=xc.v(c, 0, n), scalar1=self.vcol('skw', c, c + 1), scalar2=None, op0=ALU.mult),
                                 reads=xc.t(c) + [self.tv], writes=skx.t(c))
                    if dr == 1:
                        for c in range(8):
                            s = self.wload(d['w_in1c'][8 + c], 1024)
                            ps, tps = self.fm_proj(s, 128, lambda kc: hb.v(kc, 1, n + 1), lambda kc: hb.t(kc), n)
                            P.op('act', lambda e, ps=ps, c=c: e.activation(out=sgm.v(c, 0, n), in_=ps[:, 0:n], func=AF.Sigmoid), reads=[tps], writes=sgm.t(c))
                            yield
                    phase2[0] = True
                    for h in range(4):
                        for dc in range(2):
                            s = self.wload(d['wqc'][h * 2 + dc], 256)
                            ps, tps = self.fm_proj(s, 128, lambda kc: xc.v(h * 2 + kc, 0, n), lambda kc: xc.t(h * 2 + kc), n, nk=2)
                            P.op('act', lambda e, ps=ps, h=h, dc=dc: e.activation(out=qT.v(h * 2 + dc, 0, n), in_=ps[:, 0:n], func=AF.Copy), reads=[tps], writes=qT.t(h * 2 + dc))
                            s = self.wload(d['wkc'][h * 2 + dc], 256)
                            ps, tps = self.fm_proj(s, 128, lambda kc: xc.v(h * 2 + kc, 0, n), lambda kc: xc.t(h * 2 + kc), n, nk=2)
                            P.op('act', lambda e, ps=ps, h=h, dc=dc: e.activation(out=kT.v(h * 2 + dc, 0, n), in_=ps[:, 0:n], func=AF.Identity, scale=1.0 / 16), reads=[tps], writes=kT.t(h * 2 + dc))
                            yield
                        sk = self.wload(d['wkw'][h], 512)
                        sv = self.wload(d['wvw'][h], 512)
                        for sub in range(nsub):
                            ps, tps = self.bank()
                            for kk in range(2):
                                P.op('pe', lambda e, ps=ps, kk=kk, sub=sub, h=h, sk=sk: e.matmul(ps[:, 0:256], lhsT=xc.v(h * 2 + kk, sub * 128, (sub + 1) * 128), rhs=sk.ap[:, kk * 256:(kk + 1) * 256],
                                                                                                 start=(kk == 0), stop=(kk == 1)), reads=xc.t(h * 2 + kk) + sk.all(), writes=[tps])
                            P.op('act', lambda e, ps=ps, sub=sub, h=h: e.activation(out=ktok.v(sub * 4 + h), in_=ps[:, 0:256], func=AF.Identity, scale=1.0 / 16), reads=[tps], writes=ktok.t(sub * 4 + h))
                            ps, tps = self.bank()
                            for kk in range(2):
                                P.op('pe', lambda e, ps=ps, kk=kk, sub=sub, h=h, sv=sv: e.matmul(ps[:, 0:256], lhsT=xm.v(h * 2 + kk, 1 + sub * 128, 1 + (sub + 1) * 128), rhs=sv.ap[:, kk * 256:(kk + 1) * 256],
                                                                                                 start=(kk == 0), stop=(kk == 1)), reads=xm.t(h * 2 + kk) + sv.all(), writes=[tps])
                            P.op('act', lambda e, ps=ps, sub=sub, h=h: e.activation(out=vaug.v(sub * 4 + h, 0, 256), in_=ps[:, 0:256], func=AF.Copy), reads=[tps], writes=vaug.t(sub * 4 + h))
                            yield

                first_sub = subs[0]
                phase2 = [False]

                def gated_gate_thread():
                    while not phase2[0]:
                        yield
                    yield from gate_thread(first_sub, gms[first_sub % 2], mst[mcur], mst[1 - mcur])

                run_rr([proj_thread(), gated_gate_thread()])
                for si_, sub in enumerate(subs):
                    ch = (s0 + t0) // 128 + sub
                    mc = mst[mcur]; mn = mst[1 - mcur]
                    gmb = gms[sub % 2]
                    hfb = hst[sub % 2]
                    if dr == 1:
                        P.op('sp', lambda e, hfb=hfb, ch=ch: e.dma_start(out=hfb.ap, in_=Hfd[ch, :, :]), reads=[tH(ch)], writes=hfb.all(), dma=True)
                    threads = [head_thread(sub, h, gmb, hfb) for h in range(4)]
                    mcur = 1 - mcur
                    if si_ + 1 < len(subs):
                        nsb = subs[si_ + 1]
                        threads.append(gate_thread(nsb, gms[nsb % 2], mst[mcur], mst[1 - mcur]))
                    run_rr(threads)
                    cs0, cs1 = sub * 128, (sub + 1) * 128
                    if dr == 0:
                        P.op('sp', lambda e, hfb=hfb, ch=ch: e.dma_start(out=Hfd[ch, :, :], in_=hfb.ap), reads=hfb.all(), writes=[tH(ch)], dma=True)
                    else:
                        pT, tpT = self.bank()
                        pTb = pT[:].bitcast(BF16)
                        for kc in range(8):
                            P.op('pe', lambda e, pTb=pTb, kc=kc: e.transpose(pTb[:, kc * 128:(kc + 1) * 128], hnb.ap[:, kc * 128:(kc + 1) * 128], self.identb),
                                 reads=hnb.all() + [self.tcb], writes=[tpT])
                        P.op('dve', lambda e, pTb=pTb: e.tensor_tensor(out=tt1.ap.rearrange("p (k t) -> p k t", t=128), in0=pTb[:, 0:1024].rearrange("p (k t) -> p k t", t=128),
                                                                       in1=self.vcol('hnw').unsqueeze(2).to_broadcast([128, 8, 128]), op=ALU.mult),
                             reads=[tpT, self.tv], writes=tt1.all())
                        P.op('dve', lambda e: e.tensor_tensor(out=tt1.ap.rearrange("p (k t) -> p k t", t=128), in0=tt1.ap.rearrange("p (k t) -> p k t", t=128),
                                                              in1=skx.v3(0, 8, cs0, cs1), op=ALU.add), reads=tt1.all() + skx.all(), writes=tt1.all())
                        P.op('dve', lambda e: e.tensor_tensor(out=ybuf.v3(0, 8, cs0, cs1), in0=tt1.ap.rearrange("p (k t) -> p k t", t=128),
                                                              in1=sgm.v3(0, 8, cs0, cs1), op=ALU.mult), reads=tt1.all() + sgm.all(), writes=ybuf.all())
                if last_of_seq and not sample:
                    for h in range(4):
                        for dk in range(2):
                            P.op('sp', lambda e, h=h, dk=dk, si=si: e.dma_start(out=d['nC'][si, dr * 4 + h, dk, :, :], in_=C.v(h * 2 + dk, 0, 256)),
                                 reads=C.t(h * 2 + dk), dma=True)
                            col = (dr * 4 + h) * 2 + dk
                            P.op('sp', lambda e, h=h, dk=dk, si=si, col=col: e.dma_start(out=d['nn'][si, :, col:col + 1], in_=C.v(h * 2 + dk, 256, 257), allow_slow_non_contiguous=True),
                                 reads=C.t(h * 2 + dk), dma=True)
                    P.op('sp', lambda e, si=si, mcur=mcur: e.dma_start(out=d['nm'][si, :, dr * 4:dr * 4 + 4], in_=mst[mcur].ap[0:1, :]), reads=mst[mcur].all(), dma=True)
                if dr == 1:
                    for o in range(8):
                        s = self.wload(d['w_out1c'][o], 1024)
                        ps, tps = self.fm_proj(s, 128, lambda kc: ybuf.v(kc, 0, n), lambda kc: ybuf.t(kc), n)
                        ca, cb_ = s0 + t0, s0 + t0 + n
                        P.op('dve', lambda e, ps=ps, o=o, ca=ca, cb_=cb_: e.scalar_tensor_tensor(out=X.v(o, ca, cb_), in0=ps[:, 0:n], scalar=G1(o), in1=X.v(o, ca, cb_),
                                                                                               op0=ALU.mult, op1=ALU.add),
                             reads=[tps, self.tAB] + X.t(o, ca, cb_), writes=X.t(o, ca, cb_))

    def final(self, g):
        P, d = self.P, self.d
        X = g['X']
        dst = d['ys'] if g['sample'] else d['yp']
        self.reset_pages()
        ms = self.mod_scratch()
        yf = self.page_buf(8, 512, F32)
        ost = [self.page_buf(1, 1024, F32) for _ in range(2)]
        A = lambda kc: self.vcol('fnw', kc, kc + 1)
        for ti in range(g['T'] // 512):
            c0 = ti * 512
            self.modulate(g, c0, 512, A, None, yf, 0, ms)
            for sub in range(4):
                ob = ost[sub % 2]
                for half in range(2):
                    ps, tps = self.bank()
                    for k4 in range(4):
                        kc = half * 4 + k4
                        P.op('pe', lambda e, ps=ps, kc=kc, k4=k4, sub=sub: e.transpose(ps[:, k4 * 128:(k4 + 1) * 128], yf.v(kc, sub * 128, (sub + 1) * 128), self.ccol('ident')),
                             reads=yf.t(kc) + [self.tc], writes=[tps])
                    if half == 0:
                        P.op('act', lambda e, ps=ps, ob=ob: e.activation(out=ob.ap[:, 0:512], in_=ps[:, 0:512], func=AF.Copy), reads=[tps], writes=ob.all())
                    else:
                        P.op('dve', lambda e, ps=ps, ob=ob: e.tensor_copy(out=ob.ap[:, 512:1024], in_=ps[:, 0:512]), reads=[tps], writes=ob.all())
                r0 = c0 + sub * 128
                P.op('sp', lambda e, ob=ob, r0=r0: e.dma_start(out=dst[r0:r0 + 128, :], in_=ob.ap), reads=ob.all(), dma=True)


def prep_shared(inp, TS):
    f = lambda a: np.ascontiguousarray(np.asarray(a, dtype=np.float32))
    sh = {}
    w_in0 = f(inp['w_in0'])[0]
    sh['w_in0c'] = chunkmajor(w_in0)
    sh['w_in0w'] = np.ascontiguousarray(np.stack([chunkmajor(w_in0[:, 512:1024], 512)[0], chunkmajor(w_in0[:, 1024:1536], 512)[0],
                                                  chunkmajor(w_in0[:, 2048:2560], 512)[0]]))
    sh['w_out0c'] = chunkmajor(f(inp['w_out0'])[0])
    w_in1 = f(inp['w_in1'])[0]
    sh['w_in1c'] = chunkmajor(w_in1[:, :2048])
    sh['w_in1g'] = chunkmajor(w_in1[:, 2048:2064], 16)[0]
    wq, wk, wv = f(inp['w_q'])[0], f(inp['w_k'])[0], f(inp['w_v'])[0]
    sh['wqc'] = np.ascontiguousarray(np.concatenate([chunkmajor(wq[h]) for h in range(4)]))
    sh['wkc'] = np.ascontiguousarray(np.concatenate([chunkmajor(wk[h]) for h in range(4)]))
    sh['wkw'] = np.ascontiguousarray(np.stack([chunkmajor(wk[h], 256)[0] for h in range(4)]))
    sh['wvw'] = np.ascontiguousarray(np.stack([chunkmajor(wv[h], 256)[0] for h in range(4)]))
    sh['w_out1c'] = chunkmajor(f(inp['w_out1'])[0])
    w_up = f(inp['w_up'])
    sh['w_upc'] = np.ascontiguousarray(np.concatenate([chunkmajor(w_up[l]) for l in range(2)]))
    w_down = f(inp['w_down'])
    sh['w_downc'] = np.ascontiguousarray(np.concatenate([chunkmajor(w_down[l]) for l in range(2)]))
    w_mod = f(inp['w_mod'])
    sh['w_modc'] = np.ascontiguousarray(np.concatenate([chunkmajor(w_mod[l]) for l in range(2)]))
    vec = np.zeros((128, NVEC), np.float32)

    def put(n, a):
        o, w = VEC_OFF[n]
        a = np.asarray(a, np.float32)
        vec[:a.shape[0], o:o + w] = a
    for l in range(2):
        put('bmod%d' % l, colvec(f(inp['b_mod'])[l]))
        put('n1w%d' % l, colvec(f(inp['norm1_w'])[l]))
        put('n2w%d' % l, colvec(f(inp['norm2_w'])[l]))
        fw = f(inp['fconv_w'])[l]
        put('fcw%d' % l, np.concatenate([colvec(fw[k]) for k in range(3)], axis=1))
        put('fcb%d' % l, colvec(f(inp['fconv_b'])[l]))
    put('fnw', colvec(f(inp['final_norm_w'])))
    mw = f(inp['mconv_w'])[0]
    put('mcw', np.concatenate([colvec(mw[k]) for k in range(3)], axis=1))
    put('mcb', colvec(f(inp['mconv_b'])[0]))
    put('hnw', colvec(f(inp['head_norm_w'])[0]))
    put('skw', colvec(f(inp['skip_w'])[0]))
    put('sbw', f(inp['subln_w'])[0].reshape(128, 1))
    put('lam', np.stack([f(inp['lam_q1'])[0], f(inp['lam_k1'])[0], f(inp['lam_q2'])[0], f(inp['lam_k2'])[0]], axis=1))
    put('gnw', np.broadcast_to(f(inp['gate_norm_w'])[0][None, :], (128, 512)))
    put('bg', np.broadcast_to(f(inp['b_gates'])[0][None, :], (128, 16)))
    ws = f(inp['w_spatial'])[0]
    put('wsT', ws.transpose(2, 0, 1).reshape(128, 512))
    sh['vec'] = vec
    sh['bsrow'] = np.ascontiguousarray(f(inp['b_spatial'])[0].reshape(1, 512))
    sh['cst'] = make_consts()
    C, S = rope_tables(TS)
    sh['ropeC'] = C
    sh['ropeS'] = S
    return sh


_CACHE = {}


def kernel(**inp):
    f = lambda a: np.ascontiguousarray(np.asarray(a, dtype=np.float32))
    xp, xs = f(inp['x_prompt']), f(inp['x_sample'])
    NB, TS = xs.shape[0], xs.shape[1]
    BP, TP = xp.shape[0], xp.shape[1]
    ncores = NB
    NPR = BP // ncores
    PAST = inp['cache_k'].shape[2]
    key = (TS, NPR, TP, PAST)
    if key not in _CACHE:
        _CACHE[key] = Builder(TS, NPR, TP, PAST)
    bld = _CACHE[key]
    sh = prep_shared(inp, TS)
    ck, cv = f(inp['cache_k']), f(inp['cache_v'])
    sC, sn, sm = f(inp['state_C']), f(inp['state_n']), f(inp['state_m'])
    c, cctx = f(inp['c']), f(inp['c_ctx'])
    in_maps = []
    for b in range(ncores):
        m = dict(sh)
        m['xs'] = xs[b]
        m['xp'] = np.ascontiguousarray(xp[b * NPR:(b + 1) * NPR].reshape(NPR * TP, 1024))
        m['ck'] = np.ascontiguousarray(ck[b, 0].reshape(PAST, 512))
        m['cv'] = np.ascontiguousarray(cv[b, 0].reshape(PAST, 512))
        m['sC'] = np.ascontiguousarray(sC[b, 0].reshape(8, 2, 128, 256))
        m['sn'] = np.ascontiguousarray(sn[b, 0].reshape(8, 2, 128).transpose(2, 0, 1).reshape(128, 16))
        m['sm'] = np.ascontiguousarray(np.broadcast_to(sm[b, 0].reshape(1, 8), (128, 8)))
        cd = np.stack([c[b], cctx], axis=1)
        m['cond'] = np.ascontiguousarray(cd.reshape(8, 128, 2).transpose(1, 0, 2).reshape(128, 16))
        in_maps.append(m)
    res = run_bass_kernel_spmd(bld.nc, in_maps, core_ids=list(range(ncores)))
    R = res.results
    y_prompt = np.concatenate([R[b]['yp'].reshape(NPR, TP, 1024) for b in range(ncores)], axis=0)
    y_sample = np.stack([R[b]['ys'] for b in range(ncores)], axis=0)
    nk = np.concatenate([R[b]['nk'].reshape(NPR, 1, TP, 4, 128) for b in range(ncores)], axis=0)
    nv = np.concatenate([R[b]['nv'].reshape(NPR, 1, TP, 4, 128) for b in range(ncores)], axis=0)
    nC = np.concatenate([R[b]['nC'].reshape(NPR, 1, 2, 4, 256, 256) for b in range(ncores)], axis=0)
    nn = np.concatenate([R[b]['nn'].reshape(NPR, 128, 8, 2).transpose(0, 2, 3, 1).reshape(NPR, 1, 2, 4, 256) for b in range(ncores)], axis=0)
    nm = np.concatenate([R[b]['nm'].reshape(NPR, 1, 2, 4) for b in range(ncores)], axis=0)
    return (y_prompt.astype(np.float32), y_sample.astype(np.float32), nk.astype(np.float32), nv.astype(np.float32),
            nC.astype(np.float32), nn.astype(np.float32), nm.astype(np.float32))
```
